# Optimizing a Trainium2 kernel written in Bass

```python
import functools
import jax, jax.numpy as jnp
from jax import lax
import numpy as np

D_MODEL = 1024
BATCH = 4
SEQ = 4096
DEPTH = 2
DEC_BATCH = 32
DEC_SEQ = 1
PAST_LEN = 8192
PAGE_SIZE = 128

N_HEADS = 8
HEAD_DIM = 64
D_ATTN = N_HEADS * HEAD_DIM
D_CONV = 512
CONV_W = 3
D_FF = -(-8 * D_MODEL // (3 * 256)) * 256
Q_BLOCK = 128
ALPHA = (2 * DEPTH) ** 0.25
BETA = (8 * DEPTH) ** -0.25
LN_EPS = 1e-5
SCALE = HEAD_DIM ** -0.5

OFF_Q = 0
OFF_K = OFF_Q + D_ATTN
OFF_V = OFF_K + D_ATTN
OFF_F = OFF_V + D_ATTN
OFF_H = OFF_F + N_HEADS
OFF_B = OFF_H + D_CONV
OFF_C = OFF_B + D_CONV
OFF_G = OFF_C + D_CONV
D_IN = OFF_G + 2 * D_MODEL

kernel_name = "fox_shortconv_gated_hybrid_step"


def layer_norm(x, g, b):
    xf = x.astype(jnp.float32)
    mu = jnp.mean(xf, axis=-1, keepdims=True)
    var = jnp.mean(jnp.square(xf - mu), axis=-1, keepdims=True)
    y = (xf - mu) * lax.rsqrt(var + LN_EPS) * g.astype(jnp.float32) + b.astype(jnp.float32)
    return y.astype(x.dtype)


def split_in(x, w_in, b_f, b_gate):
    bsz, s = x.shape[:2]
    z = jnp.einsum('bsd,de->bse', x, w_in)
    q = z[..., OFF_Q:OFF_K].reshape(bsz, s, N_HEADS, HEAD_DIM)
    k = z[..., OFF_K:OFF_V].reshape(bsz, s, N_HEADS, HEAD_DIM)
    v = z[..., OFF_V:OFF_F].reshape(bsz, s, N_HEADS, HEAD_DIM)
    logf = jax.nn.log_sigmoid((z[..., OFF_F:OFF_H] + b_f).astype(jnp.float32))
    h = z[..., OFF_H:OFF_B]
    gb = z[..., OFF_B:OFF_C]
    gc = z[..., OFF_C:OFF_G]
    gates = jax.nn.sigmoid(z[..., OFF_G:] + b_gate)
    return q, k, v, logf, h, gb, gc, gates


def fox_block(q_blk, c_q, pos_q, k, v, c_k, pos_k):
    logits = jnp.einsum('bthd,bshd->bhts', q_blk, k).astype(jnp.float32) * SCALE
    bias = jnp.transpose(c_q, (0, 2, 1))[:, :, :, None] - jnp.transpose(c_k, (0, 2, 1))[:, :, None, :]
    mask = pos_k[None, :] <= pos_q[:, None]
    logits = jnp.where(mask, logits + bias, -jnp.inf)
    p = jax.nn.softmax(logits, axis=-1)
    return jnp.einsum('bhts,bshd->bthd', p.astype(v.dtype), v)


def fox_prompt(q, k, v, logf):
    bsz, s = q.shape[:2]
    nb = s // Q_BLOCK
    c = jnp.cumsum(logf, axis=1)
    pos = jnp.arange(s)
    qb = jnp.transpose(q.reshape(bsz, nb, Q_BLOCK, N_HEADS, HEAD_DIM), (1, 0, 2, 3, 4))
    cb = jnp.transpose(c.reshape(bsz, nb, Q_BLOCK, N_HEADS), (1, 0, 2, 3))
    pb = pos.reshape(nb, Q_BLOCK)
    out = lax.map(lambda a: fox_block(a[0], a[1], a[2], k, v, c, pos), (qb, cb, pb))
    return jnp.transpose(out, (1, 0, 2, 3, 4)).reshape(bsz, s, D_ATTN)


def fox_sample(q, k_new, v_new, logf_new, cache_k_l, cache_v_l, cache_logf_l, page_table):
    db, ds = q.shape[:2]
    past = page_table.shape[1] * PAGE_SIZE
    k_past = cache_k_l[page_table].reshape(db, past, N_HEADS, HEAD_DIM)
    v_past = cache_v_l[page_table].reshape(db, past, N_HEADS, HEAD_DIM)
    lf_past = cache_logf_l[page_table].reshape(db, past, N_HEADS)
    k = jnp.concatenate([k_past, k_new.astype(k_past.dtype)], axis=1)
    v = jnp.concatenate([v_past, v_new.astype(v_past.dtype)], axis=1)
    c = jnp.cumsum(jnp.concatenate([lf_past.astype(jnp.float32), logf_new], axis=1), axis=1)
    pos = jnp.arange(past + ds)
    out = fox_block(q.astype(k.dtype), c[:, past:], pos[past:], k, v, c, pos)
    return out.reshape(db, ds, D_ATTN)


def short_conv(u, w, buf):
    s = u.shape[1]
    ue = jnp.concatenate([buf.astype(u.dtype), u], axis=1)
    out = w[0] * ue[:, 0:s]
    for j in range(1, CONV_W):
        out = out + w[j] * ue[:, j:j + s]
    return out, ue[:, -(CONV_W - 1):]


def trunk_layer(x, attn_fn, conv_buf, w_in, b_f, b_gate, conv_w, w_pa, w_pc, w_o,
                ln1_g, ln1_b, w_gu, w_down, ln2_g, ln2_b):
    q, k, v, logf, h, gb, gc, gates = split_in(x, w_in, b_f, b_gate)
    a = attn_fn(q, k, v, logf)
    cv, new_buf = short_conv(gc * h, conv_w, conv_buf)
    cb = gb * cv
    a_out = jnp.einsum('bse,ed->bsd', a.astype(x.dtype), w_pa)
    c_out = jnp.einsum('bse,ed->bsd', cb, w_pc)
    m = gates[..., :D_MODEL] * a_out + gates[..., D_MODEL:] * c_out
    tm = jnp.einsum('bsd,de->bse', m, w_o)
    x = layer_norm(ALPHA * x + tm, ln1_g, ln1_b)
    gu = jnp.einsum('bsd,df->bsf', x, w_gu)
    f = jnp.einsum('bsf,fd->bsd', jax.nn.silu(gu[..., :D_FF]) * gu[..., D_FF:], w_down)
    x = layer_norm(ALPHA * x + f, ln2_g, ln2_b)
    return x, k, v, logf, new_buf


def setup_inputs(seed: int = 0) -> dict:
    key = jax.random.key(seed)
    ks = jax.random.split(key, 24)
    f32 = jnp.float32
    n_pages = PAST_LEN // PAGE_SIZE
    n_used = DEC_BATCH * n_pages
    n_pool = (5 * n_used) // 4
    nrm = lambda k, shape, s: jax.random.normal(k, shape, f32) * s

    x_prompt = nrm(ks[0], (BATCH, SEQ, D_MODEL), 1.0)
    x_sample = nrm(ks[1], (DEC_BATCH, DEC_SEQ, D_MODEL), 1.0)
    cache_k = nrm(ks[2], (DEPTH, n_pool, PAGE_SIZE, N_HEADS, HEAD_DIM), 1.0)
    cache_v = nrm(ks[3], (DEPTH, n_pool, PAGE_SIZE, N_HEADS, HEAD_DIM), BETA)
    cache_logf = jax.nn.log_sigmoid(
        jax.random.uniform(ks[4], (DEPTH, 1, 1, N_HEADS), f32, 1.0, 5.0)
        + nrm(ks[5], (DEPTH, n_pool, PAGE_SIZE, N_HEADS), 0.5))
    state_conv = nrm(ks[6], (DEPTH, DEC_BATCH, CONV_W - 1, D_CONV), 0.5)
    page_table = jax.random.permutation(ks[7], n_pool)[:n_used].reshape(DEC_BATCH, n_pages).astype(jnp.int32)

    col_scale = jnp.concatenate([
        jnp.ones((2 * D_ATTN,), f32),
        jnp.full((D_ATTN,), BETA, f32),
        jnp.full((N_HEADS,), 0.1, f32),
        jnp.full((D_CONV,), BETA, f32),
        jnp.ones((2 * D_CONV + 2 * D_MODEL,), f32)])
    w_in = nrm(ks[8], (DEPTH, D_MODEL, D_IN), D_MODEL ** -0.5) * col_scale
    b_f = jax.random.uniform(ks[9], (DEPTH, N_HEADS), f32, 1.0, 5.0)
    b_gate = nrm(ks[10], (DEPTH, 2 * D_MODEL), 0.1)
    conv_w = nrm(ks[11], (DEPTH, CONV_W, D_CONV), CONV_W ** -0.5)
    w_attn_proj = nrm(ks[12], (DEPTH, D_ATTN, D_MODEL), D_ATTN ** -0.5)
    w_conv_proj = nrm(ks[13], (DEPTH, D_CONV, D_MODEL), D_CONV ** -0.5)
    w_out = nrm(ks[14], (DEPTH, D_MODEL, D_MODEL), BETA * D_MODEL ** -0.5)
    ln1_g = 1.0 + nrm(ks[15], (DEPTH, D_MODEL), 0.02)
    ln1_b = nrm(ks[16], (DEPTH, D_MODEL), 0.02)
    w_gate_up = nrm(ks[17], (DEPTH, D_MODEL, 2 * D_FF), BETA * D_MODEL ** -0.5)
    w_down = nrm(ks[18], (DEPTH, D_FF, D_MODEL), BETA * D_FF ** -0.5)
    ln2_g = 1.0 + nrm(ks[19], (DEPTH, D_MODEL), 0.02)
    ln2_b = nrm(ks[20], (DEPTH, D_MODEL), 0.02)
    return {
        "x_prompt": x_prompt, "x_sample": x_sample,
        "cache_k": cache_k, "cache_v": cache_v, "cache_logf": cache_logf,
        "state_conv": state_conv, "page_table": page_table,
        "w_in": w_in, "b_f": b_f, "b_gate": b_gate, "conv_w": conv_w,
        "w_attn_proj": w_attn_proj, "w_conv_proj": w_conv_proj, "w_out": w_out,
        "ln1_g": ln1_g, "ln1_b": ln1_b, "w_gate_up": w_gate_up, "w_down": w_down,
        "ln2_g": ln2_g, "ln2_b": ln2_b,
    }


def reference(x_prompt, x_sample, cache_k, cache_v, cache_logf, state_conv, page_table,
              w_in, b_f, b_gate, conv_w, w_attn_proj, w_conv_proj, w_out,
              ln1_g, ln1_b, w_gate_up, w_down, ln2_g, ln2_b):
    xp = x_prompt
    xs = x_sample
    kp, vp, lp, cp = [], [], [], []
    ksm, vsm, lsm, csm = [], [], [], []
    zero_buf = jnp.zeros((x_prompt.shape[0], CONV_W - 1, D_CONV), x_prompt.dtype)
    for l in range(DEPTH):
        params = (w_in[l], b_f[l], b_gate[l], conv_w[l], w_attn_proj[l], w_conv_proj[l], w_out[l],
                  ln1_g[l], ln1_b[l], w_gate_up[l], w_down[l], ln2_g[l], ln2_b[l])
        xp, k1, v1, lf1, buf1 = trunk_layer(xp, fox_prompt, zero_buf, *params)
        attn_s = functools.partial(fox_sample, cache_k_l=cache_k[l], cache_v_l=cache_v[l],
                                   cache_logf_l=cache_logf[l], page_table=page_table)
        xs, k2, v2, lf2, buf2 = trunk_layer(xs, attn_s, state_conv[l], *params)
        kp.append(k1); vp.append(v1); lp.append(lf1); cp.append(buf1)
        ksm.append(k2); vsm.append(v2); lsm.append(lf2); csm.append(buf2)
    k_prompt = jnp.stack(kp)
    v_prompt = jnp.stack(vp)
    logf_prompt = jnp.stack(lp)
    conv_prompt = jnp.stack(cp)
    k_sample = jnp.stack(ksm)
    v_sample = jnp.stack(vsm)
    logf_sample = jnp.stack(lsm)
    conv_sample = jnp.stack(csm)
    return (xp, xs, k_prompt, v_prompt, logf_prompt, conv_prompt,
            k_sample, v_sample, logf_sample, conv_sample)
```

```python
import os
import numpy as np
import ml_dtypes
import concourse.bass as bass
import concourse.mybir as mybir
from concourse.bass_utils import run_bass_kernel_spmd

F32 = mybir.dt.float32
BF16 = mybir.dt.bfloat16
I32 = mybir.dt.int32
AF = mybir.ActivationFunctionType
ALU = mybir.AluOpType
AX = mybir.AxisListType

D = 1024
DCH = 8
H = 8
HD = 64
DIN = 5128
DFF = 2816
FT = 22
OFF_Q, OFF_K, OFF_V, OFF_F, OFF_H, OFF_B, OFF_C, OFF_G = 0, 512, 1024, 1536, 1544, 2056, 2568, 3080
NS = 4
TB = 512
KA = 70
ALPHA = float(4 ** 0.25)
LN_EPS = 1e-5
NEG = -30000.0


class _Proxy:
    def __init__(self):
        self.call = None

    def __getattr__(self, name):
        def f(*a, **k):
            self.call = (name, a, k)
            return self
        return f


class Rec:
    ENG = ("pe", "act", "dve", "pool", "sp")

    def __init__(self, nc):
        self.nc = nc
        self.st = {e: [] for e in self.ENG}
        self.psem = {e: nc.alloc_semaphore("P_" + e) for e in ("pe", "act", "dve", "pool")}
        self.pcnt = {e: 0 for e in self.psem}
        self.waited = {}
        self.dsem = {}
        self.dcnt = {}
        self.last = {}

    def _waits(self, eng, waits):
        for tok in waits:
            if tok is None:
                continue
            if isinstance(tok, list):
                self._waits(eng, tok)
                continue
            sem, val = tok
            key = (eng, sem.num)
            if self.waited.get(key, 0) >= val:
                continue
            self.waited[key] = val
            self.st[eng].append(lambda e, sem=sem, val=val: e.wait_ge(sem, val))

    def op(self, eng, fn, waits=(), sig=True):
        self._waits(eng, waits)
        px = _Proxy()
        fn(px)
        name, a, k = px.call
        if sig:
            self.pcnt[eng] += 1
            sem = self.psem[eng]
            tok = (sem, self.pcnt[eng])
            self.st[eng].append(lambda e, name=name, a=a, k=k, sem=sem: getattr(e, name)(*a, **k).then_inc(sem, 1))
            self.last[eng] = tok
            return tok
        self.st[eng].append(lambda e, name=name, a=a, k=k: getattr(e, name)(*a, **k))
        return None

    def dma(self, eng, out, in_, ch, waits=(), **kw):
        self._waits(eng, waits)
        if ch not in self.dsem:
            self.dsem[ch] = self.nc.alloc_semaphore("D_" + ch)
            self.dcnt[ch] = 0
        self.dcnt[ch] += 16
        sem = self.dsem[ch]
        self.st[eng].append(lambda e, out=out, in_=in_, sem=sem, kw=kw: e.dma_start(out=out, in_=in_, **kw).then_inc(sem, 16))
        return (sem, self.dcnt[ch])

    def gather(self, out, in_, idx, ch, waits=(), eoff=0):
        eng = "pool"
        self._waits(eng, waits)
        if ch not in self.dsem:
            self.dsem[ch] = self.nc.alloc_semaphore("D_" + ch)
            self.dcnt[ch] = 0
        self.dcnt[ch] += 16
        sem = self.dsem[ch]
        self.st[eng].append(lambda e, out=out, in_=in_, idx=idx, sem=sem, eoff=eoff: e.indirect_dma_start(
            out=out, out_offset=None, in_=in_,
            in_offset=bass.IndirectOffsetOnAxis(ap=idx, axis=0), element_offset=eoff).then_inc(sem, 16))
        return (sem, self.dcnt[ch])

    def all_tokens(self):
        toks = [t for t in self.last.values()]
        toks += [(self.dsem[c], self.dcnt[c]) for c in self.dsem]
        return toks

    def barrier(self):
        toks = self.all_tokens()
        for e in self.ENG:
            self._waits(e, toks)


class Arena:
    def __init__(self, nc, lo, hi):
        self.nc, self.lo, self.hi, self.cur, self.n = nc, lo, hi, lo, 0

    def alloc(self, name, shape, dtype):
        nbytes = int(np.prod(shape[1:])) * (4 if dtype in (F32, I32) else 2)
        off = (self.cur + 31) // 32 * 32
        assert off + nbytes <= self.hi, f"SBUF arena overflow at {name}: {off + nbytes} > {self.hi}"
        self.cur = off + nbytes
        Arena_cnt[0] += 1
        return self.nc.alloc_sbuf_tensor_at(f"{name}_{Arena_cnt[0]}", list(shape), dtype, offset=off)

    def mark(self):
        return self.cur

    def reset(self, m):
        self.cur = m


Arena_cnt = [0]


class Ring:
    def __init__(self, tiles):
        self.t = tiles
        self.free = [[] for _ in tiles]
        self.i = 0

    def get(self):
        k = self.i % len(self.t)
        self.i += 1
        fr = self.free[k]
        self.free[k] = []
        return k, self.t[k], fr


class _Stop(Exception):
    pass


def build(S, NPG, NPOOL, L=2, stop=None):
    stage_ctr = [0]

    def stage_gate():
        if stop is not None and stage_ctr[0] >= stop:
            raise _Stop()
        stage_ctr[0] += 1
    NT = S + NS
    NB = S // TB
    NST = S // 128
    blocks = [(i * TB, TB) for i in range(NB)] + [(S, NS)]
    NPAGES = NS * NPG
    GP = min(128, NPAGES)
    NGRP = NPAGES // GP
    TK = 8
    NCHUNK = 128 // TK

    nc = bass.Bass("TRN2", target_bir_lowering=False)

    def din(name, shape, dt=F32):
        return nc.dram_tensor(name, list(shape), dt, kind="ExternalInput").ap()

    def dout(name, shape, dt=F32):
        return nc.dram_tensor(name, list(shape), dt, kind="ExternalOutput").ap()

    def dscr(name, shape, dt):
        return nc.dram_tensor(name, list(shape), dt, kind="Internal").ap()

    xT = din("xT", [D, NT])
    w_in = din("w_in", [L, D, DIN])
    b_f = din("b_f", [L, H, 1])
    b_fr = din("b_fr", [L, NS, H])
    b_gate = din("b_gate", [L, 128, 16])
    conv_w = din("conv_w", [L, 128, 4, 3])
    w_pa = din("w_pa", [L, 512, D])
    w_pc = din("w_pc", [L, 512, D])
    w_o = din("w_o", [L, D, D])
    ln1g = din("ln1g", [L, 128, DCH])
    ln1b = din("ln1b", [L, 128, DCH])
    ln2g = din("ln2g", [L, 128, DCH])
    ln2b = din("ln2b", [L, 128, DCH])
    w_gu = din("w_gu", [L, D, 2 * DFF])
    w_dn = din("w_dn", [L, DFF, D])
    cache_k = [din(f"cache_k{l}", [NPOOL, 128 * 512]) for l in range(L)]
    cache_v = [din(f"cache_v{l}", [NPOOL, 128 * 512]) for l in range(L)]
    cache_lf = [din(f"cache_lf{l}", [NPOOL, 128 * H]) for l in range(L)]
    stT = din("stT", [L, 128, 4, NS, 2])
    pt = din("pt", [GP, NGRP], I32)
    c_ident = din("c_ident", [128, 128], BF16)
    c_maskb = din("c_maskb", [128, 128], BF16)
    c_onesb = din("c_onesb", [128, 128], BF16)
    c_onesf = din("c_onesf", [128, 128])
    c_ind = din("c_ind", [NGRP, GP, NS])
    c_indT = din("c_indT", [NGRP, NS, GP])
    c_after = din("c_after", [GP, GP])

    yT = dout("yT", [D, NT])
    kT_out = dout("kT_out", [L, 512, S])
    v_out = dout("v_out", [L, S, 512])
    lf_out = dout("lf_out", [L, H, S])
    convp_out = dout("convp_out", [L, 128, 4, 2])
    ks_out = dout("ks_out", [L, NS, 512])
    vs_out = dout("vs_out", [L, NS, 512])
    lfs_out = dout("lfs_out", [L, NS, H])
    convs_out = dout("convs_out", [L, 128, 4, NS, 2])

    x32_s = [dscr("x1_s", [128, DCH, NT], F32), dscr("x2_s", [128, DCH, NT], F32)]
    x1bf_s = dscr("x1bf_s", [128, DCH, NT], BF16)
    qT_s = dscr("qT_s", [H, KA, NT], BF16)
    kT_s = dscr("kT_s", [H, KA, NT], BF16)
    V_s = dscr("V_s", [H, S, HD + 1], BF16)
    g_s = dscr("g_s", [128, 16, NT], BF16)
    cb_s = dscr("cb_s", [128, 4, NT], BF16)
    a_s = dscr("a_s", [128, 4, NT], BF16)
    s_s = dscr("s_s", [128, FT, NT], BF16)

    R = Rec(nc)
    LO = 16512
    HI = 229344
    A = Arena(nc, LO, HI)
    banks = [nc.alloc_psum_tensor(f"bank{i}", [128, 512], F32) for i in range(8)]
    PS = Ring(banks)

    def psget():
        return PS.get()

    ident = A.alloc("ident", [128, 128], BF16)
    maskb = A.alloc("maskb", [128, 128], BF16)
    onesb = A.alloc("onesb", [128, 128], BF16)
    onesf = A.alloc("onesf", [128, 128], F32)
    lnp = A.alloc("lnp", [128, L, 4, DCH], F32)
    bg = A.alloc("bg", [128, L, 16], F32)
    cw = A.alloc("cw", [128, L, 4, 3], F32)
    bfc = A.alloc("bfc", [H, L], F32)
    bfr = A.alloc("bfr", [NS, L, H], F32)
    qs_t = A.alloc("qs_t", [NS, 512], F32)
    ks_t = A.alloc("ks_t", [NS, 512], F32)
    vs_t = A.alloc("vs_t", [NS, 512], F32)
    lfs_t = A.alloc("lfs_t", [NS, H], F32)
    epsc = A.alloc("epsc", [128, 1], F32)
    R.op("pool", lambda e: e.memset(epsc[:, :], LN_EPS))
    ctoks = []
    ctoks.append(R.dma("sp", ident[:, :], c_ident, "const"))
    ctoks.append(R.dma("sp", maskb[:, :], c_maskb, "const"))
    ctoks.append(R.dma("sp", onesb[:, :], c_onesb, "const"))
    ctoks.append(R.dma("sp", onesf[:, :], c_onesf, "const"))
    for l in range(L):
        for i, t in enumerate((ln1g, ln1b, ln2g, ln2b)):
            ctoks.append(R.dma("sp", lnp[:, l, i, :], t[l], "const"))
        ctoks.append(R.dma("sp", bg[:, l, :], b_gate[l], "const"))
        ctoks.append(R.dma("sp", cw[:, l, :, :], conv_w[l], "const"))
        ctoks.append(R.dma("sp", bfc[:, l:l + 1], b_f[l], "const"))
        ctoks.append(R.dma("sp", bfr[:, l, :], b_fr[l], "const"))
    ctok = ctoks[-1]
    nbf_tok = R.op("dve", lambda e: e.tensor_scalar(out=bfc[:, :], in0=bfc[:, :], scalar1=-1.0, scalar2=None, op0=ALU.mult), waits=[ctok])
    R.barrier()
    PBASE = A.mark()

    def layer_norm_block(l, which, r, w, out_dram32, out_drambf, t0, bufs, rtoks):
        rbf, r2, mean, rstd, tmp, y32, ybf = (bufs[k] for k in ("rbf", "r2", "mean", "rstd", "tmp", "y32", "ybf"))
        gi, bi = (0, 1) if which == 1 else (2, 3)
        t_rbf, t_r2 = [], []
        for dt in range(DCH):
            t_rbf.append(R.op("pool", lambda e, dt=dt: e.tensor_copy(out=rbf[:, dt, :w], in_=r[:, dt, :w]), waits=[rtoks[dt]] + bufs["free_rbf"]))
            t_r2.append(R.op("act", lambda e, dt=dt: e.activation(out=r2[:, dt, :w], in_=r[:, dt, :w], func=AF.Square), waits=[rtoks[dt]] + bufs["free_r2"]))
        k1, S1, fr1 = psget()
        for dt in range(DCH):
            tS1 = R.op("pe", lambda e, dt=dt: e.matmul(S1[:, :w], lhsT=onesb[:, :], rhs=rbf[:, dt, :w], start=(dt == 0), stop=(dt == DCH - 1)),
                       waits=[t_rbf[dt]] + (fr1 if dt == 0 else []), sig=(dt == DCH - 1))
        k2, S2, fr2 = psget()
        for dt in range(DCH):
            tS2 = R.op("pe", lambda e, dt=dt: e.matmul(S2[:, :w], lhsT=onesb[:, :], rhs=r2[:, dt, :w], start=(dt == 0), stop=(dt == DCH - 1)),
                       waits=[t_r2[dt]] + (fr2 if dt == 0 else []), sig=(dt == DCH - 1))
        bufs["free_rbf"] = [tS1]
        bufs["free_r2"] = [tS2]
        tm = R.op("act", lambda e: e.activation(out=mean[:, :w], in_=S1[:, :w], func=AF.Copy, scale=1.0 / D), waits=[tS1] + bufs["free_stat"])
        PS.free[k1] = [tm]
        tq = R.op("dve", lambda e: e.tensor_tensor(out=tmp[:, :w], in0=mean[:, :w], in1=mean[:, :w], op=ALU.mult), waits=[tm] + bufs["free_tmp"])
        tv = R.op("dve", lambda e: e.scalar_tensor_tensor(out=rstd[:, :w], in0=S2[:, :w], scalar=1.0 / D, in1=tmp[:, :w], op0=ALU.mult, op1=ALU.subtract), waits=[tS2, tq] + bufs["free_stat"])
        PS.free[k2] = [tv]
        tsq = R.op("act", lambda e: e.activation(out=rstd[:, :w], in_=rstd[:, :w], func=AF.Sqrt, bias=epsc[:, 0:1]), waits=[tv])
        tr = R.op("dve", lambda e: e.reciprocal(out=rstd[:, :w], in_=rstd[:, :w]), waits=[tsq])
        ty, tb = [], []
        tmpR = bufs["tmpR"]
        t1 = t3 = None
        for dt in range(DCH):
            kt, tm_, frt = tmpR.get()
            t1 = R.op("pool", lambda e: e.tensor_tensor(out=tm_[:, :w], in0=r[:, dt, :w], in1=mean[:, :w], op=ALU.subtract), waits=[tm, rtoks[dt]] + frt)
            t2 = R.op("dve", lambda e: e.tensor_tensor(out=tm_[:, :w], in0=tm_[:, :w], in1=rstd[:, :w], op=ALU.mult), waits=[t1, tr])
            t3 = R.op("act", lambda e: e.activation(out=y32[:, dt, :w], in_=tm_[:, :w], func=AF.Identity,
                                                    scale=lnp[:, l, gi, dt:dt + 1], bias=lnp[:, l, bi, dt:dt + 1]),
                      waits=[t2] + (bufs["free_y32"] if dt == 0 else []))
            tmpR.free[kt] = [t3]
            ty.append(t3)
            if out_drambf is not None:
                tb.append(R.op("pool", lambda e: e.tensor_copy(out=ybf[:, dt, :w], in_=y32[:, dt, :w]), waits=[t3] + (bufs["free_ybf"] if dt == 0 else [])))
        last_tmp = [t1, t3]
        bufs["free_tmp"] = []
        bufs["free_stat"] = [t1]
        d1 = R.dma("sp", out_dram32[:, :, t0:t0 + w], y32[:, :, :w], "ln_y32", waits=[ty[-1]])
        bufs["free_y32"] = [d1]
        outs = [d1]
        if out_drambf is not None:
            d2 = R.dma("sp", out_drambf[:, :, t0:t0 + w], ybf[:, :, :w], "ln_ybf", waits=[tb[-1]])
            bufs["free_ybf"] = [d2]
            outs.append(d2)
        return outs, last_tmp

    def ln_bufs():
        b = {
            "rbf": A.alloc("rbf", [128, DCH, TB], BF16), "r2": A.alloc("r2", [128, DCH, TB], BF16),
            "mean": A.alloc("mean", [128, TB], F32), "rstd": A.alloc("rstd", [128, TB], F32),
            "tmp": A.alloc("tmp", [128, TB], F32), "y32": A.alloc("y32", [128, DCH, TB], F32),
            "tmpR": Ring([A.alloc("tmpr", [128, TB], F32) for _ in range(3)]),
            "ybf": A.alloc("ybf", [128, DCH, TB], BF16),
        }
        for k in ("free_rbf", "free_r2", "free_stat", "free_tmp", "free_y32", "free_ybf"):
            b[k] = []
        return b

    try:
     for l in range(L):
        x_src = xT.rearrange("(c p) t -> p c t", p=128) if l == 0 else x32_s[1]
        last = (l == L - 1)
        y_dst = yT.rearrange("(c p) t -> p c t", p=128) if last else x32_s[1]

        stage_gate()
        A.reset(PBASE)
        xbf = A.alloc("xbf", [128, DCH, NT], BF16)
        GW = 256
        wst = [A.alloc("wst", [128, DCH, GW], F32) for _ in range(2)]
        wstR = Ring(wst)
        wbf = [A.alloc("wbf", [128, DCH, GW], BF16) for _ in range(6)]
        wbfR = Ring(wbf)
        obf = [A.alloc("obf", [128, TB], BF16) for _ in range(4)]
        obfR = Ring(obf)
        o32 = [A.alloc("o32", [128, TB], F32) for _ in range(3)]
        o32R = Ring(o32)
        hbuf = A.alloc("hbuf", [128, NT], F32)
        ubuf = A.alloc("ubuf", [128, NT + 2], F32)
        zf = A.alloc("zf", [H, NT], F32)
        Vst = [A.alloc("Vst", [128, H, HD + 1], BF16) for _ in range(2)]
        VstR = Ring(Vst)

        xtok = {}
        for (t0, w) in blocks:
            for hb in range(0, w, GW):
                ww = min(GW, w - hb)
                k, st, fr = wstR.get()
                d = R.dma("sp", st[:, :, :ww], x_src[:, :, t0 + hb:t0 + hb + ww], f"wst{k}", waits=fr)
                c = R.op("pool", lambda e, st=st, ww=ww, a=t0 + hb: e.tensor_copy(out=xbf[:, :, a:a + ww], in_=st[:, :, :ww]), waits=[d])
                wstR.free[k] = [c]
                xtok[(t0, hb)] = c
        xall = c

        def load_w(src_ap, ncols):
            k, st, fr = wstR.get()
            d = R.dma("sp", st[:, :, :ncols], src_ap.rearrange("(c p) n -> p c n", p=128), f"wst{k}", waits=fr)
            kb, wb, frb = wbfR.get()
            c = R.op("pool", lambda e: e.tensor_copy(out=wb[:, :, :ncols], in_=st[:, :, :ncols]), waits=[d] + frb)
            wstR.free[k] = [c]
            return wb, kb, c

        def fm_tile(wb, wtok, j, t0, w):
            k, bk, fr = psget()
            for c in range(DCH):
                tk = R.op("pe", lambda e, c=c: e.matmul(bk[:, :w], lhsT=wb[:, c, j * 128:(j + 1) * 128], rhs=xbf[:, c, t0:t0 + w],
                                                        start=(c == 0), stop=(c == DCH - 1)),
                          waits=([wtok, xall] + fr) if c == 0 else [], sig=(c == DCH - 1))
            return k, bk, tk

        stage_gate()
        for which, off, dst in (("q", OFF_Q, qT_s), ("k", OFF_K, kT_s)):
            for g in range(2):
                wb, kb, wtok = load_w(w_in[l, :, off + g * GW: off + (g + 1) * GW], GW)
                pe_last = None
                for j in range(2):
                    h0 = (g * 2 + j) * 2
                    for (t0, w) in blocks:
                        k, bk, tk = fm_tile(wb, wtok, j, t0, w)
                        ko, ob, fro = obfR.get()
                        if which == "k" and t0 < S:
                            k3, o3, fr3 = o32R.get()
                            ev0 = R.op("act", lambda e: e.activation(out=o3[:, :w], in_=bk[:, :w], func=AF.Copy), waits=[tk] + fr3)
                            PS.free[k] = [ev0]
                            ev = R.op("pool", lambda e: e.tensor_copy(out=ob[:, :w], in_=o3[:, :w]), waits=[ev0] + fro)
                            c0 = (g * 2 + j) * 128
                            d3 = R.dma("sp", kT_out[l, c0:c0 + 128, t0:t0 + w], o3[:, :w], f"o32{k3}", waits=[ev0])
                            o32R.free[k3] = [d3, ev]
                        else:
                            ev = R.op("act", lambda e: e.activation(out=ob[:, :w], in_=bk[:, :w], func=AF.Copy), waits=[tk] + fro)
                            PS.free[k] = [ev]
                        d1 = R.dma("sp", dst[h0, 0:64, t0:t0 + w], ob[0:64, :w], f"obf{ko}", waits=[ev])
                        d2 = R.dma("sp", dst[h0 + 1, 0:64, t0:t0 + w], ob[64:128, :w], f"obf{ko}", waits=[ev])
                        obfR.free[ko] = [d2]
                        pe_last = tk
                if os.environ.get("SKIP_TM"):
                    wbfR.free[kb] = [R.last["pe"]]
                    continue
                k, bk, fr = psget()
                for c in range(DCH):
                    tk = R.op("pe", lambda e, c=c: e.matmul(bk[0:NS, :GW], lhsT=xbf[:, c, S:S + NS], rhs=wb[:, c, :GW], start=(c == 0), stop=(c == DCH - 1)),
                              waits=([wtok, xall] + fr) if c == 0 else [], sig=(c == DCH - 1))
                tgt = qs_t if which == "q" else ks_t
                ev = R.op("act", lambda e, tgt=tgt, bk=bk, g=g: e.activation(out=tgt[:, g * GW:(g + 1) * GW], in_=bk[0:NS, :GW], func=AF.Copy), waits=[tk])
                PS.free[k] = [ev]
                wbfR.free[kb] = [tk]
                if which == "k":
                    R.dma("sp", ks_out[l, :, g * GW:(g + 1) * GW], ks_t[:, g * GW:(g + 1) * GW], "small_out", waits=[ev])

        stage_gate()
        wv = [load_w(w_in[l, :, OFF_V + g * GW: OFF_V + (g + 1) * GW], GW) for g in range(2)]
        for tt in range(NST + 1):
            if tt < NST:
                a0, m = tt * 128, 128
            else:
                a0, m = S, NS
            k, bk, fr = psget()
            for g in range(2):
                wb, kb, wtok = wv[g]
                for c in range(DCH):
                    tk = R.op("pe", lambda e, c=c, g=g, wb=wb: e.matmul(bk[0:m, g * GW:(g + 1) * GW], lhsT=xbf[:, c, a0:a0 + m], rhs=wb[:, c, :GW],
                                                                       start=(c == 0), stop=(c == DCH - 1)),
                              waits=([wtok, xall] + fr) if c == 0 else [], sig=(c == DCH - 1 and g == 1))
            if tt < NST:
                k3, o3, fr3 = o32R.get()
                ev = R.op("act", lambda e, o3=o3, bk=bk: e.activation(out=o3[:, :], in_=bk[:, :], func=AF.Copy), waits=[tk] + fr3)
                d3 = R.dma("sp", v_out[l, a0:a0 + 128, :], o3[:, :], f"o32{k3}", waits=[ev])
                kv, vs, frv = VstR.get()
                ev2 = R.op("pool", lambda e: e.tensor_copy(out=vs[:, :, 0:HD], in_=o3[:, :].rearrange("p (h d) -> p h d", h=H)), waits=[ev] + frv)
                ev3 = R.op("pool", lambda e, vs=vs: e.memset(vs[:, :, HD:HD + 1], 1.0), waits=frv)
                d4 = R.dma("sp", V_s[:, a0:a0 + 128, :].rearrange("h s d -> s h d"), vs[:, :, :], f"Vst{kv}", waits=[ev2, ev3])
                VstR.free[kv] = [d4]
                o32R.free[k3] = [d3, ev2]
                PS.free[k] = [ev]
            else:
                ev = R.op("act", lambda e, bk=bk: e.activation(out=vs_t[:, :], in_=bk[0:NS, :], func=AF.Copy), waits=[tk])
                R.dma("sp", vs_out[l], vs_t[:, :], "small_out", waits=[ev])
                PS.free[k] = [ev]
        for g in range(2):
            wbfR.free[wv[g][1]] = [tk]

        stage_gate()
        wb, kb, wtok = load_w(w_in[l, :, OFF_F:OFF_F + H], H)
        zf_tok = None
        for (t0, w) in blocks:
            k, bk, fr = psget()
            for c in range(DCH):
                tk = R.op("pe", lambda e, c=c: e.matmul(bk[0:H, :w], lhsT=wb[:, c, 0:H], rhs=xbf[:, c, t0:t0 + w], start=(c == 0), stop=(c == DCH - 1)),
                          waits=([wtok, xall] + fr) if c == 0 else [], sig=(c == DCH - 1))
            zf_tok = R.op("act", lambda e, bk=bk, t0=t0, w=w: e.activation(out=zf[:, t0:t0 + w], in_=bk[0:H, :w], func=AF.Exp, scale=-1.0, bias=bfc[:, l:l + 1]),
                          waits=[tk, nbf_tok])
            PS.free[k] = [zf_tok]
        k, bk, fr = psget()
        for c in range(DCH):
            tk = R.op("pe", lambda e, c=c: e.matmul(bk[0:NS, 0:H], lhsT=xbf[:, c, S:S + NS], rhs=wb[:, c, 0:H], start=(c == 0), stop=(c == DCH - 1)),
                      waits=([wtok, xall] + fr) if c == 0 else [], sig=(c == DCH - 1))
        wbfR.free[kb] = [tk]
        t1 = R.op("dve", lambda e: e.tensor_tensor(out=lfs_t[:, :], in0=bk[0:NS, 0:H], in1=bfr[:, l, :], op=ALU.add), waits=[tk])
        PS.free[k] = [t1]
        t2 = R.op("act", lambda e: e.activation(out=lfs_t[:, :], in_=lfs_t[:, :], func=AF.Exp, scale=-1.0), waits=[t1])
        t3 = R.op("act", lambda e: e.activation(out=lfs_t[:, :], in_=lfs_t[:, :], func=AF.Ln, bias=1.0), waits=[t2])
        lfs_tok = R.op("dve", lambda e: e.tensor_scalar(out=lfs_t[:, :], in0=lfs_t[:, :], scalar1=-1.0, scalar2=None, op0=ALU.mult), waits=[t3])
        R.dma("sp", lfs_out[l], lfs_t[:, :], "small_out", waits=[lfs_tok])
        t4 = R.op("act", lambda e: e.activation(out=zf[:, :], in_=zf[:, :], func=AF.Ln, bias=1.0), waits=[zf_tok])
        lf_tok = R.op("dve", lambda e: e.tensor_scalar(out=zf[:, :], in0=zf[:, :], scalar1=-1.0, scalar2=None, op0=ALU.mult), waits=[t4])
        dlf = R.dma("sp", lf_out[l], zf[:, 0:S], "small_out", waits=[lf_tok])
        cs = hbuf
        zero8 = ubuf
        tz = R.op("pool", lambda e: e.memset(zero8[0:H, 0:S], 0.0))
        tc = R.op("dve", lambda e: e.tensor_tensor_scan(out=cs[0:H, 0:S], data0=zf[:, 0:S], data1=zero8[0:H, 0:S], initial=0.0, op0=ALU.add, op1=ALU.add), waits=[lf_tok, tz])
        tc = R.op("dve", lambda e: e.tensor_scalar(out=cs[0:H, 0:S], in0=cs[0:H, 0:S], scalar1=8.0, scalar2=None, op0=ALU.mult), waits=[tc])
        er1 = A.alloc("er1", [H, S], BF16)
        nr1 = A.alloc("nr1", [H, S], BF16)
        on1 = A.alloc("on1", [H, S], BF16)
        to = R.op("pool", lambda e: e.memset(on1[:, :], 1.0))
        res = zero8
        cur = cs
        tprev = tc
        erfree, nrfree = [], []
        aug = []
        for i in range(3):
            ta = R.op("dve", lambda e, cur=cur: e.tensor_copy(out=er1[:, :], in_=cur[0:H, 0:S]), waits=[tprev] + erfree)
            tn = R.op("dve", lambda e: e.tensor_scalar(out=nr1[:, :], in0=er1[:, :], scalar1=-1.0, scalar2=None, op0=ALU.mult), waits=[ta] + nrfree)
            if i < 2:
                tprev = R.op("dve", lambda e, cur=cur: e.tensor_tensor(out=res[0:H, 0:S], in0=cur[0:H, 0:S], in1=er1[:, :], op=ALU.subtract), waits=[tn])
                cur = res
            d_e = R.dma("sp", qT_s[:, 64 + i, 0:S], er1[:, :], "aug_e", waits=[ta])
            d_n = R.dma("sp", kT_s[:, 67 + i, 0:S], nr1[:, :], "aug_n", waits=[tn])
            erfree, nrfree = [d_e], [d_n]
            aug.append(R.dma("sp", qT_s[:, 67 + i, 0:S], on1[:, :], "aug_o", waits=[to]))
            aug.append(R.dma("sp", kT_s[:, 64 + i, 0:S], on1[:, :], "aug_o", waits=[to]))
        aug += [d_e, d_n]
        lastdve_split = R.last["dve"]
        conv_start_wait = aug + [dlf, lastdve_split]

        stage_gate()
        for gp in range(2):
            wh = load_w(w_in[l, :, OFF_H + gp * GW: OFF_H + (gp + 1) * GW], GW)
            wc = load_w(w_in[l, :, OFF_C + gp * GW: OFF_C + (gp + 1) * GW], GW)
            wg = load_w(w_in[l, :, OFF_B + gp * GW: OFF_B + (gp + 1) * GW], GW)
            for j in range(2):
                ct = gp * 2 + j
                th = None
                for (t0, w) in blocks:
                    k, bk, tk = fm_tile(wh[0], wh[2], j, t0, w)
                    th = R.op("act", lambda e, bk=bk, t0=t0, w=w: e.activation(out=hbuf[:, t0:t0 + w], in_=bk[:, :w], func=AF.Copy), waits=[tk] + conv_start_wait)
                    PS.free[k] = [th]
                conv_start_wait = []
                tz = R.op("pool", lambda e: e.memset(ubuf[:, 0:2], 0.0), waits=[th])
                tu = None
                for (t0, w) in blocks:
                    k, bk, tk = fm_tile(wc[0], wc[2], j, t0, w)
                    tu = R.op("dve", lambda e, bk=bk, t0=t0, w=w: e.tensor_tensor(out=ubuf[:, 2 + t0:2 + t0 + w], in0=bk[:, :w], in1=hbuf[:, t0:t0 + w], op=ALU.mult), waits=[tk, th])
                    PS.free[k] = [tu]
                dcp = R.dma("sp", convp_out[l, :, ct, :], ubuf[:, S:S + 2], "convp", waits=[tu])
                c1 = R.op("dve", lambda e, ct=ct: e.tensor_scalar(out=hbuf[:, 0:S], in0=ubuf[:, 0:S], scalar1=cw[:, l, ct, 0:1], scalar2=None, op0=ALU.mult), waits=[tu, tz])
                c2 = R.op("dve", lambda e, ct=ct: e.scalar_tensor_tensor(out=hbuf[:, 0:S], in0=ubuf[:, 1:S + 1], scalar=cw[:, l, ct, 1:2], in1=hbuf[:, 0:S], op0=ALU.mult, op1=ALU.add), waits=[c1])
                c3 = R.op("dve", lambda e, ct=ct: e.scalar_tensor_tensor(out=hbuf[:, 0:S], in0=ubuf[:, 2:S + 2], scalar=cw[:, l, ct, 2:3], in1=hbuf[:, 0:S], op0=ALU.mult, op1=ALU.add), waits=[c2])
                sst = A.alloc("sst", [128, NS, 2], F32)
                nst = A.alloc("nst", [128, NS, 2], F32)
                dS = R.dma("sp", sst[:, :, :], stT[l, :, ct, :, :], "sst")
                s1 = R.op("dve", lambda e, ct=ct: e.tensor_scalar(out=hbuf[:, S:S + NS], in0=sst[:, :, 0], scalar1=cw[:, l, ct, 0:1], scalar2=None, op0=ALU.mult), waits=[dS, tu])
                s2 = R.op("dve", lambda e, ct=ct: e.scalar_tensor_tensor(out=hbuf[:, S:S + NS], in0=sst[:, :, 1], scalar=cw[:, l, ct, 1:2], in1=hbuf[:, S:S + NS], op0=ALU.mult, op1=ALU.add), waits=[s1])
                s3 = R.op("dve", lambda e, ct=ct: e.scalar_tensor_tensor(out=hbuf[:, S:S + NS], in0=ubuf[:, 2 + S:2 + S + NS], scalar=cw[:, l, ct, 2:3], in1=hbuf[:, S:S + NS], op0=ALU.mult, op1=ALU.add), waits=[s2])
                n1 = R.op("pool", lambda e: e.tensor_copy(out=nst[:, :, 0], in_=sst[:, :, 1]), waits=[dS])
                n2 = R.op("pool", lambda e: e.tensor_copy(out=nst[:, :, 1], in_=ubuf[:, 2 + S:2 + S + NS]), waits=[tu])
                R.dma("sp", convs_out[l, :, ct, :, :], nst[:, :, :], "small_out", waits=[n1, n2])
                tcb = None
                for (t0, w) in blocks:
                    k, bk, tk = fm_tile(wg[0], wg[2], j, t0, w)
                    ko, ob, fro = obfR.get()
                    tcb = R.op("dve", lambda e, bk=bk, ob=ob, t0=t0, w=w: e.tensor_tensor(out=ob[:, :w], in0=bk[:, :w], in1=hbuf[:, t0:t0 + w], op=ALU.mult), waits=[tk, c3, s3] + fro)
                    PS.free[k] = [tcb]
                    d1 = R.dma("sp", cb_s[:, ct, t0:t0 + w], ob[:, :w], f"obf{ko}", waits=[tcb])
                    obfR.free[ko] = [d1]
                conv_start_wait = [tcb, n2, dcp]
            lastpe = R.last["pe"]
            for ww_ in (wh, wc, wg):
                wbfR.free[ww_[1]] = [lastpe]

        stage_gate()
        for g in range(8):
            wb, kb, wtok = load_w(w_in[l, :, OFF_G + g * GW: OFF_G + (g + 1) * GW], GW)
            for j in range(2):
                gt = g * 2 + j
                for (t0, w) in blocks:
                    k, bk, tk = fm_tile(wb, wtok, j, t0, w)
                    ko, ob, fro = obfR.get()
                    ev = R.op("act", lambda e, bk=bk, ob=ob, w=w, gt=gt: e.activation(out=ob[:, :w], in_=bk[:, :w], func=AF.Sigmoid, bias=bg[:, l, gt:gt + 1]), waits=[tk] + fro)
                    PS.free[k] = [ev]
                    d1 = R.dma("sp", g_s[:, gt, t0:t0 + w], ob[:, :w], f"obf{ko}", waits=[ev])
                    obfR.free[ko] = [d1]
            wbfR.free[kb] = [R.last["pe"]]
        R.barrier()

        stage_gate()
        A.reset(PBASE)
        kbuf = [A.alloc("kbuf", [KA, S], BF16) for _ in range(2)]
        qbuf = [A.alloc("qbuf", [KA, S], BF16) for _ in range(2)]
        vbuf = [A.alloc("vbuf", [128, NST, HD + 1], BF16) for _ in range(2)]
        pT = [A.alloc("pT", [128, TB], BF16) for _ in range(4)]
        pTR = Ring(pT)
        rl = A.alloc("rl", [65, TB], F32)
        rb = A.alloc("rb", [64, TB], F32)
        ast = [A.alloc("ast", [64, TB], BF16) for _ in range(2)]
        astR = Ring(ast)
        PO = Ring(banks[0:2])
        PSb = Ring(banks[2:5])
        PB = Ring(banks[5:6])
        OS = banks[6]
        MISC = banks[7]

        idx = A.alloc("idx", [GP, NGRP], I32)
        ind = A.alloc("ind", [GP, NGRP, NS], F32)
        indb = A.alloc("indb", [GP, NGRP, NS], BF16)
        indT = A.alloc("indT", [NS, NGRP, GP], F32)
        aft = A.alloc("aft", [GP, GP], F32)
        lfp = A.alloc("lfp", [GP, 128, H], F32)
        pfx = A.alloc("pfx", [GP, 128, H], F32)
        bias = A.alloc("bias", [GP, 128, H], F32)
        tot = A.alloc("tot", [GP, H], F32)
        qpp = A.alloc("qpp", [GP, 512], F32)
        Kc = [A.alloc("Kc", [GP, TK, 512], F32) for _ in range(2)]
        Vc = [A.alloc("Vc", [GP, TK, 512], F32) for _ in range(2)]
        prod = A.alloc("prod", [GP, TK, 512], F32)
        pv = A.alloc("pv", [GP, TK, 512], BF16)
        sc = A.alloc("sc", [GP, TK, H], F32)
        pe_ = A.alloc("pe_", [GP, 128, H], F32)
        rs = A.alloc("rs", [GP, H], F32)
        sm = A.alloc("sm", [NS, 8, 512], F32)
        zt = A.alloc("zt", [GP, 128], F32)
        asb = A.alloc("asb", [NS, 512], BF16)
        qs_d = dscr(f"qs_d{l}", [NS, 512], F32)
        as_d = dscr(f"as_d{l}", [NS, 512], BF16)

        def sample_stage():
            R.dma("sp", qs_d, qs_t[:, :], "smp")
            R.dma("sp", idx[:, :], pt, "smp")
            R.dma("sp", ind[:, :, :], c_ind.rearrange("g p b -> p g b"), "smp")
            R.dma("sp", indT[:, :, :], c_indT.rearrange("g b p -> b g p"), "smp")
            dcst = R.dma("sp", aft[:, :], c_after, "smp")
            tzt = R.op("pool", lambda e: e.memset(zt[:, :], 0.0))
            tib = R.op("pool", lambda e: e.tensor_copy(out=indb[:, :, :], in_=ind[:, :, :]), waits=[dcst])
            LSg = [MISC[0:NS, g_ * H:(g_ + 1) * H] for g_ in range(NGRP)]
            Bq = MISC[0:GP, 64:64 + H]
            bq_free = []
            first_mm = True
            r2_ = None
            grp_done = None
            prev_tmm = None
            tmm = None
            BG = GP // NPG
            for g in range(NGRP):
                dqq = None
                for bl in range(BG):
                    b = g * BG + bl
                    dqq = R.dma("sp", qpp[bl * NPG:(bl + 1) * NPG, :], qs_d[b:b + 1, :].partition_broadcast(NPG), "smp2", waits=[dcst, grp_done])
                glf = R.gather(lfp[:, :, :].rearrange("p t h -> p (t h)"), cache_lf[l], idx[:, g:g + 1], "glf", waits=[dcst, grp_done])
                tp = None
                for hh in range(H):
                    tp = R.op("dve", lambda e: e.tensor_tensor_scan(out=pfx[:, :, hh], data0=lfp[:, :, hh], data1=zt[:, :],
                                                                   initial=0.0, op0=ALU.add, op1=ALU.add), waits=[glf, tzt, grp_done])
                tt_ = R.op("dve", lambda e: e.tensor_copy(out=tot[:, :], in_=pfx[:, 127, :]), waits=[tp])
                R.op("pe", lambda e: e.matmul(Bq, lhsT=aft[:, :], rhs=tot[:, :], start=True, stop=False), waits=[tt_, dcst] + bq_free, sig=False)
                tb_ = R.op("pe", lambda e: e.matmul(Bq, lhsT=indT[:, g, :], rhs=lfs_t[:, :], start=False, stop=True), waits=[lfs_tok])
                tt2 = R.op("dve", lambda e: e.tensor_tensor(out=tot[:, :], in0=tot[:, :], in1=Bq, op=ALU.add), waits=[tb_])
                bq_free = [tt2]
                tbias = R.op("dve", lambda e: e.tensor_tensor(out=bias[:, :, :], in0=tot[:, :].unsqueeze(1).to_broadcast([GP, 128, H]), in1=pfx[:, :, :], op=ALU.subtract), waits=[tt2])
                kfree = [[], []]
                vfree = [[], []]
                prev_m4 = None
                yield
                for ch in range(NCHUNK):
                    sl = ch % 2
                    a0 = ch * TK * 512
                    gk = R.gather(Kc[sl][:, :, :].rearrange("p t e -> p (t e)"), cache_k[l], idx[:, g:g + 1], f"gk{sl}", waits=kfree[sl], eoff=a0)
                    gv = R.gather(Vc[sl][:, :, :].rearrange("p t e -> p (t e)"), cache_v[l], idx[:, g:g + 1], f"gv{sl}", waits=vfree[sl], eoff=a0)
                    m1 = R.op("dve", lambda e: e.tensor_tensor(out=prod[:, :, :], in0=Kc[sl][:, :, :], in1=qpp[:, :].unsqueeze(1).to_broadcast([GP, TK, 512]), op=ALU.mult), waits=[gk, dqq])
                    kfree[sl] = [m1]
                    m2 = R.op("dve", lambda e: e.tensor_reduce(out=sc[:, :, :].rearrange("p t h -> p (t h)"), in_=prod[:, :, :].rearrange("p t (h d) -> p (t h) d", h=H), axis=AX.X, op=ALU.add), waits=[m1, prev_m4])
                    m3 = R.op("dve", lambda e: e.scalar_tensor_tensor(out=sc[:, :, :], in0=sc[:, :, :], scalar=0.125, in1=bias[:, ch * TK:(ch + 1) * TK, :], op0=ALU.mult, op1=ALU.add), waits=[m2, tbias])
                    m4 = R.op("act", lambda e: e.activation(out=pe_[:, ch * TK:(ch + 1) * TK, :], in_=sc[:, :, :], func=AF.Exp), waits=[m3])
                    prev_m4 = m4
                    m5 = R.op("dve", lambda e: e.tensor_tensor(out=pv[:, :, :].rearrange("p t (h d) -> p t h d", h=H), in0=Vc[sl][:, :, :].rearrange("p t (h d) -> p t h d", h=H),
                                                               in1=pe_[:, ch * TK:(ch + 1) * TK, :].unsqueeze(3).to_broadcast([GP, TK, H, HD]), op=ALU.mult), waits=[m4, gv, prev_tmm])
                    vfree[sl] = [m5]
                    for tk_ in range(TK):
                        lastmm = (g == NGRP - 1 and ch == NCHUNK - 1 and tk_ == TK - 1)
                        tmm = R.op("pe", lambda e: e.matmul(OS[0:NS, :], lhsT=indb[:, g, :], rhs=pv[:, tk_, :], start=first_mm, stop=lastmm),
                                   waits=([m5, tib]) if tk_ == 0 else [], sig=(tk_ == TK - 1))
                        first_mm = False
                    prev_tmm = tmm
                    yield
                r1 = R.op("dve", lambda e: e.tensor_reduce(out=rs[:, :], in_=pe_[:, :, :].rearrange("p t h -> p h t"), axis=AX.X, op=ALU.add), waits=[prev_m4, r2_])
                r2_ = R.op("pe", lambda e: e.matmul(LSg[g], lhsT=ind[:, g, :], rhs=rs[:, :], start=True, stop=True), waits=[r1])
                grp_done = r1
            w_q, w_p, w_o_, w_e, w_l = sm[:, 0, :], sm[:, 1, :], sm[:, 2, :], sm[:, 3, 0:H], sm[:, 4, 0:H]
            n1 = R.op("dve", lambda e: e.tensor_tensor(out=w_q, in0=qs_t[:, :], in1=ks_t[:, :], op=ALU.mult))
            n2 = R.op("dve", lambda e: e.tensor_reduce(out=w_e, in_=w_q.rearrange("p (h d) -> p h d", h=H), axis=AX.X, op=ALU.add), waits=[n1])
            n3 = R.op("act", lambda e: e.activation(out=w_e, in_=w_e, func=AF.Exp, scale=0.125), waits=[n2])
            n4 = R.op("dve", lambda e: e.tensor_tensor(out=w_l, in0=w_e, in1=LSg[0], op=ALU.add), waits=[n3, r2_])
            for g_ in range(1, NGRP):
                n4 = R.op("dve", lambda e: e.tensor_tensor(out=w_l, in0=w_l, in1=LSg[g_], op=ALU.add), waits=[n4])
            n5 = R.op("dve", lambda e: e.reciprocal(out=w_l, in_=w_l), waits=[n4])
            n6 = R.op("dve", lambda e: e.tensor_tensor(out=w_p.rearrange("p (h d) -> p h d", h=H), in0=vs_t[:, :].rearrange("p (h d) -> p h d", h=H),
                                                       in1=w_e.unsqueeze(2).to_broadcast([NS, H, HD]), op=ALU.mult), waits=[n3])
            n7 = R.op("dve", lambda e: e.tensor_tensor(out=w_o_, in0=w_p, in1=OS[0:NS, :], op=ALU.add), waits=[n6, tmm])
            n8 = R.op("dve", lambda e: e.tensor_tensor(out=asb[:, :].rearrange("p (h d) -> p h d", h=H), in0=w_o_.rearrange("p (h d) -> p h d", h=H),
                                                       in1=w_l.unsqueeze(2).to_broadcast([NS, H, HD]), op=ALU.mult), waits=[n7, n5])
            d1 = R.dma("sp", as_d, asb[:, :], "smp3", waits=[n8])
            for b in range(NS):
                R.dma("sp", a_s[:, :, S + b:S + b + 1], as_d[b:b + 1, :].rearrange("b (c p) -> p c b", p=128), "smp3", waits=[d1], allow_slow_non_contiguous=True)
            yield

        sgen = sample_stage()

        hfree = [[], []]
        htoks = {}

        def load_head(h):
            sl = h % 2
            fr = hfree[sl]
            R.dma("sp", kbuf[sl][:, :], kT_s[h, :, 0:S], f"hd{sl}", waits=fr)
            R.dma("sp", qbuf[sl][:, :], qT_s[h, :, 0:S], f"hd{sl}", waits=fr)
            htoks[h] = R.dma("sp", vbuf[sl][:, :, :], V_s[h].rearrange("(t p) d -> p t d", p=128), f"hd{sl}", waits=fr)

        tiles = []
        for h in range(H):
            for tb in range(NB):
                n_s = (tb * TB + TB) // 128
                for si in range(n_s):
                    tiles.append((h, tb, si, n_s))
        qk_state = {}
        blk_state = {}
        norm_state = {"rl_free": [], "rb_free": []}

        def emit_qk(i):
            h, tb, si, n_s = tiles[i]
            sl = h % 2
            kb_, qb_ = kbuf[sl], qbuf[sl]
            t0 = tb * TB
            j = si - t0 // 128
            diag = j >= 0
            c0 = 128 * j if diag else 0
            ks_, Sb, frs = PSb.get()
            w0 = [htoks[h]] + frs
            if diag:
                R.op("pe", lambda e: e.matmul(Sb[:, c0:c0 + 128], lhsT=kb_[:, si * 128:(si + 1) * 128], rhs=qb_[:, t0 + c0:t0 + c0 + 128], start=True, stop=False), waits=w0, sig=False)
                tq = R.op("pe", lambda e: e.matmul(Sb[:, c0:c0 + 128], lhsT=ident[:, :], rhs=maskb[:, :], start=False, stop=True))
                if c0 + 128 < TB:
                    tq = R.op("pe", lambda e: e.matmul(Sb[:, c0 + 128:TB], lhsT=kb_[:, si * 128:(si + 1) * 128], rhs=qb_[:, t0 + c0 + 128:t0 + TB], start=True, stop=True))
            else:
                tq = R.op("pe", lambda e: e.matmul(Sb[:, :], lhsT=kb_[:, si * 128:(si + 1) * 128], rhs=qb_[:, t0:t0 + TB], start=True, stop=True), waits=w0)
            kp, P_, frp = pTR.get()
            te = R.op("act", lambda e: e.activation(out=P_[:, c0:TB], in_=Sb[:, c0:TB], func=AF.Exp, scale=0.125), waits=[tq] + frp)
            PSb.free[ks_] = [te]
            qk_state[i] = (te, P_, kp, c0)

        def emit_pv(i):
            h, tb, si, n_s = tiles[i]
            sl = h % 2
            vb_ = vbuf[sl]
            t0 = tb * TB
            te, P_, kp, c0 = qk_state.pop(i)
            if si == 0:
                ko_, O, fro = PO.get()
                blk_state[(h, tb)] = (ko_, O)
            else:
                ko_, O = blk_state[(h, tb)]
                fro = []
            pv_tok = R.op("pe", lambda e: e.matmul(O[0:HD + 1, c0:TB], lhsT=vb_[:, si, :], rhs=P_[:, c0:TB], start=(si == 0), stop=(si == n_s - 1)),
                          waits=[te] + fro)
            pTR.free[kp] = [pv_tok]
            if si == n_s - 1:
                t1 = R.op("dve", lambda e: e.reciprocal(out=rl[64:65, :], in_=O[64:65, :]), waits=[pv_tok] + norm_state["rl_free"])
                kb2, Bc, frb = PB.get()
                t2 = R.op("pe", lambda e: e.matmul(Bc[0:64, :], lhsT=onesf[64:65, 0:64], rhs=rl[64:65, :], start=True, stop=True), waits=[t1] + frb)
                norm_state["rl_free"] = [t2]
                t3 = R.op("act", lambda e: e.activation(out=rb[:, :], in_=Bc[0:64, :], func=AF.Copy), waits=[t2] + norm_state["rb_free"])
                PB.free[kb2] = [t3]
                ka, as_, fra = astR.get()
                t4 = R.op("dve", lambda e: e.tensor_tensor(out=as_[:, :], in0=O[0:64, :], in1=rb[:, :], op=ALU.mult), waits=[t3] + fra)
                norm_state["rb_free"] = [t4]
                PO.free[ko_] = [t4]
                d = R.dma("sp", a_s[(h % 2) * 64:(h % 2) * 64 + 64, h // 2, t0:t0 + TB], as_[:, :], f"ast{ka}", waits=[t4])
                astR.free[ka] = [d]
                del blk_state[(h, tb)]
                if tb == NB - 1:
                    hfree[sl] = [pv_tok]
                    if h + 2 < H:
                        load_head(h + 2)

        load_head(0)
        if H > 1:
            load_head(1)
        LA = 2
        NTL = len(tiles)
        for i in range(min(LA, NTL)):
            emit_qk(i)
        for i in range(NTL):
            if i + LA < NTL:
                emit_qk(i + LA)
            emit_pv(i)
            if i % 20 == 10:
                next(sgen, None)
        for _ in sgen:
            pass
        R.barrier()

        stage_gate()
        A.reset(PBASE)
        wpa = A.alloc("wpa", [128, 4, D], BF16)
        wpc = A.alloc("wpc", [128, 4, D], BF16)
        wo = A.alloc("wo", [128, DCH, D], BF16)
        wst3 = A.alloc("wst3", [128, DCH, D // 2], F32)
        wtoks = []
        stfree = []
        for (dst_, src_, nch) in ((wpa, w_pa[l], 4), (wpc, w_pc[l], 4), (wo, w_o[l], DCH)):
            for hf in range(2):
                d = R.dma("sp", wst3[:, 0:nch, :], src_.rearrange("(c p) n -> p c n", p=128)[:, :, hf * 512:(hf + 1) * 512], "wst3", waits=stfree)
                c = R.op("pool", lambda e, dst_=dst_, nch=nch, hf=hf: e.tensor_copy(out=dst_[:, :, hf * 512:(hf + 1) * 512], in_=wst3[:, 0:nch, :]), waits=[d])
                stfree = [c]
        w3tok = c
        MB = A.mark()
        ab = A.alloc("ab", [128, 4, TB], BF16)
        cbb = A.alloc("cbb", [128, 4, TB], BF16)
        gb_ = A.alloc("gb_", [128, 16, TB], BF16)
        xb = A.alloc("xb", [128, DCH, TB], F32)
        mb = A.alloc("mb", [128, DCH, TB], BF16)
        t1R = Ring([A.alloc("t1b", [128, TB], F32) for _ in range(2)])
        t2R = Ring([A.alloc("t2b", [128, TB], F32) for _ in range(2)])
        rbuf = A.alloc("rbuf", [128, DCH, TB], F32)
        lb = ln_bufs()
        in_free = []
        r_free = []
        for (t0, w) in blocks:
            dl = [R.dma("sp", ab[:, :, :w], a_s[:, :, t0:t0 + w], "s3in", waits=in_free),
                  R.dma("sp", cbb[:, :, :w], cb_s[:, :, t0:t0 + w], "s3in", waits=in_free),
                  R.dma("sp", gb_[:, :, :w], g_s[:, :, t0:t0 + w], "s3in", waits=in_free),
                  R.dma("sp", xb[:, :, :w], x_src[:, :, t0:t0 + w], "s3in", waits=in_free)]
            dld = dl[-1]
            mt = []
            for dt in range(DCH):
                ka_, Ab, fra = psget()
                for e_ in range(4):
                    ta = R.op("pe", lambda e, e_=e_, dt=dt, Ab=Ab: e.matmul(Ab[:, :w], lhsT=wpa[:, e_, dt * 128:(dt + 1) * 128], rhs=ab[:, e_, :w], start=(e_ == 0), stop=(e_ == 3)),
                              waits=([dld, w3tok] + fra) if e_ == 0 else [], sig=(e_ == 3))
                kc_, Cb, frc = psget()
                for e_ in range(4):
                    tc_ = R.op("pe", lambda e, e_=e_, dt=dt, Cb=Cb: e.matmul(Cb[:, :w], lhsT=wpc[:, e_, dt * 128:(dt + 1) * 128], rhs=cbb[:, e_, :w], start=(e_ == 0), stop=(e_ == 3)),
                               waits=frc if e_ == 0 else [], sig=(e_ == 3))
                k1_, t1b, f1_ = t1R.get()
                k2_, t2b, f2_ = t2R.get()
                u1 = R.op("dve", lambda e: e.tensor_tensor(out=t1b[:, :w], in0=Ab[:, :w], in1=gb_[:, dt, :w], op=ALU.mult), waits=[ta] + f1_)
                u2 = R.op("dve", lambda e: e.tensor_tensor(out=t2b[:, :w], in0=Cb[:, :w], in1=gb_[:, 8 + dt, :w], op=ALU.mult), waits=[tc_] + f2_)
                PS.free[ka_] = [u1]
                PS.free[kc_] = [u2]
                u3 = R.op("pool", lambda e: e.tensor_tensor(out=mb[:, dt, :w], in0=t1b[:, :w], in1=t2b[:, :w], op=ALU.add), waits=[u1, u2] + (r_free if dt == 0 else []))
                t1R.free[k1_] = [u3]
                t2R.free[k2_] = [u3]
                mt.append(u3)
            rtoks = []
            for dt in range(DCH):
                kt_, Tb, frt = psget()
                for c in range(DCH):
                    tt_ = R.op("pe", lambda e, c=c, dt=dt, Tb=Tb: e.matmul(Tb[:, :w], lhsT=wo[:, c, dt * 128:(dt + 1) * 128], rhs=mb[:, c, :w], start=(c == 0), stop=(c == DCH - 1)),
                               waits=(mt + frt) if c == 0 else [], sig=(c == DCH - 1))
                rt = R.op("dve", lambda e, dt=dt, Tb=Tb: e.scalar_tensor_tensor(out=rbuf[:, dt, :w], in0=xb[:, dt, :w], scalar=ALPHA, in1=Tb[:, :w], op0=ALU.mult, op1=ALU.add),
                          waits=[tt_] + (lb.get("r_free", []) if dt == 0 else []))
                PS.free[kt_] = [rt]
                rtoks.append(rt)
            in_free = [rtoks[-1], R.last["pe"]]
            r_free = [R.last["pe"]]
            outs, lt = layer_norm_block(l, 1, rbuf, w, x32_s[0], x1bf_s, t0, lb, rtoks)
            lb["r_free"] = lt
        R.barrier()

        stage_gate()
        A.reset(PBASE)
        xbf = A.alloc("xbf4", [128, DCH, NT], BF16)
        dx = R.dma("sp", xbf[:, :, :], x1bf_s, "s4x")
        wst4 = [A.alloc("wst4", [128, DCH, 256], F32) for _ in range(2)]
        wst4R = Ring(wst4)
        wb4 = [A.alloc("wb4", [128, DCH, 256], BF16) for _ in range(3)]
        wb4R = Ring(wb4)
        sg = [A.alloc("sg", [128, TB], F32) for _ in range(2)]
        sgR = Ring(sg)
        so = [A.alloc("so", [128, TB], BF16) for _ in range(3)]
        soR = Ring(so)
        for ft in range(FT):
            k, st, fr = wst4R.get()
            d1 = R.dma("sp", st[:, :, 0:128], w_gu[l, :, ft * 128:(ft + 1) * 128].rearrange("(c p) n -> p c n", p=128), f"wst4{k}", waits=fr)
            d2 = R.dma("sp", st[:, :, 128:256], w_gu[l, :, DFF + ft * 128:DFF + (ft + 1) * 128].rearrange("(c p) n -> p c n", p=128), f"wst4{k}", waits=fr)
            kb, wb, frb = wb4R.get()
            c = R.op("pool", lambda e, wb=wb, st=st: e.tensor_copy(out=wb[:, :, :], in_=st[:, :, :]), waits=[d2] + frb)
            wst4R.free[k] = [c]
            for (t0, w) in blocks:
                kg, Gb, frg = psget()
                for cc in range(DCH):
                    tg = R.op("pe", lambda e, cc=cc, Gb=Gb, wb=wb, t0=t0, w=w: e.matmul(Gb[:, :w], lhsT=wb[:, cc, 0:128], rhs=xbf[:, cc, t0:t0 + w], start=(cc == 0), stop=(cc == DCH - 1)),
                              waits=([c, dx] + frg) if cc == 0 else [], sig=(cc == DCH - 1))
                ku, Ub, fru = psget()
                for cc in range(DCH):
                    tu = R.op("pe", lambda e, cc=cc, Ub=Ub, wb=wb, t0=t0, w=w: e.matmul(Ub[:, :w], lhsT=wb[:, cc, 128:256], rhs=xbf[:, cc, t0:t0 + w], start=(cc == 0), stop=(cc == DCH - 1)),
                              waits=fru if cc == 0 else [], sig=(cc == DCH - 1))
                ksg, sgt, frsg = sgR.get()
                a1 = R.op("act", lambda e, sgt=sgt, Gb=Gb, w=w: e.activation(out=sgt[:, :w], in_=Gb[:, :w], func=AF.Silu), waits=[tg] + frsg)
                PS.free[kg] = [a1]
                kso, sot, frso = soR.get()
                a2 = R.op("dve", lambda e, sgt=sgt, Ub=Ub, sot=sot, w=w: e.tensor_tensor(out=sot[:, :w], in0=Ub[:, :w], in1=sgt[:, :w], op=ALU.mult), waits=[a1, tu] + frso)
                PS.free[ku] = [a2]
                sgR.free[ksg] = [a2]
                d = R.dma("sp", s_s[:, ft, t0:t0 + w], sot[:, :w], f"so{kso}", waits=[a2])
                soR.free[kso] = [d]
            wb4R.free[kb] = [R.last["pe"]]
        R.barrier()

        stage_gate()
        A.reset(PBASE)
        wd = A.alloc("wd", [128, FT, D], BF16)
        wstd = [A.alloc("wstd", [128, 2, D], F32) for _ in range(2)]
        wstdR = Ring(wstd)
        c = None
        for f2 in range(FT // 2):
            k, st, fr = wstdR.get()
            d = R.dma("sp", st[:, :, :], w_dn[l, f2 * 256:(f2 + 1) * 256, :].rearrange("(c p) n -> p c n", p=128), f"wstd{k}", waits=fr)
            c = R.op("pool", lambda e, st=st, f2=f2: e.tensor_copy(out=wd[:, 2 * f2:2 * f2 + 2, :], in_=st[:, :, :]), waits=[d])
            wstdR.free[k] = [c]
        wdtok = c
        sb_ = A.alloc("sb_", [128, FT, TB], BF16)
        xb = A.alloc("xb4", [128, DCH, TB], F32)
        rbuf = A.alloc("rbuf4", [128, DCH, TB], F32)
        lb = ln_bufs()
        in_free = []
        for (t0, w) in blocks:
            d1 = R.dma("sp", sb_[:, :, :w], s_s[:, :, t0:t0 + w], "s4in", waits=in_free)
            d2 = R.dma("sp", xb[:, :, :w], x32_s[0][:, :, t0:t0 + w], "s4in", waits=in_free)
            rtoks = []
            for dt in range(DCH):
                kf, Fb, frf = psget()
                for ft in range(FT):
                    tf = R.op("pe", lambda e, ft=ft, dt=dt, Fb=Fb: e.matmul(Fb[:, :w], lhsT=wd[:, ft, dt * 128:(dt + 1) * 128], rhs=sb_[:, ft, :w], start=(ft == 0), stop=(ft == FT - 1)),
                              waits=([d2, wdtok] + frf) if ft == 0 else [], sig=(ft == FT - 1))
                rt = R.op("dve", lambda e, dt=dt, Fb=Fb: e.scalar_tensor_tensor(out=rbuf[:, dt, :w], in0=xb[:, dt, :w], scalar=ALPHA, in1=Fb[:, :w], op0=ALU.mult, op1=ALU.add),
                          waits=[tf] + (lb.get("r_free", []) if dt == 0 else []))
                PS.free[kf] = [rt]
                rtoks.append(rt)
            in_free = [rtoks[-1], R.last["pe"]]
            outs, lt = layer_norm_block(l, 2, rbuf, w, y_dst, None, t0, lb, rtoks)
            lb["r_free"] = lt
        R.barrier()

    except _Stop:
        pass
    final = R.all_tokens()
    R._waits("sp", final)
    with nc.Block() as block:
        @block.sync
        def _(e):
            for f in R.st["sp"]:
                f(e)

        @block.tensor
        def _(e):
            for f in R.st["pe"]:
                f(e)

        @block.scalar
        def _(e):
            for f in R.st["act"]:
                f(e)

        @block.vector
        def _(e):
            for f in R.st["dve"]:
                f(e)

        @block.gpsimd
        def _(e):
            for f in R.st["pool"]:
                f(e)
    return nc


def consts(NPG):
    NPAGES = NS * NPG
    GP = min(128, NPAGES)
    NGRP = NPAGES // GP
    bf = ml_dtypes.bfloat16
    s = np.arange(128)[:, None]
    t = np.arange(128)[None, :]
    c = {
        "c_ident": np.eye(128, dtype=np.float32).astype(bf),
        "c_maskb": np.where(s <= t, 0.0, NEG).astype(np.float32).astype(bf),
        "c_onesb": np.ones((128, 128), np.float32).astype(bf),
        "c_onesf": np.ones((128, 128), np.float32),
    }
    ind = np.zeros((NGRP, GP, NS), np.float32)
    p = np.arange(GP)
    for g in range(NGRP):
        ind[g, p, (g * GP + p) // NPG] = 1.0
    c["c_ind"] = ind
    c["c_indT"] = np.ascontiguousarray(ind.transpose(0, 2, 1))
    bb = p // NPG
    pg = p % NPG
    c["c_after"] = ((bb[:, None] == bb[None, :]) & (pg[:, None] > pg[None, :])).astype(np.float32)
    return c


def make_in_maps(inp, n_cores, S, NPG):
    L = inp["w_in"].shape[0]
    NPOOL = inp["cache_k"].shape[1]
    NPAGES = NS * NPG
    GP = min(128, NPAGES)
    NGRP = NPAGES // GP
    f = np.float32
    cst = consts(NPG)
    shared = {
        "w_in": np.ascontiguousarray(inp["w_in"], f),
        "b_f": np.ascontiguousarray(inp["b_f"].reshape(L, H, 1), f),
        "b_fr": np.ascontiguousarray(np.broadcast_to(inp["b_f"][:, None, :], (L, NS, H)), f),
        "b_gate": np.ascontiguousarray(inp["b_gate"].reshape(L, 16, 128).transpose(0, 2, 1), f),
        "conv_w": np.ascontiguousarray(inp["conv_w"].reshape(L, 3, 4, 128).transpose(0, 3, 2, 1), f),
        "w_pa": np.ascontiguousarray(inp["w_attn_proj"], f),
        "w_pc": np.ascontiguousarray(inp["w_conv_proj"], f),
        "w_o": np.ascontiguousarray(inp["w_out"], f),
        "ln1g": np.ascontiguousarray(inp["ln1_g"].reshape(L, DCH, 128).transpose(0, 2, 1), f),
        "ln1b": np.ascontiguousarray(inp["ln1_b"].reshape(L, DCH, 128).transpose(0, 2, 1), f),
        "ln2g": np.ascontiguousarray(inp["ln2_g"].reshape(L, DCH, 128).transpose(0, 2, 1), f),
        "ln2b": np.ascontiguousarray(inp["ln2_b"].reshape(L, DCH, 128).transpose(0, 2, 1), f),
        "w_gu": np.ascontiguousarray(inp["w_gate_up"], f),
        "w_dn": np.ascontiguousarray(inp["w_down"], f),
    }
    for l in range(L):
        shared[f"cache_k{l}"] = np.ascontiguousarray(inp["cache_k"][l], f).reshape(NPOOL, 128 * 512)
        shared[f"cache_v{l}"] = np.ascontiguousarray(inp["cache_v"][l], f).reshape(NPOOL, 128 * 512)
        shared[f"cache_lf{l}"] = np.ascontiguousarray(inp["cache_logf"][l], f).reshape(NPOOL, 128 * H)
    shared.update(cst)
    nb = inp["x_prompt"].shape[0]
    maps = []
    for c in range(n_cores):
        b = c % nb
        sb = slice(NS * c, NS * c + NS)
        m = dict(shared)
        m["xT"] = np.ascontiguousarray(np.concatenate([inp["x_prompt"][b].T, inp["x_sample"][sb, 0, :].T], axis=1), f)
        st = inp["state_conv"][:, sb]
        m["stT"] = np.ascontiguousarray(st.transpose(0, 3, 1, 2).reshape(L, 4, 128, NS, 2).transpose(0, 2, 1, 3, 4), f)
        m["pt"] = np.ascontiguousarray(inp["page_table"][sb].reshape(NGRP, GP).T, np.int32)
        maps.append(m)
    return maps


def assemble(res, inp, n_cores, S):
    L = inp["w_in"].shape[0]
    nb = inp["x_prompt"].shape[0]
    nsb = inp["x_sample"].shape[0]
    f = np.float32
    y_p = np.zeros((nb, S, D), f)
    y_s = np.zeros((nsb, 1, D), f)
    k_p = np.zeros((L, nb, S, H, HD), f)
    v_p = np.zeros((L, nb, S, H, HD), f)
    lf_p = np.zeros((L, nb, S, H), f)
    cv_p = np.zeros((L, nb, 2, 512), f)
    k_s = np.zeros((L, nsb, 1, H, HD), f)
    v_s = np.zeros((L, nsb, 1, H, HD), f)
    lf_s = np.zeros((L, nsb, 1, H), f)
    cv_s = np.zeros((L, nsb, 2, 512), f)
    for c in range(n_cores):
        r = res[c]
        sb = slice(NS * c, NS * c + NS)
        yT = np.asarray(r["yT"])
        y_s[sb, 0, :] = yT[:, S:].T
        k_s[:, sb, 0] = np.asarray(r["ks_out"]).reshape(L, NS, H, HD)
        v_s[:, sb, 0] = np.asarray(r["vs_out"]).reshape(L, NS, H, HD)
        lf_s[:, sb, 0] = np.asarray(r["lfs_out"])
        cv_s[:, sb] = np.asarray(r["convs_out"]).transpose(0, 3, 4, 2, 1).reshape(L, NS, 2, 512)
        if c < nb:
            b = c
            y_p[b] = yT[:, :S].T
            k_p[:, b] = np.asarray(r["kT_out"]).transpose(0, 2, 1).reshape(L, S, H, HD)
            v_p[:, b] = np.asarray(r["v_out"]).reshape(L, S, H, HD)
            lf_p[:, b] = np.asarray(r["lf_out"]).transpose(0, 2, 1)
            cv_p[:, b] = np.asarray(r["convp_out"]).transpose(0, 3, 2, 1).reshape(L, 2, 512)
    return (y_p, y_s, k_p, v_p, lf_p, cv_p, k_s, v_s, lf_s, cv_s)


_NC_CACHE = {}


def kernel(**inputs):
    inp = {k: np.asarray(v) for k, v in inputs.items()}
    S = inp["x_prompt"].shape[1]
    NPG = inp["page_table"].shape[1]
    NPOOL = inp["cache_k"].shape[1]
    n_cores = 8
    key = (S, NPG, NPOOL)
    if key not in _NC_CACHE:
        _NC_CACHE[key] = build(S, NPG, NPOOL)
    nc = _NC_CACHE[key]
    in_maps = make_in_maps(inp, n_cores, S, NPG)
    res = run_bass_kernel_spmd(nc, in_maps, core_ids=list(range(n_cores)))
    return assemble(res.results, inp, n_cores, S)
```

```python
import os
import numpy as np
import ml_dtypes
import concourse.bass as bass
import concourse.mybir as mybir
from concourse.bass_utils import run_bass_kernel_spmd

F32 = mybir.dt.float32
BF16 = mybir.dt.bfloat16
I32 = mybir.dt.int32
AF = mybir.ActivationFunctionType
ALU = mybir.AluOpType
AX = mybir.AxisListType

D = 1024
DCH = 8
H = 8
HD = 64
DIN = 5128
DFF = 2816
FT = 22
OFF_Q, OFF_K, OFF_V, OFF_F, OFF_H, OFF_B, OFF_C, OFF_G = 0, 512, 1024, 1536, 1544, 2056, 2568, 3080
NS = 4
TB = 512
KA = 70
ALPHA = float(4 ** 0.25)
LN_EPS = 1e-5
NEG = -30000.0


class _Proxy:
    def __init__(self):
        self.call = None

    def __getattr__(self, name):
        def f(*a, **k):
            self.call = (name, a, k)
            return self
        return f


class Rec:
    ENG = ("pe", "act", "dve", "pool", "sp")

    def __init__(self, nc):
        self.nc = nc
        self.st = {e: [] for e in self.ENG}
        self.psem = {e: nc.alloc_semaphore("P_" + e) for e in ("pe", "act", "dve", "pool")}
        self.pcnt = {e: 0 for e in self.psem}
        self.waited = {}
        self.dsem = {}
        self.dcnt = {}
        self.last = {}

    def _waits(self, eng, waits):
        for tok in waits:
            if tok is None:
                continue
            if isinstance(tok, list):
                self._waits(eng, tok)
                continue
            sem, val = tok
            key = (eng, sem.num)
            if self.waited.get(key, 0) >= val:
                continue
            self.waited[key] = val
            self.st[eng].append(lambda e, sem=sem, val=val: e.wait_ge(sem, val))

    def op(self, eng, fn, waits=(), sig=True):
        self._waits(eng, waits)
        px = _Proxy()
        fn(px)
        name, a, k = px.call
        if sig:
            self.pcnt[eng] += 1
            sem = self.psem[eng]
            tok = (sem, self.pcnt[eng])
            self.st[eng].append(lambda e, name=name, a=a, k=k, sem=sem: getattr(e, name)(*a, **k).then_inc(sem, 1))
            self.last[eng] = tok
            return tok
        self.st[eng].append(lambda e, name=name, a=a, k=k: getattr(e, name)(*a, **k))
        return None

    def dma(self, eng, out, in_, ch, waits=(), **kw):
        self._waits(eng, waits)
        if ch not in self.dsem:
            self.dsem[ch] = self.nc.alloc_semaphore("D_" + ch)
            self.dcnt[ch] = 0
        self.dcnt[ch] += 16
        sem = self.dsem[ch]
        self.st[eng].append(lambda e, out=out, in_=in_, sem=sem, kw=kw: e.dma_start(out=out, in_=in_, **kw).then_inc(sem, 16))
        return (sem, self.dcnt[ch])

    def gather(self, out, in_, idx, ch, waits=(), eoff=0):
        eng = "pool"
        self._waits(eng, waits)
        if ch not in self.dsem:
            self.dsem[ch] = self.nc.alloc_semaphore("D_" + ch)
            self.dcnt[ch] = 0
        self.dcnt[ch] += 16
        sem = self.dsem[ch]
        self.st[eng].append(lambda e, out=out, in_=in_, idx=idx, sem=sem, eoff=eoff: e.indirect_dma_start(
            out=out, out_offset=None, in_=in_,
            in_offset=bass.IndirectOffsetOnAxis(ap=idx, axis=0), element_offset=eoff).then_inc(sem, 16))
        return (sem, self.dcnt[ch])

    def all_tokens(self):
        toks = [t for t in self.last.values()]
        toks += [(self.dsem[c], self.dcnt[c]) for c in self.dsem]
        return toks

    def barrier(self):
        toks = self.all_tokens()
        for e in self.ENG:
            self._waits(e, toks)


class Arena:
    def __init__(self, nc, lo, hi):
        self.nc, self.lo, self.hi, self.cur, self.n = nc, lo, hi, lo, 0

    def alloc(self, name, shape, dtype):
        nbytes = int(np.prod(shape[1:])) * (4 if dtype in (F32, I32) else 2)
        off = (self.cur + 31) // 32 * 32
        assert off + nbytes <= self.hi, f"SBUF arena overflow at {name}: {off + nbytes} > {self.hi}"
        self.cur = off + nbytes
        Arena_cnt[0] += 1
        return self.nc.alloc_sbuf_tensor_at(f"{name}_{Arena_cnt[0]}", list(shape), dtype, offset=off)

    def mark(self):
        return self.cur

    def reset(self, m):
        self.cur = m


Arena_cnt = [0]


class Ring:
    def __init__(self, tiles):
        self.t = tiles
        self.free = [[] for _ in tiles]
        self.i = 0

    def get(self):
        k = self.i % len(self.t)
        self.i += 1
        fr = self.free[k]
        self.free[k] = []
        return k, self.t[k], fr


class _Stop(Exception):
    pass


def build(S, NPG, NPOOL, L=2, stop=None):
    stage_ctr = [0]

    def stage_gate():
        if stop is not None and stage_ctr[0] >= stop:
            raise _Stop()
        stage_ctr[0] += 1
    NT = S + NS
    NB = S // TB
    NST = S // 128
    blocks = [(i * TB, TB) for i in range(NB)] + [(S, NS)]
    NPAGES = NS * NPG
    GP = min(128, NPAGES)
    NGRP = NPAGES // GP
    TK = 8
    NCHUNK = 128 // TK

    nc = bass.Bass("TRN2", target_bir_lowering=False)

    def din(name, shape, dt=F32):
        return nc.dram_tensor(name, list(shape), dt, kind="ExternalInput").ap()

    def dout(name, shape, dt=F32):
        return nc.dram_tensor(name, list(shape), dt, kind="ExternalOutput").ap()

    def dscr(name, shape, dt):
        return nc.dram_tensor(name, list(shape), dt, kind="Internal").ap()

    xT = din("xT", [D, NT])
    w_in = din("w_in", [L, D, DIN])
    b_f = din("b_f", [L, H, 1])
    b_fr = din("b_fr", [L, NS, H])
    b_gate = din("b_gate", [L, 128, 16])
    conv_w = din("conv_w", [L, 128, 4, 3])
    w_pa = din("w_pa", [L, 512, D])
    w_pc = din("w_pc", [L, 512, D])
    w_o = din("w_o", [L, D, D])
    ln1g = din("ln1g", [L, 128, DCH])
    ln1b = din("ln1b", [L, 128, DCH])
    ln2g = din("ln2g", [L, 128, DCH])
    ln2b = din("ln2b", [L, 128, DCH])
    w_gu = din("w_gu", [L, D, 2 * DFF])
    w_dn = din("w_dn", [L, DFF, D])
    cache_k = [din(f"cache_k{l}", [NPOOL, 128 * 512]) for l in range(L)]
    cache_v = [din(f"cache_v{l}", [NPOOL, 128 * 512]) for l in range(L)]
    cache_lf = [din(f"cache_lf{l}", [NPOOL, 128 * H]) for l in range(L)]
    stT = din("stT", [L, 128, 4, NS, 2])
    pt = din("pt", [GP, NGRP], I32)
    c_ident = din("c_ident", [128, 128], BF16)
    c_maskb = din("c_maskb", [128, 128], BF16)
    c_onesb = din("c_onesb", [128, 128], BF16)
    c_onesf = din("c_onesf", [128, 128])
    c_ind = din("c_ind", [NGRP, GP, NS])
    c_indT = din("c_indT", [NGRP, NS, GP])
    c_after = din("c_after", [GP, GP])

    yT = dout("yT", [D, NT])
    kT_out = dout("kT_out", [L, 512, S])
    v_out = dout("v_out", [L, S, 512])
    lf_out = dout("lf_out", [L, H, S])
    convp_out = dout("convp_out", [L, 128, 4, 2])
    ks_out = dout("ks_out", [L, NS, 512])
    vs_out = dout("vs_out", [L, NS, 512])
    lfs_out = dout("lfs_out", [L, NS, H])
    convs_out = dout("convs_out", [L, 128, 4, NS, 2])

    x32_s = [dscr("x1_s", [128, DCH, NT], F32), dscr("x2_s", [128, DCH, NT], F32)]
    x1bf_s = dscr("x1bf_s", [128, DCH, NT], BF16)
    qT_s = dscr("qT_s", [H, KA, NT], BF16)
    kT_s = dscr("kT_s", [H, KA, NT], BF16)
    V_s = dscr("V_s", [H, S, HD + 1], BF16)
    g_s = dscr("g_s", [128, 16, NT], BF16)
    cb_s = dscr("cb_s", [128, 4, NT], BF16)
    a_s = dscr("a_s", [128, 4, NT], BF16)
    s_s = dscr("s_s", [128, FT, NT], BF16)

    R = Rec(nc)
    LO = 16512
    HI = 229344
    A = Arena(nc, LO, HI)
    banks = [nc.alloc_psum_tensor(f"bank{i}", [128, 512], F32) for i in range(8)]
    PS = Ring(banks)

    def psget():
        return PS.get()

    ident = A.alloc("ident", [128, 128], BF16)
    maskb = A.alloc("maskb", [128, 128], BF16)
    onesb = A.alloc("onesb", [128, 128], BF16)
    onesf = A.alloc("onesf", [128, 128], F32)
    lnp = A.alloc("lnp", [128, L, 4, DCH], F32)
    bg = A.alloc("bg", [128, L, 16], F32)
    cw = A.alloc("cw", [128, L, 4, 3], F32)
    bfc = A.alloc("bfc", [H, L], F32)
    bfr = A.alloc("bfr", [NS, L, H], F32)
    qs_t = A.alloc("qs_t", [NS, 512], F32)
    ks_t = A.alloc("ks_t", [NS, 512], F32)
    vs_t = A.alloc("vs_t", [NS, 512], F32)
    lfs_t = A.alloc("lfs_t", [NS, H], F32)
    epsc = A.alloc("epsc", [128, 1], F32)
    R.op("pool", lambda e: e.memset(epsc[:, :], LN_EPS))
    ctoks = []
    ctoks.append(R.dma("sp", ident[:, :], c_ident, "const"))
    ctoks.append(R.dma("sp", maskb[:, :], c_maskb, "const"))
    ctoks.append(R.dma("sp", onesb[:, :], c_onesb, "const"))
    ctoks.append(R.dma("sp", onesf[:, :], c_onesf, "const"))
    for l in range(L):
        for i, t in enumerate((ln1g, ln1b, ln2g, ln2b)):
            ctoks.append(R.dma("sp", lnp[:, l, i, :], t[l], "const"))
        ctoks.append(R.dma("sp", bg[:, l, :], b_gate[l], "const"))
        ctoks.append(R.dma("sp", cw[:, l, :, :], conv_w[l], "const"))
        ctoks.append(R.dma("sp", bfc[:, l:l + 1], b_f[l], "const"))
        ctoks.append(R.dma("sp", bfr[:, l, :], b_fr[l], "const"))
    ctok = ctoks[-1]
    nbf_tok = R.op("dve", lambda e: e.tensor_scalar(out=bfc[:, :], in0=bfc[:, :], scalar1=-1.0, scalar2=None, op0=ALU.mult), waits=[ctok])
    R.barrier()
    PBASE = A.mark()

    def layer_norm_block(l, which, r, w, out_dram32, out_drambf, t0, bufs, rtoks):
        rbf, r2, mean, rstd, tmp, y32, ybf = (bufs[k] for k in ("rbf", "r2", "mean", "rstd", "tmp", "y32", "ybf"))
        gi, bi = (0, 1) if which == 1 else (2, 3)
        t_rbf, t_r2 = [], []
        for dt in range(DCH):
            t_rbf.append(R.op("act", lambda e, dt=dt: e.activation(out=rbf[:, dt, :w], in_=r[:, dt, :w], func=AF.Copy), waits=[rtoks[dt]] + bufs["free_rbf"]))
            t_r2.append(R.op("act", lambda e, dt=dt: e.activation(out=r2[:, dt, :w], in_=r[:, dt, :w], func=AF.Square), waits=[rtoks[dt]] + bufs["free_r2"]))
        k1, S1, fr1 = psget()
        for dt in range(DCH):
            tS1 = R.op("pe", lambda e, dt=dt: e.matmul(S1[:, :w], lhsT=onesb[:, :], rhs=rbf[:, dt, :w], start=(dt == 0), stop=(dt == DCH - 1)),
                       waits=[t_rbf[dt]] + (fr1 if dt == 0 else []), sig=(dt == DCH - 1))
        k2, S2, fr2 = psget()
        for dt in range(DCH):
            tS2 = R.op("pe", lambda e, dt=dt: e.matmul(S2[:, :w], lhsT=onesb[:, :], rhs=r2[:, dt, :w], start=(dt == 0), stop=(dt == DCH - 1)),
                       waits=[t_r2[dt]] + (fr2 if dt == 0 else []), sig=(dt == DCH - 1))
        bufs["free_rbf"] = [tS1]
        bufs["free_r2"] = [tS2]
        tm = R.op("act", lambda e: e.activation(out=mean[:, :w], in_=S1[:, :w], func=AF.Copy, scale=1.0 / D), waits=[tS1] + bufs["free_stat"])
        PS.free[k1] = [tm]
        tq = R.op("dve", lambda e: e.tensor_tensor(out=tmp[:, :w], in0=mean[:, :w], in1=mean[:, :w], op=ALU.mult), waits=[tm] + bufs["free_tmp"])
        tv = R.op("dve", lambda e: e.scalar_tensor_tensor(out=rstd[:, :w], in0=S2[:, :w], scalar=1.0 / D, in1=tmp[:, :w], op0=ALU.mult, op1=ALU.subtract), waits=[tS2, tq] + bufs["free_stat"])
        PS.free[k2] = [tv]
        tsq = R.op("act", lambda e: e.activation(out=rstd[:, :w], in_=rstd[:, :w], func=AF.Sqrt, bias=epsc[:, 0:1]), waits=[tv])
        tr = R.op("dve", lambda e: e.reciprocal(out=rstd[:, :w], in_=rstd[:, :w]), waits=[tsq])
        ty, tb = [], []
        tmpR = bufs["tmpR"]
        t1 = t3 = None
        for dt in range(DCH):
            kt, tm_, frt = tmpR.get()
            t1 = R.op("dve", lambda e: e.tensor_tensor(out=tm_[:, :w], in0=r[:, dt, :w], in1=mean[:, :w], op=ALU.subtract), waits=[tm, rtoks[dt]] + frt)
            t2 = R.op("dve", lambda e: e.tensor_tensor(out=tm_[:, :w], in0=tm_[:, :w], in1=rstd[:, :w], op=ALU.mult), waits=[t1, tr])
            t3 = R.op("act", lambda e: e.activation(out=y32[:, dt, :w], in_=tm_[:, :w], func=AF.Identity,
                                                    scale=lnp[:, l, gi, dt:dt + 1], bias=lnp[:, l, bi, dt:dt + 1]),
                      waits=[t2] + (bufs["free_y32"] if dt == 0 else []))
            tmpR.free[kt] = [t3]
            ty.append(t3)
            if out_drambf is not None:
                tb.append(R.op("pool", lambda e: e.tensor_copy(out=ybf[:, dt, :w], in_=y32[:, dt, :w]), waits=[t3] + (bufs["free_ybf"] if dt == 0 else [])))
        last_tmp = [t1, t3]
        bufs["free_tmp"] = []
        bufs["free_stat"] = [t1]
        d1 = R.dma("sp", out_dram32[:, :, t0:t0 + w], y32[:, :, :w], "ln_y32", waits=[ty[-1]])
        bufs["free_y32"] = [d1]
        outs = [d1]
        if out_drambf is not None:
            d2 = R.dma("sp", out_drambf[:, :, t0:t0 + w], ybf[:, :, :w], "ln_ybf", waits=[tb[-1]])
            bufs["free_ybf"] = [d2]
            outs.append(d2)
        return outs, last_tmp

    def ln_bufs():
        b = {
            "rbf": A.alloc("rbf", [128, DCH, TB], BF16), "r2": A.alloc("r2", [128, DCH, TB], BF16),
            "mean": A.alloc("mean", [128, TB], F32), "rstd": A.alloc("rstd", [128, TB], F32),
            "tmp": A.alloc("tmp", [128, TB], F32), "y32": A.alloc("y32", [128, DCH, TB], F32),
            "tmpR": Ring([A.alloc("tmpr", [128, TB], F32) for _ in range(3)]),
            "ybf": A.alloc("ybf", [128, DCH, TB], BF16),
        }
        for k in ("free_rbf", "free_r2", "free_stat", "free_tmp", "free_y32", "free_ybf"):
            b[k] = []
        return b

    try:
     for l in range(L):
        x_src = xT.rearrange("(c p) t -> p c t", p=128) if l == 0 else x32_s[1]
        last = (l == L - 1)
        y_dst = yT.rearrange("(c p) t -> p c t", p=128) if last else x32_s[1]

        stage_gate()
        A.reset(PBASE)
        xbf = A.alloc("xbf", [128, DCH, NT], BF16)
        GW = 256
        wst = [A.alloc("wst", [128, DCH, GW], F32) for _ in range(2)]
        wstR = Ring(wst)
        wbf = [A.alloc("wbf", [128, DCH, GW], BF16) for _ in range(6)]
        wbfR = Ring(wbf)
        obf = [A.alloc("obf", [128, TB], BF16) for _ in range(4)]
        obfR = Ring(obf)
        o32 = [A.alloc("o32", [128, TB], F32) for _ in range(3)]
        o32R = Ring(o32)
        hbuf = A.alloc("hbuf", [128, NT], F32)
        ubuf = A.alloc("ubuf", [128, NT + 2], F32)
        zf = A.alloc("zf", [H, NT], F32)
        Vst = [A.alloc("Vst", [128, H, HD + 1], BF16) for _ in range(2)]
        VstR = Ring(Vst)

        xtok = {}
        for (t0, w) in blocks:
            for hb in range(0, w, GW):
                ww = min(GW, w - hb)
                k, st, fr = wstR.get()
                d = R.dma("sp", st[:, :, :ww], x_src[:, :, t0 + hb:t0 + hb + ww], f"wst{k}", waits=fr)
                c = R.op("dve", lambda e, st=st, ww=ww, a=t0 + hb: e.tensor_copy(out=xbf[:, :, a:a + ww], in_=st[:, :, :ww]), waits=[d])
                wstR.free[k] = [c]
                xtok[(t0, hb)] = c
        xall = c

        def load_w(src_ap, ncols):
            k, st, fr = wstR.get()
            d = R.dma("sp", st[:, :, :ncols], src_ap.rearrange("(c p) n -> p c n", p=128), f"wst{k}", waits=fr)
            kb, wb, frb = wbfR.get()
            c = R.op("dve", lambda e: e.tensor_copy(out=wb[:, :, :ncols], in_=st[:, :, :ncols]), waits=[d] + frb)
            wstR.free[k] = [c]
            return wb, kb, c

        def fm_tile(wb, wtok, j, t0, w):
            k, bk, fr = psget()
            for c in range(DCH):
                tk = R.op("pe", lambda e, c=c: e.matmul(bk[:, :w], lhsT=wb[:, c, j * 128:(j + 1) * 128], rhs=xbf[:, c, t0:t0 + w],
                                                        start=(c == 0), stop=(c == DCH - 1)),
                          waits=([wtok, xall] + fr) if c == 0 else [], sig=(c == DCH - 1))
            return k, bk, tk

        stage_gate()
        for which, off, dst in (("q", OFF_Q, qT_s), ("k", OFF_K, kT_s)):
            for g in range(2):
                wb, kb, wtok = load_w(w_in[l, :, off + g * GW: off + (g + 1) * GW], GW)
                pe_last = None
                for j in range(2):
                    h0 = (g * 2 + j) * 2
                    for (t0, w) in blocks:
                        k, bk, tk = fm_tile(wb, wtok, j, t0, w)
                        ko, ob, fro = obfR.get()
                        if which == "k" and t0 < S:
                            k3, o3, fr3 = o32R.get()
                            ev0 = R.op("act", lambda e: e.activation(out=o3[:, :w], in_=bk[:, :w], func=AF.Copy), waits=[tk] + fr3)
                            PS.free[k] = [ev0]
                            ev = R.op("pool", lambda e: e.tensor_copy(out=ob[:, :w], in_=o3[:, :w]), waits=[ev0] + fro)
                            c0 = (g * 2 + j) * 128
                            d3 = R.dma("sp", kT_out[l, c0:c0 + 128, t0:t0 + w], o3[:, :w], f"o32{k3}", waits=[ev0])
                            o32R.free[k3] = [d3, ev]
                        else:
                            ev = R.op("act", lambda e: e.activation(out=ob[:, :w], in_=bk[:, :w], func=AF.Copy), waits=[tk] + fro)
                            PS.free[k] = [ev]
                        d1 = R.dma("sp", dst[h0, 0:64, t0:t0 + w], ob[0:64, :w], f"obf{ko}", waits=[ev])
                        d2 = R.dma("sp", dst[h0 + 1, 0:64, t0:t0 + w], ob[64:128, :w], f"obf{ko}", waits=[ev])
                        obfR.free[ko] = [d2]
                        pe_last = tk
                if os.environ.get("SKIP_TM"):
                    wbfR.free[kb] = [R.last["pe"]]
                    continue
                k, bk, fr = psget()
                for c in range(DCH):
                    tk = R.op("pe", lambda e, c=c: e.matmul(bk[0:NS, :GW], lhsT=xbf[:, c, S:S + NS], rhs=wb[:, c, :GW], start=(c == 0), stop=(c == DCH - 1)),
                              waits=([wtok, xall] + fr) if c == 0 else [], sig=(c == DCH - 1))
                tgt = qs_t if which == "q" else ks_t
                ev = R.op("act", lambda e, tgt=tgt, bk=bk, g=g: e.activation(out=tgt[:, g * GW:(g + 1) * GW], in_=bk[0:NS, :GW], func=AF.Copy), waits=[tk])
                PS.free[k] = [ev]
                wbfR.free[kb] = [tk]
                if which == "k":
                    R.dma("sp", ks_out[l, :, g * GW:(g + 1) * GW], ks_t[:, g * GW:(g + 1) * GW], "small_out", waits=[ev])

        stage_gate()
        wv = [load_w(w_in[l, :, OFF_V + g * GW: OFF_V + (g + 1) * GW], GW) for g in range(2)]
        for tt in range(NST + 1):
            if tt < NST:
                a0, m = tt * 128, 128
            else:
                a0, m = S, NS
            k, bk, fr = psget()
            for g in range(2):
                wb, kb, wtok = wv[g]
                for c in range(DCH):
                    tk = R.op("pe", lambda e, c=c, g=g, wb=wb: e.matmul(bk[0:m, g * GW:(g + 1) * GW], lhsT=xbf[:, c, a0:a0 + m], rhs=wb[:, c, :GW],
                                                                       start=(c == 0), stop=(c == DCH - 1)),
                              waits=([wtok, xall] + fr) if c == 0 else [], sig=(c == DCH - 1 and g == 1))
            if tt < NST:
                k3, o3, fr3 = o32R.get()
                ev = R.op("act", lambda e, o3=o3, bk=bk: e.activation(out=o3[:, :], in_=bk[:, :], func=AF.Copy), waits=[tk] + fr3)
                d3 = R.dma("sp", v_out[l, a0:a0 + 128, :], o3[:, :], f"o32{k3}", waits=[ev])
                kv, vs, frv = VstR.get()
                ev2 = R.op("pool", lambda e: e.tensor_copy(out=vs[:, :, 0:HD], in_=o3[:, :].rearrange("p (h d) -> p h d", h=H)), waits=[ev] + frv)
                ev3 = R.op("pool", lambda e, vs=vs: e.memset(vs[:, :, HD:HD + 1], 1.0), waits=frv)
                d4 = R.dma("sp", V_s[:, a0:a0 + 128, :].rearrange("h s d -> s h d"), vs[:, :, :], f"Vst{kv}", waits=[ev2, ev3])
                VstR.free[kv] = [d4]
                o32R.free[k3] = [d3, ev2]
                PS.free[k] = [ev]
            else:
                ev = R.op("act", lambda e, bk=bk: e.activation(out=vs_t[:, :], in_=bk[0:NS, :], func=AF.Copy), waits=[tk])
                R.dma("sp", vs_out[l], vs_t[:, :], "small_out", waits=[ev])
                PS.free[k] = [ev]
        for g in range(2):
            wbfR.free[wv[g][1]] = [tk]

        stage_gate()
        wb, kb, wtok = load_w(w_in[l, :, OFF_F:OFF_F + H], H)
        zf_tok = None
        for (t0, w) in blocks:
            k, bk, fr = psget()
            for c in range(DCH):
                tk = R.op("pe", lambda e, c=c: e.matmul(bk[0:H, :w], lhsT=wb[:, c, 0:H], rhs=xbf[:, c, t0:t0 + w], start=(c == 0), stop=(c == DCH - 1)),
                          waits=([wtok, xall] + fr) if c == 0 else [], sig=(c == DCH - 1))
            zf_tok = R.op("act", lambda e, bk=bk, t0=t0, w=w: e.activation(out=zf[:, t0:t0 + w], in_=bk[0:H, :w], func=AF.Exp, scale=-1.0, bias=bfc[:, l:l + 1]),
                          waits=[tk, nbf_tok])
            PS.free[k] = [zf_tok]
        k, bk, fr = psget()
        for c in range(DCH):
            tk = R.op("pe", lambda e, c=c: e.matmul(bk[0:NS, 0:H], lhsT=xbf[:, c, S:S + NS], rhs=wb[:, c, 0:H], start=(c == 0), stop=(c == DCH - 1)),
                      waits=([wtok, xall] + fr) if c == 0 else [], sig=(c == DCH - 1))
        wbfR.free[kb] = [tk]
        t1 = R.op("dve", lambda e: e.tensor_tensor(out=lfs_t[:, :], in0=bk[0:NS, 0:H], in1=bfr[:, l, :], op=ALU.add), waits=[tk])
        PS.free[k] = [t1]
        t2 = R.op("act", lambda e: e.activation(out=lfs_t[:, :], in_=lfs_t[:, :], func=AF.Exp, scale=-1.0), waits=[t1])
        t3 = R.op("act", lambda e: e.activation(out=lfs_t[:, :], in_=lfs_t[:, :], func=AF.Ln, bias=1.0), waits=[t2])
        lfs_tok = R.op("dve", lambda e: e.tensor_scalar(out=lfs_t[:, :], in0=lfs_t[:, :], scalar1=-1.0, scalar2=None, op0=ALU.mult), waits=[t3])
        R.dma("sp", lfs_out[l], lfs_t[:, :], "small_out", waits=[lfs_tok])
        t4 = R.op("act", lambda e: e.activation(out=zf[:, :], in_=zf[:, :], func=AF.Ln, bias=1.0), waits=[zf_tok])
        lf_tok = R.op("dve", lambda e: e.tensor_scalar(out=zf[:, :], in0=zf[:, :], scalar1=-1.0, scalar2=None, op0=ALU.mult), waits=[t4])
        dlf = R.dma("sp", lf_out[l], zf[:, 0:S], "small_out", waits=[lf_tok])
        cs = hbuf
        zero8 = ubuf
        tz = R.op("pool", lambda e: e.memset(zero8[0:H, 0:S], 0.0))
        tc = R.op("dve", lambda e: e.tensor_tensor_scan(out=cs[0:H, 0:S], data0=zf[:, 0:S], data1=zero8[0:H, 0:S], initial=0.0, op0=ALU.add, op1=ALU.add), waits=[lf_tok, tz])
        tc = R.op("dve", lambda e: e.tensor_scalar(out=cs[0:H, 0:S], in0=cs[0:H, 0:S], scalar1=8.0, scalar2=None, op0=ALU.mult), waits=[tc])
        er1 = A.alloc("er1", [H, S], BF16)
        nr1 = A.alloc("nr1", [H, S], BF16)
        on1 = A.alloc("on1", [H, S], BF16)
        to = R.op("pool", lambda e: e.memset(on1[:, :], 1.0))
        res = zero8
        cur = cs
        tprev = tc
        erfree, nrfree = [], []
        aug = []
        for i in range(3):
            ta = R.op("dve", lambda e, cur=cur: e.tensor_copy(out=er1[:, :], in_=cur[0:H, 0:S]), waits=[tprev] + erfree)
            tn = R.op("dve", lambda e: e.tensor_scalar(out=nr1[:, :], in0=er1[:, :], scalar1=-1.0, scalar2=None, op0=ALU.mult), waits=[ta] + nrfree)
            if i < 2:
                tprev = R.op("dve", lambda e, cur=cur: e.tensor_tensor(out=res[0:H, 0:S], in0=cur[0:H, 0:S], in1=er1[:, :], op=ALU.subtract), waits=[tn])
                cur = res
            d_e = R.dma("sp", qT_s[:, 64 + i, 0:S], er1[:, :], "aug_e", waits=[ta])
            d_n = R.dma("sp", kT_s[:, 67 + i, 0:S], nr1[:, :], "aug_n", waits=[tn])
            erfree, nrfree = [d_e], [d_n]
            aug.append(R.dma("sp", qT_s[:, 67 + i, 0:S], on1[:, :], "aug_o", waits=[to]))
            aug.append(R.dma("sp", kT_s[:, 64 + i, 0:S], on1[:, :], "aug_o", waits=[to]))
        aug += [d_e, d_n]
        lastdve_split = R.last["dve"]
        conv_start_wait = aug + [dlf, lastdve_split]

        stage_gate()
        for gp in range(2):
            wh = load_w(w_in[l, :, OFF_H + gp * GW: OFF_H + (gp + 1) * GW], GW)
            wc = load_w(w_in[l, :, OFF_C + gp * GW: OFF_C + (gp + 1) * GW], GW)
            wg = load_w(w_in[l, :, OFF_B + gp * GW: OFF_B + (gp + 1) * GW], GW)
            for j in range(2):
                ct = gp * 2 + j
                th = None
                for (t0, w) in blocks:
                    k, bk, tk = fm_tile(wh[0], wh[2], j, t0, w)
                    th = R.op("act", lambda e, bk=bk, t0=t0, w=w: e.activation(out=hbuf[:, t0:t0 + w], in_=bk[:, :w], func=AF.Copy), waits=[tk] + conv_start_wait)
                    PS.free[k] = [th]
                conv_start_wait = []
                tz = R.op("pool", lambda e: e.memset(ubuf[:, 0:2], 0.0), waits=[th])
                tu = None
                for (t0, w) in blocks:
                    k, bk, tk = fm_tile(wc[0], wc[2], j, t0, w)
                    tu = R.op("dve", lambda e, bk=bk, t0=t0, w=w: e.tensor_tensor(out=ubuf[:, 2 + t0:2 + t0 + w], in0=bk[:, :w], in1=hbuf[:, t0:t0 + w], op=ALU.mult), waits=[tk, th])
                    PS.free[k] = [tu]
                dcp = R.dma("sp", convp_out[l, :, ct, :], ubuf[:, S:S + 2], "convp", waits=[tu])
                c1 = R.op("dve", lambda e, ct=ct: e.tensor_scalar(out=hbuf[:, 0:S], in0=ubuf[:, 0:S], scalar1=cw[:, l, ct, 0:1], scalar2=None, op0=ALU.mult), waits=[tu, tz])
                c2 = R.op("dve", lambda e, ct=ct: e.scalar_tensor_tensor(out=hbuf[:, 0:S], in0=ubuf[:, 1:S + 1], scalar=cw[:, l, ct, 1:2], in1=hbuf[:, 0:S], op0=ALU.mult, op1=ALU.add), waits=[c1])
                c3 = R.op("dve", lambda e, ct=ct: e.scalar_tensor_tensor(out=hbuf[:, 0:S], in0=ubuf[:, 2:S + 2], scalar=cw[:, l, ct, 2:3], in1=hbuf[:, 0:S], op0=ALU.mult, op1=ALU.add), waits=[c2])
                sst = A.alloc("sst", [128, NS, 2], F32)
                nst = A.alloc("nst", [128, NS, 2], F32)
                dS = R.dma("sp", sst[:, :, :], stT[l, :, ct, :, :], "sst")
                s1 = R.op("dve", lambda e, ct=ct: e.tensor_scalar(out=hbuf[:, S:S + NS], in0=sst[:, :, 0], scalar1=cw[:, l, ct, 0:1], scalar2=None, op0=ALU.mult), waits=[dS, tu])
                s2 = R.op("dve", lambda e, ct=ct: e.scalar_tensor_tensor(out=hbuf[:, S:S + NS], in0=sst[:, :, 1], scalar=cw[:, l, ct, 1:2], in1=hbuf[:, S:S + NS], op0=ALU.mult, op1=ALU.add), waits=[s1])
                s3 = R.op("dve", lambda e, ct=ct: e.scalar_tensor_tensor(out=hbuf[:, S:S + NS], in0=ubuf[:, 2 + S:2 + S + NS], scalar=cw[:, l, ct, 2:3], in1=hbuf[:, S:S + NS], op0=ALU.mult, op1=ALU.add), waits=[s2])
                n1 = R.op("pool", lambda e: e.tensor_copy(out=nst[:, :, 0], in_=sst[:, :, 1]), waits=[dS])
                n2 = R.op("pool", lambda e: e.tensor_copy(out=nst[:, :, 1], in_=ubuf[:, 2 + S:2 + S + NS]), waits=[tu])
                R.dma("sp", convs_out[l, :, ct, :, :], nst[:, :, :], "small_out", waits=[n1, n2])
                tcb = None
                for (t0, w) in blocks:
                    k, bk, tk = fm_tile(wg[0], wg[2], j, t0, w)
                    ko, ob, fro = obfR.get()
                    tcb = R.op("dve", lambda e, bk=bk, ob=ob, t0=t0, w=w: e.tensor_tensor(out=ob[:, :w], in0=bk[:, :w], in1=hbuf[:, t0:t0 + w], op=ALU.mult), waits=[tk, c3, s3] + fro)
                    PS.free[k] = [tcb]
                    d1 = R.dma("sp", cb_s[:, ct, t0:t0 + w], ob[:, :w], f"obf{ko}", waits=[tcb])
                    obfR.free[ko] = [d1]
                conv_start_wait = [tcb, n2, dcp]
            lastpe = R.last["pe"]
            for ww_ in (wh, wc, wg):
                wbfR.free[ww_[1]] = [lastpe]

        stage_gate()
        for g in range(8):
            wb, kb, wtok = load_w(w_in[l, :, OFF_G + g * GW: OFF_G + (g + 1) * GW], GW)
            for j in range(2):
                gt = g * 2 + j
                for (t0, w) in blocks:
                    k, bk, tk = fm_tile(wb, wtok, j, t0, w)
                    ko, ob, fro = obfR.get()
                    ev = R.op("act", lambda e, bk=bk, ob=ob, w=w, gt=gt: e.activation(out=ob[:, :w], in_=bk[:, :w], func=AF.Sigmoid, bias=bg[:, l, gt:gt + 1]), waits=[tk] + fro)
                    PS.free[k] = [ev]
                    d1 = R.dma("sp", g_s[:, gt, t0:t0 + w], ob[:, :w], f"obf{ko}", waits=[ev])
                    obfR.free[ko] = [d1]
            wbfR.free[kb] = [R.last["pe"]]
        R.barrier()

        stage_gate()
        A.reset(PBASE)
        kbuf = [A.alloc("kbuf", [KA, S], BF16) for _ in range(2)]
        qbuf = [A.alloc("qbuf", [KA, S], BF16) for _ in range(2)]
        vbuf = [A.alloc("vbuf", [128, NST, HD + 1], BF16) for _ in range(2)]
        pT = [A.alloc("pT", [128, TB], BF16) for _ in range(4)]
        pTR = Ring(pT)
        rl = A.alloc("rl", [65, TB], F32)
        rb = A.alloc("rb", [64, TB], F32)
        ast = [A.alloc("ast", [64, TB], BF16) for _ in range(2)]
        astR = Ring(ast)
        PO = Ring(banks[0:2])
        PSb = Ring(banks[2:5])
        PB = Ring(banks[5:6])
        OS = banks[6]
        MISC = banks[7]

        idx = A.alloc("idx", [GP, NGRP], I32)
        ind = A.alloc("ind", [GP, NGRP, NS], F32)
        indb = A.alloc("indb", [GP, NGRP, NS], BF16)
        indT = A.alloc("indT", [NS, NGRP, GP], F32)
        aft = A.alloc("aft", [GP, GP], F32)
        lfp = A.alloc("lfp", [GP, 128, H], F32)
        pfx = A.alloc("pfx", [GP, 128, H], F32)
        bias = A.alloc("bias", [GP, 128, H], F32)
        tot = A.alloc("tot", [GP, H], F32)
        qpp = A.alloc("qpp", [GP, 512], F32)
        Kc = [A.alloc("Kc", [GP, TK, 512], F32) for _ in range(2)]
        Vc = [A.alloc("Vc", [GP, TK, 512], F32) for _ in range(2)]
        prod = A.alloc("prod", [GP, TK, 512], F32)
        pv = A.alloc("pv", [GP, TK, 512], BF16)
        sc = A.alloc("sc", [GP, TK, H], F32)
        pe_ = A.alloc("pe_", [GP, 128, H], F32)
        rs = A.alloc("rs", [GP, H], F32)
        sm = A.alloc("sm", [NS, 8, 512], F32)
        zt = A.alloc("zt", [GP, 128], F32)
        asb = A.alloc("asb", [NS, 512], BF16)
        qs_d = dscr(f"qs_d{l}", [NS, 512], F32)
        as_d = dscr(f"as_d{l}", [NS, 512], BF16)

        def sample_stage():
            R.dma("sp", qs_d, qs_t[:, :], "smp")
            R.dma("sp", idx[:, :], pt, "smp")
            R.dma("sp", ind[:, :, :], c_ind.rearrange("g p b -> p g b"), "smp")
            R.dma("sp", indT[:, :, :], c_indT.rearrange("g b p -> b g p"), "smp")
            dcst = R.dma("sp", aft[:, :], c_after, "smp")
            tzt = R.op("pool", lambda e: e.memset(zt[:, :], 0.0))
            tib = R.op("pool", lambda e: e.tensor_copy(out=indb[:, :, :], in_=ind[:, :, :]), waits=[dcst])
            LSg = [MISC[0:NS, g_ * H:(g_ + 1) * H] for g_ in range(NGRP)]
            Bq = MISC[0:GP, 64:64 + H]
            bq_free = []
            first_mm = True
            r2_ = None
            grp_done = None
            prev_tmm = None
            tmm = None
            BG = GP // NPG
            for g in range(NGRP):
                dqq = None
                for bl in range(BG):
                    b = g * BG + bl
                    dqq = R.dma("sp", qpp[bl * NPG:(bl + 1) * NPG, :], qs_d[b:b + 1, :].partition_broadcast(NPG), "smp2", waits=[dcst, grp_done])
                glf = R.gather(lfp[:, :, :].rearrange("p t h -> p (t h)"), cache_lf[l], idx[:, g:g + 1], "glf", waits=[dcst, grp_done])
                tp = None
                for hh in range(H):
                    tp = R.op("dve", lambda e: e.tensor_tensor_scan(out=pfx[:, :, hh], data0=lfp[:, :, hh], data1=zt[:, :],
                                                                   initial=0.0, op0=ALU.add, op1=ALU.add), waits=[glf, tzt, grp_done])
                tt_ = R.op("dve", lambda e: e.tensor_copy(out=tot[:, :], in_=pfx[:, 127, :]), waits=[tp])
                R.op("pe", lambda e: e.matmul(Bq, lhsT=aft[:, :], rhs=tot[:, :], start=True, stop=False), waits=[tt_, dcst] + bq_free, sig=False)
                tb_ = R.op("pe", lambda e: e.matmul(Bq, lhsT=indT[:, g, :], rhs=lfs_t[:, :], start=False, stop=True), waits=[lfs_tok])
                tt2 = R.op("dve", lambda e: e.tensor_tensor(out=tot[:, :], in0=tot[:, :], in1=Bq, op=ALU.add), waits=[tb_])
                bq_free = [tt2]
                tbias = R.op("dve", lambda e: e.tensor_tensor(out=bias[:, :, :], in0=tot[:, :].unsqueeze(1).to_broadcast([GP, 128, H]), in1=pfx[:, :, :], op=ALU.subtract), waits=[tt2])
                kfree = [[], []]
                vfree = [[], []]
                prev_m4 = None
                prev_m2 = None
                yield
                for ch in range(NCHUNK):
                    sl = ch % 2
                    a0 = ch * TK * 512
                    gk = R.gather(Kc[sl][:, :, :].rearrange("p t e -> p (t e)"), cache_k[l], idx[:, g:g + 1], f"gk{sl}", waits=kfree[sl], eoff=a0)
                    gv = R.gather(Vc[sl][:, :, :].rearrange("p t e -> p (t e)"), cache_v[l], idx[:, g:g + 1], f"gv{sl}", waits=vfree[sl], eoff=a0)
                    m1 = R.op("pool", lambda e: e.tensor_tensor(out=prod[:, :, :], in0=Kc[sl][:, :, :], in1=qpp[:, :].unsqueeze(1).to_broadcast([GP, TK, 512]), op=ALU.mult), waits=[gk, dqq, prev_m2])
                    kfree[sl] = [m1]
                    m2 = R.op("dve", lambda e: e.tensor_reduce(out=sc[:, :, :].rearrange("p t h -> p (t h)"), in_=prod[:, :, :].rearrange("p t (h d) -> p (t h) d", h=H), axis=AX.X, op=ALU.add), waits=[m1, prev_m4])
                    prev_m2 = m2
                    m3 = R.op("dve", lambda e: e.scalar_tensor_tensor(out=sc[:, :, :], in0=sc[:, :, :], scalar=0.125, in1=bias[:, ch * TK:(ch + 1) * TK, :], op0=ALU.mult, op1=ALU.add), waits=[m2, tbias])
                    m4 = R.op("act", lambda e: e.activation(out=pe_[:, ch * TK:(ch + 1) * TK, :], in_=sc[:, :, :], func=AF.Exp), waits=[m3])
                    prev_m4 = m4
                    m5 = R.op("dve", lambda e: e.tensor_tensor(out=pv[:, :, :].rearrange("p t (h d) -> p t h d", h=H), in0=Vc[sl][:, :, :].rearrange("p t (h d) -> p t h d", h=H),
                                                               in1=pe_[:, ch * TK:(ch + 1) * TK, :].unsqueeze(3).to_broadcast([GP, TK, H, HD]), op=ALU.mult), waits=[m4, gv, prev_tmm])
                    vfree[sl] = [m5]
                    for tk_ in range(TK):
                        lastmm = (g == NGRP - 1 and ch == NCHUNK - 1 and tk_ == TK - 1)
                        tmm = R.op("pe", lambda e: e.matmul(OS[0:NS, :], lhsT=indb[:, g, :], rhs=pv[:, tk_, :], start=first_mm, stop=lastmm),
                                   waits=([m5, tib]) if tk_ == 0 else [], sig=(tk_ == TK - 1))
                        first_mm = False
                    prev_tmm = tmm
                    yield
                r1 = R.op("dve", lambda e: e.tensor_reduce(out=rs[:, :], in_=pe_[:, :, :].rearrange("p t h -> p h t"), axis=AX.X, op=ALU.add), waits=[prev_m4, r2_])
                r2_ = R.op("pe", lambda e: e.matmul(LSg[g], lhsT=ind[:, g, :], rhs=rs[:, :], start=True, stop=True), waits=[r1])
                grp_done = r1
            w_q, w_p, w_o_, w_e, w_l = sm[:, 0, :], sm[:, 1, :], sm[:, 2, :], sm[:, 3, 0:H], sm[:, 4, 0:H]
            n1 = R.op("dve", lambda e: e.tensor_tensor(out=w_q, in0=qs_t[:, :], in1=ks_t[:, :], op=ALU.mult))
            n2 = R.op("dve", lambda e: e.tensor_reduce(out=w_e, in_=w_q.rearrange("p (h d) -> p h d", h=H), axis=AX.X, op=ALU.add), waits=[n1])
            n3 = R.op("act", lambda e: e.activation(out=w_e, in_=w_e, func=AF.Exp, scale=0.125), waits=[n2])
            n4 = R.op("dve", lambda e: e.tensor_tensor(out=w_l, in0=w_e, in1=LSg[0], op=ALU.add), waits=[n3, r2_])
            for g_ in range(1, NGRP):
                n4 = R.op("dve", lambda e: e.tensor_tensor(out=w_l, in0=w_l, in1=LSg[g_], op=ALU.add), waits=[n4])
            n5 = R.op("dve", lambda e: e.reciprocal(out=w_l, in_=w_l), waits=[n4])
            n6 = R.op("dve", lambda e: e.tensor_tensor(out=w_p.rearrange("p (h d) -> p h d", h=H), in0=vs_t[:, :].rearrange("p (h d) -> p h d", h=H),
                                                       in1=w_e.unsqueeze(2).to_broadcast([NS, H, HD]), op=ALU.mult), waits=[n3])
            n7 = R.op("dve", lambda e: e.tensor_tensor(out=w_o_, in0=w_p, in1=OS[0:NS, :], op=ALU.add), waits=[n6, tmm])
            n8 = R.op("dve", lambda e: e.tensor_tensor(out=asb[:, :].rearrange("p (h d) -> p h d", h=H), in0=w_o_.rearrange("p (h d) -> p h d", h=H),
                                                       in1=w_l.unsqueeze(2).to_broadcast([NS, H, HD]), op=ALU.mult), waits=[n7, n5])
            d1 = R.dma("sp", as_d, asb[:, :], "smp3", waits=[n8])
            for b in range(NS):
                R.dma("sp", a_s[:, :, S + b:S + b + 1], as_d[b:b + 1, :].rearrange("b (c p) -> p c b", p=128), "smp3", waits=[d1], allow_slow_non_contiguous=True)
            yield

        sgen = sample_stage()

        hfree = [[], []]
        htoks = {}

        def load_head(h):
            sl = h % 2
            fr = hfree[sl]
            R.dma("sp", kbuf[sl][:, :], kT_s[h, :, 0:S], f"hd{sl}", waits=fr)
            R.dma("sp", qbuf[sl][:, :], qT_s[h, :, 0:S], f"hd{sl}", waits=fr)
            htoks[h] = R.dma("sp", vbuf[sl][:, :, :], V_s[h].rearrange("(t p) d -> p t d", p=128), f"hd{sl}", waits=fr)

        tiles = []
        for h in range(H):
            for tb in range(NB):
                n_s = (tb * TB + TB) // 128
                for si in range(n_s):
                    tiles.append((h, tb, si, n_s))
        qk_state = {}
        blk_state = {}
        norm_state = {"rl_free": [], "rb_free": []}

        def emit_qk(i):
            h, tb, si, n_s = tiles[i]
            sl = h % 2
            kb_, qb_ = kbuf[sl], qbuf[sl]
            t0 = tb * TB
            j = si - t0 // 128
            diag = j >= 0
            c0 = 128 * j if diag else 0
            ks_, Sb, frs = PSb.get()
            w0 = [htoks[h]] + frs
            if diag:
                R.op("pe", lambda e: e.matmul(Sb[:, c0:c0 + 128], lhsT=kb_[:, si * 128:(si + 1) * 128], rhs=qb_[:, t0 + c0:t0 + c0 + 128], start=True, stop=False), waits=w0, sig=False)
                tq = R.op("pe", lambda e: e.matmul(Sb[:, c0:c0 + 128], lhsT=ident[:, :], rhs=maskb[:, :], start=False, stop=True))
                if c0 + 128 < TB:
                    tq = R.op("pe", lambda e: e.matmul(Sb[:, c0 + 128:TB], lhsT=kb_[:, si * 128:(si + 1) * 128], rhs=qb_[:, t0 + c0 + 128:t0 + TB], start=True, stop=True))
            else:
                tq = R.op("pe", lambda e: e.matmul(Sb[:, :], lhsT=kb_[:, si * 128:(si + 1) * 128], rhs=qb_[:, t0:t0 + TB], start=True, stop=True), waits=w0)
            kp, P_, frp = pTR.get()
            te = R.op("act", lambda e: e.activation(out=P_[:, c0:TB], in_=Sb[:, c0:TB], func=AF.Exp, scale=0.125), waits=[tq] + frp)
            PSb.free[ks_] = [te]
            qk_state[i] = (te, P_, kp, c0)

        def emit_pv(i):
            h, tb, si, n_s = tiles[i]
            sl = h % 2
            vb_ = vbuf[sl]
            t0 = tb * TB
            te, P_, kp, c0 = qk_state.pop(i)
            if si == 0:
                ko_, O, fro = PO.get()
                blk_state[(h, tb)] = (ko_, O)
            else:
                ko_, O = blk_state[(h, tb)]
                fro = []
            pv_tok = R.op("pe", lambda e: e.matmul(O[0:HD + 1, c0:TB], lhsT=vb_[:, si, :], rhs=P_[:, c0:TB], start=(si == 0), stop=(si == n_s - 1)),
                          waits=[te] + fro)
            pTR.free[kp] = [pv_tok]
            if si == n_s - 1:
                t1 = R.op("dve", lambda e: e.reciprocal(out=rl[64:65, :], in_=O[64:65, :]), waits=[pv_tok] + norm_state["rl_free"])
                kb2, Bc, frb = PB.get()
                t2 = R.op("pe", lambda e: e.matmul(Bc[0:64, :], lhsT=onesf[64:65, 0:64], rhs=rl[64:65, :], start=True, stop=True), waits=[t1] + frb)
                norm_state["rl_free"] = [t2]
                t3 = R.op("act", lambda e: e.activation(out=rb[:, :], in_=Bc[0:64, :], func=AF.Copy), waits=[t2] + norm_state["rb_free"])
                PB.free[kb2] = [t3]
                ka, as_, fra = astR.get()
                t4 = R.op("dve", lambda e: e.tensor_tensor(out=as_[:, :], in0=O[0:64, :], in1=rb[:, :], op=ALU.mult), waits=[t3] + fra)
                norm_state["rb_free"] = [t4]
                PO.free[ko_] = [t4]
                d = R.dma("sp", a_s[(h % 2) * 64:(h % 2) * 64 + 64, h // 2, t0:t0 + TB], as_[:, :], f"ast{ka}", waits=[t4])
                astR.free[ka] = [d]
                del blk_state[(h, tb)]
                if tb == NB - 1:
                    hfree[sl] = [pv_tok]
                    if h + 2 < H:
                        load_head(h + 2)

        load_head(0)
        if H > 1:
            load_head(1)
        LA = 2
        NTL = len(tiles)
        for i in range(min(LA, NTL)):
            emit_qk(i)
        for i in range(NTL):
            if i + LA < NTL:
                emit_qk(i + LA)
            emit_pv(i)
            if i % 20 == 10:
                next(sgen, None)
        for _ in sgen:
            pass
        R.barrier()

        stage_gate()
        A.reset(PBASE)
        wpa = A.alloc("wpa", [128, 4, D], BF16)
        wpc = A.alloc("wpc", [128, 4, D], BF16)
        wo = A.alloc("wo", [128, DCH, D], BF16)
        wst3 = [A.alloc("wst3", [128, DCH, D // 4], F32) for _ in range(2)]
        wst3R = Ring(wst3)
        for (dst_, src_, nch) in ((wpa, w_pa[l], 4), (wpc, w_pc[l], 4), (wo, w_o[l], DCH)):
            for hf in range(4):
                k3_, st3, fr3_ = wst3R.get()
                d = R.dma("sp", st3[:, 0:nch, :], src_.rearrange("(c p) n -> p c n", p=128)[:, :, hf * 256:(hf + 1) * 256], f"wst3{k3_}", waits=fr3_)
                c = R.op("dve", lambda e: e.tensor_copy(out=dst_[:, :, hf * 256:(hf + 1) * 256], in_=st3[:, 0:nch, :]), waits=[d])
                wst3R.free[k3_] = [c]
        w3tok = c
        MB = A.mark()
        ab = A.alloc("ab", [128, 4, TB], BF16)
        cbb = A.alloc("cbb", [128, 4, TB], BF16)
        gb_ = A.alloc("gb_", [128, 16, TB], BF16)
        xb = A.alloc("xb", [128, DCH, TB], F32)
        mb = A.alloc("mb", [128, DCH, TB], BF16)
        t1R = Ring([A.alloc("t1b", [128, TB], F32) for _ in range(2)])
        t2R = Ring([A.alloc("t2b", [128, TB], F32) for _ in range(2)])
        rbuf = A.alloc("rbuf", [128, DCH, TB], F32)
        lb = ln_bufs()
        in_free = []
        r_free = []
        for (t0, w) in blocks:
            dl = [R.dma("sp", ab[:, :, :w], a_s[:, :, t0:t0 + w], "s3in", waits=in_free),
                  R.dma("sp", cbb[:, :, :w], cb_s[:, :, t0:t0 + w], "s3in", waits=in_free),
                  R.dma("sp", gb_[:, :, :w], g_s[:, :, t0:t0 + w], "s3in", waits=in_free),
                  R.dma("sp", xb[:, :, :w], x_src[:, :, t0:t0 + w], "s3in", waits=in_free)]
            dld = dl[-1]
            mt = []
            for dt in range(DCH):
                ka_, Ab, fra = psget()
                for e_ in range(4):
                    ta = R.op("pe", lambda e, e_=e_, dt=dt, Ab=Ab: e.matmul(Ab[:, :w], lhsT=wpa[:, e_, dt * 128:(dt + 1) * 128], rhs=ab[:, e_, :w], start=(e_ == 0), stop=(e_ == 3)),
                              waits=([dld, w3tok] + fra) if e_ == 0 else [], sig=(e_ == 3))
                kc_, Cb, frc = psget()
                for e_ in range(4):
                    tc_ = R.op("pe", lambda e, e_=e_, dt=dt, Cb=Cb: e.matmul(Cb[:, :w], lhsT=wpc[:, e_, dt * 128:(dt + 1) * 128], rhs=cbb[:, e_, :w], start=(e_ == 0), stop=(e_ == 3)),
                               waits=frc if e_ == 0 else [], sig=(e_ == 3))
                k1_, t1b, f1_ = t1R.get()
                k2_, t2b, f2_ = t2R.get()
                u1 = R.op("dve", lambda e: e.tensor_tensor(out=t1b[:, :w], in0=Ab[:, :w], in1=gb_[:, dt, :w], op=ALU.mult), waits=[ta] + f1_)
                u2 = R.op("dve", lambda e: e.tensor_tensor(out=t2b[:, :w], in0=Cb[:, :w], in1=gb_[:, 8 + dt, :w], op=ALU.mult), waits=[tc_] + f2_)
                PS.free[ka_] = [u1]
                PS.free[kc_] = [u2]
                u3 = R.op("pool", lambda e: e.tensor_tensor(out=mb[:, dt, :w], in0=t1b[:, :w], in1=t2b[:, :w], op=ALU.add), waits=[u1, u2] + (r_free if dt == 0 else []))
                t1R.free[k1_] = [u3]
                t2R.free[k2_] = [u3]
                mt.append(u3)
            rtoks = []
            for dt in range(DCH):
                kt_, Tb, frt = psget()
                for c in range(DCH):
                    tt_ = R.op("pe", lambda e, c=c, dt=dt, Tb=Tb: e.matmul(Tb[:, :w], lhsT=wo[:, c, dt * 128:(dt + 1) * 128], rhs=mb[:, c, :w], start=(c == 0), stop=(c == DCH - 1)),
                               waits=(mt + frt) if c == 0 else [], sig=(c == DCH - 1))
                rt = R.op("dve", lambda e, dt=dt, Tb=Tb: e.scalar_tensor_tensor(out=rbuf[:, dt, :w], in0=xb[:, dt, :w], scalar=ALPHA, in1=Tb[:, :w], op0=ALU.mult, op1=ALU.add),
                          waits=[tt_] + (lb.get("r_free", []) if dt == 0 else []))
                PS.free[kt_] = [rt]
                rtoks.append(rt)
            in_free = [rtoks[-1], R.last["pe"]]
            r_free = [R.last["pe"]]
            outs, lt = layer_norm_block(l, 1, rbuf, w, x32_s[0], x1bf_s, t0, lb, rtoks)
            lb["r_free"] = lt
        R.barrier()

        stage_gate()
        A.reset(PBASE)
        xbf = A.alloc("xbf4", [128, DCH, NT], BF16)
        dxb = {}
        for (t0_, w_) in blocks:
            dxb[t0_] = R.dma("sp", xbf[:, :, t0_:t0_ + w_], x1bf_s[:, :, t0_:t0_ + w_], "s4x")
        dx = dxb[blocks[-1][0]]
        wst4 = [A.alloc("wst4", [128, DCH, 256], F32) for _ in range(2)]
        wst4R = Ring(wst4)
        wb4 = [A.alloc("wb4", [128, DCH, 256], BF16) for _ in range(3)]
        wb4R = Ring(wb4)
        sg = [A.alloc("sg", [128, TB], F32) for _ in range(2)]
        sgR = Ring(sg)
        so = [A.alloc("so", [128, TB], BF16) for _ in range(3)]
        soR = Ring(so)
        for ft in range(FT):
            k, st, fr = wst4R.get()
            d1 = R.dma("sp", st[:, :, 0:128], w_gu[l, :, ft * 128:(ft + 1) * 128].rearrange("(c p) n -> p c n", p=128), f"wst4{k}", waits=fr)
            d2 = R.dma("sp", st[:, :, 128:256], w_gu[l, :, DFF + ft * 128:DFF + (ft + 1) * 128].rearrange("(c p) n -> p c n", p=128), f"wst4{k}", waits=fr)
            kb, wb, frb = wb4R.get()
            c = R.op("pool", lambda e, wb=wb, st=st: e.tensor_copy(out=wb[:, :, :], in_=st[:, :, :]), waits=[d2] + frb)
            wst4R.free[k] = [c]
            for (t0, w) in blocks:
                kg, Gb, frg = psget()
                for cc in range(DCH):
                    tg = R.op("pe", lambda e, cc=cc, Gb=Gb, wb=wb, t0=t0, w=w: e.matmul(Gb[:, :w], lhsT=wb[:, cc, 0:128], rhs=xbf[:, cc, t0:t0 + w], start=(cc == 0), stop=(cc == DCH - 1)),
                              waits=([c, dx] + frg) if cc == 0 else [], sig=(cc == DCH - 1))
                ku, Ub, fru = psget()
                for cc in range(DCH):
                    tu = R.op("pe", lambda e, cc=cc, Ub=Ub, wb=wb, t0=t0, w=w: e.matmul(Ub[:, :w], lhsT=wb[:, cc, 128:256], rhs=xbf[:, cc, t0:t0 + w], start=(cc == 0), stop=(cc == DCH - 1)),
                              waits=fru if cc == 0 else [], sig=(cc == DCH - 1))
                ksg, sgt, frsg = sgR.get()
                a1 = R.op("act", lambda e, sgt=sgt, Gb=Gb, w=w: e.activation(out=sgt[:, :w], in_=Gb[:, :w], func=AF.Silu), waits=[tg] + frsg)
                PS.free[kg] = [a1]
                kso, sot, frso = soR.get()
                a2 = R.op("dve", lambda e, sgt=sgt, Ub=Ub, sot=sot, w=w: e.tensor_tensor(out=sot[:, :w], in0=Ub[:, :w], in1=sgt[:, :w], op=ALU.mult), waits=[a1, tu] + frso)
                PS.free[ku] = [a2]
                sgR.free[ksg] = [a2]
                d = R.dma("sp", s_s[:, ft, t0:t0 + w], sot[:, :w], f"so{kso}", waits=[a2])
                soR.free[kso] = [d]
            wb4R.free[kb] = [R.last["pe"]]
        R.barrier()

        stage_gate()
        A.reset(PBASE)
        wd = A.alloc("wd", [128, FT, D], BF16)
        wstd = [A.alloc("wstd", [128, 2, D], F32) for _ in range(2)]
        wstdR = Ring(wstd)
        c = None
        for f2 in range(FT // 2):
            k, st, fr = wstdR.get()
            d = R.dma("sp", st[:, :, :], w_dn[l, f2 * 256:(f2 + 1) * 256, :].rearrange("(c p) n -> p c n", p=128), f"wstd{k}", waits=fr)
            c = R.op("dve", lambda e, st=st, f2=f2: e.tensor_copy(out=wd[:, 2 * f2:2 * f2 + 2, :], in_=st[:, :, :]), waits=[d])
            wstdR.free[k] = [c]
        wdtok = c
        sb_ = A.alloc("sb_", [128, FT, TB], BF16)
        xb = A.alloc("xb4", [128, DCH, TB], F32)
        rbuf = A.alloc("rbuf4", [128, DCH, TB], F32)
        lb = ln_bufs()
        in_free = []
        for (t0, w) in blocks:
            d1 = R.dma("sp", sb_[:, :, :w], s_s[:, :, t0:t0 + w], "s4in", waits=in_free)
            d2 = R.dma("sp", xb[:, :, :w], x32_s[0][:, :, t0:t0 + w], "s4in", waits=in_free)
            rtoks = []
            for dt in range(DCH):
                kf, Fb, frf = psget()
                for ft in range(FT):
                    tf = R.op("pe", lambda e, ft=ft, dt=dt, Fb=Fb: e.matmul(Fb[:, :w], lhsT=wd[:, ft, dt * 128:(dt + 1) * 128], rhs=sb_[:, ft, :w], start=(ft == 0), stop=(ft == FT - 1)),
                              waits=([d2, wdtok] + frf) if ft == 0 else [], sig=(ft == FT - 1))
                rt = R.op("dve", lambda e, dt=dt, Fb=Fb: e.scalar_tensor_tensor(out=rbuf[:, dt, :w], in0=xb[:, dt, :w], scalar=ALPHA, in1=Fb[:, :w], op0=ALU.mult, op1=ALU.add),
                          waits=[tf] + (lb.get("r_free", []) if dt == 0 else []))
                PS.free[kf] = [rt]
                rtoks.append(rt)
            in_free = [rtoks[-1], R.last["pe"]]
            outs, lt = layer_norm_block(l, 2, rbuf, w, y_dst, None, t0, lb, rtoks)
            lb["r_free"] = lt
        R.barrier()

    except _Stop:
        pass
    final = R.all_tokens()
    R._waits("sp", final)
    with nc.Block() as block:
        @block.sync
        def _(e):
            for f in R.st["sp"]:
                f(e)

        @block.tensor
        def _(e):
            for f in R.st["pe"]:
                f(e)

        @block.scalar
        def _(e):
            for f in R.st["act"]:
                f(e)

        @block.vector
        def _(e):
            for f in R.st["dve"]:
                f(e)

        @block.gpsimd
        def _(e):
            for f in R.st["pool"]:
                f(e)
    return nc


def consts(NPG):
    NPAGES = NS * NPG
    GP = min(128, NPAGES)
    NGRP = NPAGES // GP
    bf = ml_dtypes.bfloat16
    s = np.arange(128)[:, None]
    t = np.arange(128)[None, :]
    c = {
        "c_ident": np.eye(128, dtype=np.float32).astype(bf),
        "c_maskb": np.where(s <= t, 0.0, NEG).astype(np.float32).astype(bf),
        "c_onesb": np.ones((128, 128), np.float32).astype(bf),
        "c_onesf": np.ones((128, 128), np.float32),
    }
    ind = np.zeros((NGRP, GP, NS), np.float32)
    p = np.arange(GP)
    for g in range(NGRP):
        ind[g, p, (g * GP + p) // NPG] = 1.0
    c["c_ind"] = ind
    c["c_indT"] = np.ascontiguousarray(ind.transpose(0, 2, 1))
    bb = p // NPG
    pg = p % NPG
    c["c_after"] = ((bb[:, None] == bb[None, :]) & (pg[:, None] > pg[None, :])).astype(np.float32)
    return c


def make_in_maps(inp, n_cores, S, NPG):
    L = inp["w_in"].shape[0]
    NPOOL = inp["cache_k"].shape[1]
    NPAGES = NS * NPG
    GP = min(128, NPAGES)
    NGRP = NPAGES // GP
    f = np.float32
    cst = consts(NPG)
    shared = {
        "w_in": np.ascontiguousarray(inp["w_in"], f),
        "b_f": np.ascontiguousarray(inp["b_f"].reshape(L, H, 1), f),
        "b_fr": np.ascontiguousarray(np.broadcast_to(inp["b_f"][:, None, :], (L, NS, H)), f),
        "b_gate": np.ascontiguousarray(inp["b_gate"].reshape(L, 16, 128).transpose(0, 2, 1), f),
        "conv_w": np.ascontiguousarray(inp["conv_w"].reshape(L, 3, 4, 128).transpose(0, 3, 2, 1), f),
        "w_pa": np.ascontiguousarray(inp["w_attn_proj"], f),
        "w_pc": np.ascontiguousarray(inp["w_conv_proj"], f),
        "w_o": np.ascontiguousarray(inp["w_out"], f),
        "ln1g": np.ascontiguousarray(inp["ln1_g"].reshape(L, DCH, 128).transpose(0, 2, 1), f),
        "ln1b": np.ascontiguousarray(inp["ln1_b"].reshape(L, DCH, 128).transpose(0, 2, 1), f),
        "ln2g": np.ascontiguousarray(inp["ln2_g"].reshape(L, DCH, 128).transpose(0, 2, 1), f),
        "ln2b": np.ascontiguousarray(inp["ln2_b"].reshape(L, DCH, 128).transpose(0, 2, 1), f),
        "w_gu": np.ascontiguousarray(inp["w_gate_up"], f),
        "w_dn": np.ascontiguousarray(inp["w_down"], f),
    }
    for l in range(L):
        shared[f"cache_k{l}"] = np.ascontiguousarray(inp["cache_k"][l], f).reshape(NPOOL, 128 * 512)
        shared[f"cache_v{l}"] = np.ascontiguousarray(inp["cache_v"][l], f).reshape(NPOOL, 128 * 512)
        shared[f"cache_lf{l}"] = np.ascontiguousarray(inp["cache_logf"][l], f).reshape(NPOOL, 128 * H)
    shared.update(cst)
    nb = inp["x_prompt"].shape[0]
    maps = []
    for c in range(n_cores):
        b = c % nb
        sb = slice(NS * c, NS * c + NS)
        m = dict(shared)
        m["xT"] = np.ascontiguousarray(np.concatenate([inp["x_prompt"][b].T, inp["x_sample"][sb, 0, :].T], axis=1), f)
        st = inp["state_conv"][:, sb]
        m["stT"] = np.ascontiguousarray(st.transpose(0, 3, 1, 2).reshape(L, 4, 128, NS, 2).transpose(0, 2, 1, 3, 4), f)
        m["pt"] = np.ascontiguousarray(inp["page_table"][sb].reshape(NGRP, GP).T, np.int32)
        maps.append(m)
    return maps


def assemble(res, inp, n_cores, S):
    L = inp["w_in"].shape[0]
    nb = inp["x_prompt"].shape[0]
    nsb = inp["x_sample"].shape[0]
    f = np.float32
    y_p = np.zeros((nb, S, D), f)
    y_s = np.zeros((nsb, 1, D), f)
    k_p = np.zeros((L, nb, S, H, HD), f)
    v_p = np.zeros((L, nb, S, H, HD), f)
    lf_p = np.zeros((L, nb, S, H), f)
    cv_p = np.zeros((L, nb, 2, 512), f)
    k_s = np.zeros((L, nsb, 1, H, HD), f)
    v_s = np.zeros((L, nsb, 1, H, HD), f)
    lf_s = np.zeros((L, nsb, 1, H), f)
    cv_s = np.zeros((L, nsb, 2, 512), f)
    for c in range(n_cores):
        r = res[c]
        sb = slice(NS * c, NS * c + NS)
        yT = np.asarray(r["yT"])
        y_s[sb, 0, :] = yT[:, S:].T
        k_s[:, sb, 0] = np.asarray(r["ks_out"]).reshape(L, NS, H, HD)
        v_s[:, sb, 0] = np.asarray(r["vs_out"]).reshape(L, NS, H, HD)
        lf_s[:, sb, 0] = np.asarray(r["lfs_out"])
        cv_s[:, sb] = np.asarray(r["convs_out"]).transpose(0, 3, 4, 2, 1).reshape(L, NS, 2, 512)
        if c < nb:
            b = c
            y_p[b] = yT[:, :S].T
            k_p[:, b] = np.asarray(r["kT_out"]).transpose(0, 2, 1).reshape(L, S, H, HD)
            v_p[:, b] = np.asarray(r["v_out"]).reshape(L, S, H, HD)
            lf_p[:, b] = np.asarray(r["lf_out"]).transpose(0, 2, 1)
            cv_p[:, b] = np.asarray(r["convp_out"]).transpose(0, 3, 2, 1).reshape(L, 2, 512)
    return (y_p, y_s, k_p, v_p, lf_p, cv_p, k_s, v_s, lf_s, cv_s)


_NC_CACHE = {}


def kernel(**inputs):
    inp = {k: np.asarray(v) for k, v in inputs.items()}
    S = inp["x_prompt"].shape[1]
    NPG = inp["page_table"].shape[1]
    NPOOL = inp["cache_k"].shape[1]
    n_cores = 8
    key = (S, NPG, NPOOL)
    if key not in _NC_CACHE:
        _NC_CACHE[key] = build(S, NPG, NPOOL)
    nc = _NC_CACHE[key]
    in_maps = make_in_maps(inp, n_cores, S, NPG)
    res = run_bass_kernel_spmd(nc, in_maps, core_ids=list(range(n_cores)))
    return assemble(res.results, inp, n_cores, S)
```

```python
import os
import numpy as np
import ml_dtypes
import concourse.bass as bass
import concourse.mybir as mybir
from concourse.bass_utils import run_bass_kernel_spmd

F32 = mybir.dt.float32
BF16 = mybir.dt.bfloat16
I32 = mybir.dt.int32
AF = mybir.ActivationFunctionType
ALU = mybir.AluOpType
AX = mybir.AxisListType

D = 1024
DCH = 8
H = 8
HD = 64
DIN = 5128
DFF = 2816
FT = 22
OFF_Q, OFF_K, OFF_V, OFF_F, OFF_H, OFF_B, OFF_C, OFF_G = 0, 512, 1024, 1536, 1544, 2056, 2568, 3080
NS = 4
TB = 512
KA = 70
ALPHA = float(4 ** 0.25)
LN_EPS = 1e-5
NEG = -30000.0


class _Proxy:
    def __init__(self):
        self.call = None

    def __getattr__(self, name):
        def f(*a, **k):
            self.call = (name, a, k)
            return self
        return f


class Rec:
    ENG = ("pe", "act", "dve", "pool", "sp")

    def __init__(self, nc):
        self.nc = nc
        self.st = {e: [] for e in self.ENG}
        self.psem = {e: nc.alloc_semaphore("P_" + e) for e in ("pe", "act", "dve", "pool")}
        self.pcnt = {e: 0 for e in self.psem}
        self.waited = {}
        self.dsem = {}
        self.dcnt = {}
        self.last = {}

    def _waits(self, eng, waits):
        for tok in waits:
            if tok is None:
                continue
            if isinstance(tok, list):
                self._waits(eng, tok)
                continue
            sem, val = tok
            key = (eng, sem.num)
            if self.waited.get(key, 0) >= val:
                continue
            self.waited[key] = val
            self.st[eng].append(lambda e, sem=sem, val=val: e.wait_ge(sem, val))

    def op(self, eng, fn, waits=(), sig=True):
        self._waits(eng, waits)
        px = _Proxy()
        fn(px)
        name, a, k = px.call
        if sig:
            self.pcnt[eng] += 1
            sem = self.psem[eng]
            tok = (sem, self.pcnt[eng])
            self.st[eng].append(lambda e, name=name, a=a, k=k, sem=sem: getattr(e, name)(*a, **k).then_inc(sem, 1))
            self.last[eng] = tok
            return tok
        self.st[eng].append(lambda e, name=name, a=a, k=k: getattr(e, name)(*a, **k))
        return None

    def dma(self, eng, out, in_, ch, waits=(), **kw):
        self._waits(eng, waits)
        if ch not in self.dsem:
            self.dsem[ch] = self.nc.alloc_semaphore("D_" + ch)
            self.dcnt[ch] = 0
        self.dcnt[ch] += 16
        sem = self.dsem[ch]
        self.st[eng].append(lambda e, out=out, in_=in_, sem=sem, kw=kw: e.dma_start(out=out, in_=in_, **kw).then_inc(sem, 16))
        return (sem, self.dcnt[ch])

    def gather(self, out, in_, idx, ch, waits=(), eoff=0):
        eng = "pool"
        self._waits(eng, waits)
        if ch not in self.dsem:
            self.dsem[ch] = self.nc.alloc_semaphore("D_" + ch)
            self.dcnt[ch] = 0
        self.dcnt[ch] += 16
        sem = self.dsem[ch]
        self.st[eng].append(lambda e, out=out, in_=in_, idx=idx, sem=sem, eoff=eoff: e.indirect_dma_start(
            out=out, out_offset=None, in_=in_,
            in_offset=bass.IndirectOffsetOnAxis(ap=idx, axis=0), element_offset=eoff).then_inc(sem, 16))
        return (sem, self.dcnt[ch])

    def all_tokens(self):
        toks = [t for t in self.last.values()]
        toks += [(self.dsem[c], self.dcnt[c]) for c in self.dsem]
        return toks

    def barrier(self):
        toks = self.all_tokens()
        for e in self.ENG:
            self._waits(e, toks)


class Arena:
    def __init__(self, nc, lo, hi):
        self.nc, self.lo, self.hi, self.cur, self.n = nc, lo, hi, lo, 0

    def alloc(self, name, shape, dtype):
        nbytes = int(np.prod(shape[1:])) * (4 if dtype in (F32, I32) else 2)
        off = (self.cur + 31) // 32 * 32
        assert off + nbytes <= self.hi, f"SBUF arena overflow at {name}: {off + nbytes} > {self.hi}"
        self.cur = off + nbytes
        Arena_cnt[0] += 1
        return self.nc.alloc_sbuf_tensor_at(f"{name}_{Arena_cnt[0]}", list(shape), dtype, offset=off)

    def mark(self):
        return self.cur

    def reset(self, m):
        self.cur = m


Arena_cnt = [0]


class Ring:
    def __init__(self, tiles):
        self.t = tiles
        self.free = [[] for _ in tiles]
        self.i = 0

    def get(self):
        k = self.i % len(self.t)
        self.i += 1
        fr = self.free[k]
        self.free[k] = []
        return k, self.t[k], fr


class _Stop(Exception):
    pass


def build(S, NPG, NPOOL, L=2, stop=None):
    stage_ctr = [0]

    def stage_gate():
        if stop is not None and stage_ctr[0] >= stop:
            raise _Stop()
        stage_ctr[0] += 1
    NT = S + NS
    NB = S // TB
    NST = S // 128
    blocks = [(i * TB, TB) for i in range(NB)] + [(S, NS)]
    NPAGES = NS * NPG
    GP = min(128, NPAGES)
    NGRP = NPAGES // GP
    TK = 8
    NCHUNK = 128 // TK

    nc = bass.Bass("TRN2", target_bir_lowering=False)

    def din(name, shape, dt=F32):
        return nc.dram_tensor(name, list(shape), dt, kind="ExternalInput").ap()

    def dout(name, shape, dt=F32):
        return nc.dram_tensor(name, list(shape), dt, kind="ExternalOutput").ap()

    def dscr(name, shape, dt):
        return nc.dram_tensor(name, list(shape), dt, kind="Internal").ap()

    xT = din("xT", [D, NT])
    w_in = din("w_in", [L, D, DIN])
    b_f = din("b_f", [L, H, 1])
    b_fr = din("b_fr", [L, NS, H])
    b_gate = din("b_gate", [L, 128, 16])
    conv_w = din("conv_w", [L, 128, 4, 3])
    w_pa = din("w_pa", [L, 512, D])
    w_pc = din("w_pc", [L, 512, D])
    w_o = din("w_o", [L, D, D])
    ln1g = din("ln1g", [L, 128, DCH])
    ln1b = din("ln1b", [L, 128, DCH])
    ln2g = din("ln2g", [L, 128, DCH])
    ln2b = din("ln2b", [L, 128, DCH])
    w_gu = din("w_gu", [L, D, 2 * DFF])
    w_dn = din("w_dn", [L, DFF, D])
    cache_k = [din(f"cache_k{l}", [NPOOL, 128 * 512]) for l in range(L)]
    cache_v = [din(f"cache_v{l}", [NPOOL, 128 * 512]) for l in range(L)]
    cache_lf = [din(f"cache_lf{l}", [NPOOL, 128 * H]) for l in range(L)]
    stT = din("stT", [L, 128, 4, NS, 2])
    pt = din("pt", [GP, NGRP], I32)
    c_ident = din("c_ident", [128, 128], BF16)
    c_maskb = din("c_maskb", [128, 128], BF16)
    c_onesb = din("c_onesb", [128, 128], BF16)
    c_onesf = din("c_onesf", [128, 128])
    c_ind = din("c_ind", [NGRP, GP, NS])
    c_indT = din("c_indT", [NGRP, NS, GP])
    c_after = din("c_after", [GP, GP])

    yT = dout("yT", [D, NT])
    kT_out = dout("kT_out", [L, 512, S])
    v_out = dout("v_out", [L, S, 512])
    lf_out = dout("lf_out", [L, H, S])
    convp_out = dout("convp_out", [L, 128, 4, 2])
    ks_out = dout("ks_out", [L, NS, 512])
    vs_out = dout("vs_out", [L, NS, 512])
    lfs_out = dout("lfs_out", [L, NS, H])
    convs_out = dout("convs_out", [L, 128, 4, NS, 2])

    x32_s = [dscr("x1_s", [128, DCH, NT], F32), dscr("x2_s", [128, DCH, NT], F32)]
    x1bf_s = dscr("x1bf_s", [128, DCH, NT], BF16)
    qT_s = dscr("qT_s", [H, KA, NT], BF16)
    kT_s = dscr("kT_s", [H, KA, NT], BF16)
    V_s = dscr("V_s", [H, S, HD + 1], BF16)
    g_s = dscr("g_s", [128, 16, NT], BF16)
    cb_s = dscr("cb_s", [128, 4, NT], BF16)
    a_s = dscr("a_s", [128, 4, NT], BF16)
    s_s = dscr("s_s", [128, FT, NT], BF16)

    R = Rec(nc)
    LO = 16512
    HI = 229344
    A = Arena(nc, LO, HI)
    banks = [nc.alloc_psum_tensor(f"bank{i}", [128, 512], F32) for i in range(8)]
    PS = Ring(banks)

    def psget():
        return PS.get()

    ident = A.alloc("ident", [128, 128], BF16)
    maskb = A.alloc("maskb", [128, 128], BF16)
    onesb = A.alloc("onesb", [128, 128], BF16)
    onesf = A.alloc("onesf", [128, 128], F32)
    lnp = A.alloc("lnp", [128, L, 4, DCH], F32)
    bg = A.alloc("bg", [128, L, 16], F32)
    cw = A.alloc("cw", [128, L, 4, 3], F32)
    bfc = A.alloc("bfc", [H, L], F32)
    bfr = A.alloc("bfr", [NS, L, H], F32)
    qs_t = A.alloc("qs_t", [NS, 512], F32)
    ks_t = A.alloc("ks_t", [NS, 512], F32)
    vs_t = A.alloc("vs_t", [NS, 512], F32)
    lfs_t = A.alloc("lfs_t", [NS, H], F32)
    epsc = A.alloc("epsc", [128, 1], F32)
    R.op("pool", lambda e: e.memset(epsc[:, :], LN_EPS))
    ctoks = []
    ctoks.append(R.dma("sp", ident[:, :], c_ident, "const"))
    ctoks.append(R.dma("sp", maskb[:, :], c_maskb, "const"))
    ctoks.append(R.dma("sp", onesb[:, :], c_onesb, "const"))
    ctoks.append(R.dma("sp", onesf[:, :], c_onesf, "const"))
    for l in range(L):
        for i, t in enumerate((ln1g, ln1b, ln2g, ln2b)):
            ctoks.append(R.dma("sp", lnp[:, l, i, :], t[l], "const"))
        ctoks.append(R.dma("sp", bg[:, l, :], b_gate[l], "const"))
        ctoks.append(R.dma("sp", cw[:, l, :, :], conv_w[l], "const"))
        ctoks.append(R.dma("sp", bfc[:, l:l + 1], b_f[l], "const"))
        ctoks.append(R.dma("sp", bfr[:, l, :], b_fr[l], "const"))
    ctok = ctoks[-1]
    nbf_tok = R.op("dve", lambda e: e.tensor_scalar(out=bfc[:, :], in0=bfc[:, :], scalar1=-1.0, scalar2=None, op0=ALU.mult), waits=[ctok])
    R.barrier()
    PBASE = A.mark()

    def layer_norm_block(l, which, r, w, out_dram32, out_drambf, t0, bufs, rtoks):
        rbf, r2, mean, rstd, tmp, y32, ybf = (bufs[k] for k in ("rbf", "r2", "mean", "rstd", "tmp", "y32", "ybf"))
        gi, bi = (0, 1) if which == 1 else (2, 3)
        t_rbf, t_r2 = [], []
        for dt in range(DCH):
            t_rbf.append(R.op("act", lambda e, dt=dt: e.activation(out=rbf[:, dt, :w], in_=r[:, dt, :w], func=AF.Copy), waits=[rtoks[dt]] + bufs["free_rbf"]))
            t_r2.append(R.op("act", lambda e, dt=dt: e.activation(out=r2[:, dt, :w], in_=r[:, dt, :w], func=AF.Square), waits=[rtoks[dt]] + bufs["free_r2"]))
        k1, S1, fr1 = psget()
        for dt in range(DCH):
            tS1 = R.op("pe", lambda e, dt=dt: e.matmul(S1[:, :w], lhsT=onesb[:, :], rhs=rbf[:, dt, :w], start=(dt == 0), stop=(dt == DCH - 1)),
                       waits=[t_rbf[dt]] + (fr1 if dt == 0 else []), sig=(dt == DCH - 1))
        k2, S2, fr2 = psget()
        for dt in range(DCH):
            tS2 = R.op("pe", lambda e, dt=dt: e.matmul(S2[:, :w], lhsT=onesb[:, :], rhs=r2[:, dt, :w], start=(dt == 0), stop=(dt == DCH - 1)),
                       waits=[t_r2[dt]] + (fr2 if dt == 0 else []), sig=(dt == DCH - 1))
        bufs["free_rbf"] = [tS1]
        bufs["free_r2"] = [tS2]
        tm = R.op("act", lambda e: e.activation(out=mean[:, :w], in_=S1[:, :w], func=AF.Copy, scale=1.0 / D), waits=[tS1] + bufs["free_stat"])
        PS.free[k1] = [tm]
        tq = R.op("dve", lambda e: e.tensor_tensor(out=tmp[:, :w], in0=mean[:, :w], in1=mean[:, :w], op=ALU.mult), waits=[tm] + bufs["free_tmp"])
        tv = R.op("dve", lambda e: e.scalar_tensor_tensor(out=rstd[:, :w], in0=S2[:, :w], scalar=1.0 / D, in1=tmp[:, :w], op0=ALU.mult, op1=ALU.subtract), waits=[tS2, tq] + bufs["free_stat"])
        PS.free[k2] = [tv]
        tsq = R.op("act", lambda e: e.activation(out=rstd[:, :w], in_=rstd[:, :w], func=AF.Sqrt, bias=epsc[:, 0:1]), waits=[tv])
        tr = R.op("dve", lambda e: e.reciprocal(out=rstd[:, :w], in_=rstd[:, :w]), waits=[tsq])
        ty, tb = [], []
        tmpR = bufs["tmpR"]
        t1 = t3 = None
        for dt in range(DCH):
            kt, tm_, frt = tmpR.get()
            t1 = R.op("dve", lambda e: e.tensor_tensor(out=tm_[:, :w], in0=r[:, dt, :w], in1=mean[:, :w], op=ALU.subtract), waits=[tm, rtoks[dt]] + frt)
            t2 = R.op("dve", lambda e: e.tensor_tensor(out=tm_[:, :w], in0=tm_[:, :w], in1=rstd[:, :w], op=ALU.mult), waits=[t1, tr])
            t3 = R.op("act", lambda e: e.activation(out=y32[:, dt, :w], in_=tm_[:, :w], func=AF.Identity,
                                                    scale=lnp[:, l, gi, dt:dt + 1], bias=lnp[:, l, bi, dt:dt + 1]),
                      waits=[t2] + (bufs["free_y32"] if dt == 0 else []))
            tmpR.free[kt] = [t3]
            ty.append(t3)
            if out_drambf is not None:
                tb.append(R.op("pool", lambda e: e.tensor_copy(out=ybf[:, dt, :w], in_=y32[:, dt, :w]), waits=[t3] + (bufs["free_ybf"] if dt == 0 else [])))
        last_tmp = [t1, t3]
        bufs["free_tmp"] = []
        bufs["free_stat"] = [t1]
        d1 = R.dma("sp", out_dram32[:, :, t0:t0 + w], y32[:, :, :w], "ln_y32", waits=[ty[-1]])
        bufs["free_y32"] = [d1]
        outs = [d1]
        if out_drambf is not None:
            d2 = R.dma("sp", out_drambf[:, :, t0:t0 + w], ybf[:, :, :w], "ln_ybf", waits=[tb[-1]])
            bufs["free_ybf"] = [d2]
            outs.append(d2)
        return outs, last_tmp

    def ln_bufs():
        b = {
            "rbf": A.alloc("rbf", [128, DCH, TB], BF16), "r2": A.alloc("r2", [128, DCH, TB], BF16),
            "mean": A.alloc("mean", [128, TB], F32), "rstd": A.alloc("rstd", [128, TB], F32),
            "tmp": A.alloc("tmp", [128, TB], F32), "y32": A.alloc("y32", [128, DCH, TB], F32),
            "tmpR": Ring([A.alloc("tmpr", [128, TB], F32) for _ in range(3)]),
            "ybf": A.alloc("ybf", [128, DCH, TB], BF16),
        }
        for k in ("free_rbf", "free_r2", "free_stat", "free_tmp", "free_y32", "free_ybf"):
            b[k] = []
        return b

    try:
     for l in range(L):
        x_src = xT.rearrange("(c p) t -> p c t", p=128) if l == 0 else x32_s[1]
        last = (l == L - 1)
        y_dst = yT.rearrange("(c p) t -> p c t", p=128) if last else x32_s[1]

        stage_gate()
        A.reset(PBASE)
        xbf = A.alloc("xbf", [128, DCH, NT], BF16)
        GW = 256
        wst = [A.alloc("wst", [128, DCH, GW], F32) for _ in range(2)]
        wstR = Ring(wst)
        wbf = [A.alloc("wbf", [128, DCH, GW], BF16) for _ in range(6)]
        wbfR = Ring(wbf)
        obf = [A.alloc("obf", [128, TB], BF16) for _ in range(4)]
        obfR = Ring(obf)
        o32 = [A.alloc("o32", [128, TB], F32) for _ in range(3)]
        o32R = Ring(o32)
        hbuf = A.alloc("hbuf", [128, NT], F32)
        ubuf = A.alloc("ubuf", [128, NT + 2], F32)
        zf = A.alloc("zf", [H, NT], F32)
        Vst = [A.alloc("Vst", [128, H, HD + 1], BF16) for _ in range(2)]
        VstR = Ring(Vst)

        xtok = {}
        for (t0, w) in blocks:
            for hb in range(0, w, GW):
                ww = min(GW, w - hb)
                k, st, fr = wstR.get()
                d = R.dma("sp", st[:, :, :ww], x_src[:, :, t0 + hb:t0 + hb + ww], f"wst{k}", waits=fr)
                c = R.op("dve", lambda e, st=st, ww=ww, a=t0 + hb: e.tensor_copy(out=xbf[:, :, a:a + ww], in_=st[:, :, :ww]), waits=[d])
                wstR.free[k] = [c]
                xtok[(t0, hb)] = c
        xall = c

        def load_w(src_ap, ncols):
            k, st, fr = wstR.get()
            d = R.dma("sp", st[:, :, :ncols], src_ap.rearrange("(c p) n -> p c n", p=128), f"wst{k}", waits=fr)
            kb, wb, frb = wbfR.get()
            c = R.op("dve", lambda e: e.tensor_copy(out=wb[:, :, :ncols], in_=st[:, :, :ncols]), waits=[d] + frb)
            wstR.free[k] = [c]
            return wb, kb, c

        def fm_tile(wb, wtok, j, t0, w):
            k, bk, fr = psget()
            for c in range(DCH):
                tk = R.op("pe", lambda e, c=c: e.matmul(bk[:, :w], lhsT=wb[:, c, j * 128:(j + 1) * 128], rhs=xbf[:, c, t0:t0 + w],
                                                        start=(c == 0), stop=(c == DCH - 1)),
                          waits=([wtok, xall] + fr) if c == 0 else [], sig=(c == DCH - 1))
            return k, bk, tk

        stage_gate()
        for which, off, dst in (("q", OFF_Q, qT_s), ("k", OFF_K, kT_s)):
            for g in range(2):
                wb, kb, wtok = load_w(w_in[l, :, off + g * GW: off + (g + 1) * GW], GW)
                pe_last = None
                for j in range(2):
                    h0 = (g * 2 + j) * 2
                    for (t0, w) in blocks:
                        k, bk, tk = fm_tile(wb, wtok, j, t0, w)
                        ko, ob, fro = obfR.get()
                        if which == "k" and t0 < S:
                            k3, o3, fr3 = o32R.get()
                            ev0 = R.op("act", lambda e: e.activation(out=o3[:, :w], in_=bk[:, :w], func=AF.Copy), waits=[tk] + fr3)
                            PS.free[k] = [ev0]
                            ev = R.op("pool", lambda e: e.tensor_copy(out=ob[:, :w], in_=o3[:, :w]), waits=[ev0] + fro)
                            c0 = (g * 2 + j) * 128
                            d3 = R.dma("sp", kT_out[l, c0:c0 + 128, t0:t0 + w], o3[:, :w], f"o32{k3}", waits=[ev0])
                            o32R.free[k3] = [d3, ev]
                        else:
                            ev = R.op("act", lambda e: e.activation(out=ob[:, :w], in_=bk[:, :w], func=AF.Copy), waits=[tk] + fro)
                            PS.free[k] = [ev]
                        d1 = R.dma("sp", dst[h0, 0:64, t0:t0 + w], ob[0:64, :w], f"obf{ko}", waits=[ev])
                        d2 = R.dma("sp", dst[h0 + 1, 0:64, t0:t0 + w], ob[64:128, :w], f"obf{ko}", waits=[ev])
                        obfR.free[ko] = [d2]
                        pe_last = tk
                if os.environ.get("SKIP_TM"):
                    wbfR.free[kb] = [R.last["pe"]]
                    continue
                k, bk, fr = psget()
                for c in range(DCH):
                    tk = R.op("pe", lambda e, c=c: e.matmul(bk[0:NS, :GW], lhsT=xbf[:, c, S:S + NS], rhs=wb[:, c, :GW], start=(c == 0), stop=(c == DCH - 1)),
                              waits=([wtok, xall] + fr) if c == 0 else [], sig=(c == DCH - 1))
                tgt = qs_t if which == "q" else ks_t
                ev = R.op("act", lambda e, tgt=tgt, bk=bk, g=g: e.activation(out=tgt[:, g * GW:(g + 1) * GW], in_=bk[0:NS, :GW], func=AF.Copy), waits=[tk])
                PS.free[k] = [ev]
                wbfR.free[kb] = [tk]
                if which == "k":
                    R.dma("sp", ks_out[l, :, g * GW:(g + 1) * GW], ks_t[:, g * GW:(g + 1) * GW], "small_out", waits=[ev])

        stage_gate()
        wv = [load_w(w_in[l, :, OFF_V + g * GW: OFF_V + (g + 1) * GW], GW) for g in range(2)]
        for tt in range(NST + 1):
            if tt < NST:
                a0, m = tt * 128, 128
            else:
                a0, m = S, NS
            k, bk, fr = psget()
            for g in range(2):
                wb, kb, wtok = wv[g]
                for c in range(DCH):
                    tk = R.op("pe", lambda e, c=c, g=g, wb=wb: e.matmul(bk[0:m, g * GW:(g + 1) * GW], lhsT=xbf[:, c, a0:a0 + m], rhs=wb[:, c, :GW],
                                                                       start=(c == 0), stop=(c == DCH - 1)),
                              waits=([wtok, xall] + fr) if c == 0 else [], sig=(c == DCH - 1 and g == 1))
            if tt < NST:
                k3, o3, fr3 = o32R.get()
                ev = R.op("act", lambda e, o3=o3, bk=bk: e.activation(out=o3[:, :], in_=bk[:, :], func=AF.Copy), waits=[tk] + fr3)
                d3 = R.dma("sp", v_out[l, a0:a0 + 128, :], o3[:, :], f"o32{k3}", waits=[ev])
                kv, vs, frv = VstR.get()
                ev2 = R.op("pool", lambda e: e.tensor_copy(out=vs[:, :, 0:HD], in_=o3[:, :].rearrange("p (h d) -> p h d", h=H)), waits=[ev] + frv)
                ev3 = R.op("pool", lambda e, vs=vs: e.memset(vs[:, :, HD:HD + 1], 1.0), waits=frv)
                d4 = R.dma("sp", V_s[:, a0:a0 + 128, :].rearrange("h s d -> s h d"), vs[:, :, :], f"Vst{kv}", waits=[ev2, ev3])
                VstR.free[kv] = [d4]
                o32R.free[k3] = [d3, ev2]
                PS.free[k] = [ev]
            else:
                ev = R.op("act", lambda e, bk=bk: e.activation(out=vs_t[:, :], in_=bk[0:NS, :], func=AF.Copy), waits=[tk])
                R.dma("sp", vs_out[l], vs_t[:, :], "small_out", waits=[ev])
                PS.free[k] = [ev]
        for g in range(2):
            wbfR.free[wv[g][1]] = [tk]

        stage_gate()
        wb, kb, wtok = load_w(w_in[l, :, OFF_F:OFF_F + H], H)
        zf_tok = None
        for (t0, w) in blocks:
            k, bk, fr = psget()
            for c in range(DCH):
                tk = R.op("pe", lambda e, c=c: e.matmul(bk[0:H, :w], lhsT=wb[:, c, 0:H], rhs=xbf[:, c, t0:t0 + w], start=(c == 0), stop=(c == DCH - 1)),
                          waits=([wtok, xall] + fr) if c == 0 else [], sig=(c == DCH - 1))
            zf_tok = R.op("act", lambda e, bk=bk, t0=t0, w=w: e.activation(out=zf[:, t0:t0 + w], in_=bk[0:H, :w], func=AF.Exp, scale=-1.0, bias=bfc[:, l:l + 1]),
                          waits=[tk, nbf_tok])
            PS.free[k] = [zf_tok]
        k, bk, fr = psget()
        for c in range(DCH):
            tk = R.op("pe", lambda e, c=c: e.matmul(bk[0:NS, 0:H], lhsT=xbf[:, c, S:S + NS], rhs=wb[:, c, 0:H], start=(c == 0), stop=(c == DCH - 1)),
                      waits=([wtok, xall] + fr) if c == 0 else [], sig=(c == DCH - 1))
        wbfR.free[kb] = [tk]
        t1 = R.op("dve", lambda e: e.tensor_tensor(out=lfs_t[:, :], in0=bk[0:NS, 0:H], in1=bfr[:, l, :], op=ALU.add), waits=[tk])
        PS.free[k] = [t1]
        t2 = R.op("act", lambda e: e.activation(out=lfs_t[:, :], in_=lfs_t[:, :], func=AF.Exp, scale=-1.0), waits=[t1])
        t3 = R.op("act", lambda e: e.activation(out=lfs_t[:, :], in_=lfs_t[:, :], func=AF.Ln, bias=1.0), waits=[t2])
        lfs_tok = R.op("dve", lambda e: e.tensor_scalar(out=lfs_t[:, :], in0=lfs_t[:, :], scalar1=-1.0, scalar2=None, op0=ALU.mult), waits=[t3])
        R.dma("sp", lfs_out[l], lfs_t[:, :], "small_out", waits=[lfs_tok])
        t4 = R.op("act", lambda e: e.activation(out=zf[:, :], in_=zf[:, :], func=AF.Ln, bias=1.0), waits=[zf_tok])
        lf_tok = R.op("dve", lambda e: e.tensor_scalar(out=zf[:, :], in0=zf[:, :], scalar1=-1.0, scalar2=None, op0=ALU.mult), waits=[t4])
        dlf = R.dma("sp", lf_out[l], zf[:, 0:S], "small_out", waits=[lf_tok])
        cs = hbuf
        zero8 = ubuf
        tz = R.op("pool", lambda e: e.memset(zero8[0:H, 0:S], 0.0))
        tc = R.op("dve", lambda e: e.tensor_tensor_scan(out=cs[0:H, 0:S], data0=zf[:, 0:S], data1=zero8[0:H, 0:S], initial=0.0, op0=ALU.add, op1=ALU.add), waits=[lf_tok, tz])
        tc = R.op("dve", lambda e: e.tensor_scalar(out=cs[0:H, 0:S], in0=cs[0:H, 0:S], scalar1=8.0, scalar2=None, op0=ALU.mult), waits=[tc])
        er1 = A.alloc("er1", [H, S], BF16)
        nr1 = A.alloc("nr1", [H, S], BF16)
        on1 = A.alloc("on1", [H, S], BF16)
        to = R.op("pool", lambda e: e.memset(on1[:, :], 1.0))
        res = zero8
        cur = cs
        tprev = tc
        erfree, nrfree = [], []
        aug = []
        for i in range(3):
            ta = R.op("dve", lambda e, cur=cur: e.tensor_copy(out=er1[:, :], in_=cur[0:H, 0:S]), waits=[tprev] + erfree)
            tn = R.op("dve", lambda e: e.tensor_scalar(out=nr1[:, :], in0=er1[:, :], scalar1=-1.0, scalar2=None, op0=ALU.mult), waits=[ta] + nrfree)
            if i < 2:
                tprev = R.op("dve", lambda e, cur=cur: e.tensor_tensor(out=res[0:H, 0:S], in0=cur[0:H, 0:S], in1=er1[:, :], op=ALU.subtract), waits=[tn])
                cur = res
            d_e = R.dma("sp", qT_s[:, 64 + i, 0:S], er1[:, :], "aug_e", waits=[ta])
            d_n = R.dma("sp", kT_s[:, 67 + i, 0:S], nr1[:, :], "aug_n", waits=[tn])
            erfree, nrfree = [d_e], [d_n]
            aug.append(R.dma("sp", qT_s[:, 67 + i, 0:S], on1[:, :], "aug_o", waits=[to]))
            aug.append(R.dma("sp", kT_s[:, 64 + i, 0:S], on1[:, :], "aug_o", waits=[to]))
        aug += [d_e, d_n]
        lastdve_split = R.last["dve"]
        conv_start_wait = aug + [dlf, lastdve_split]

        stage_gate()
        for gp in range(2):
            wh = load_w(w_in[l, :, OFF_H + gp * GW: OFF_H + (gp + 1) * GW], GW)
            wc = load_w(w_in[l, :, OFF_C + gp * GW: OFF_C + (gp + 1) * GW], GW)
            wg = load_w(w_in[l, :, OFF_B + gp * GW: OFF_B + (gp + 1) * GW], GW)
            for j in range(2):
                ct = gp * 2 + j
                th = None
                for (t0, w) in blocks:
                    k, bk, tk = fm_tile(wh[0], wh[2], j, t0, w)
                    th = R.op("act", lambda e, bk=bk, t0=t0, w=w: e.activation(out=hbuf[:, t0:t0 + w], in_=bk[:, :w], func=AF.Copy), waits=[tk] + conv_start_wait)
                    PS.free[k] = [th]
                conv_start_wait = []
                tz = R.op("pool", lambda e: e.memset(ubuf[:, 0:2], 0.0), waits=[th])
                tu = None
                for (t0, w) in blocks:
                    k, bk, tk = fm_tile(wc[0], wc[2], j, t0, w)
                    tu = R.op("dve", lambda e, bk=bk, t0=t0, w=w: e.tensor_tensor(out=ubuf[:, 2 + t0:2 + t0 + w], in0=bk[:, :w], in1=hbuf[:, t0:t0 + w], op=ALU.mult), waits=[tk, th])
                    PS.free[k] = [tu]
                dcp = R.dma("sp", convp_out[l, :, ct, :], ubuf[:, S:S + 2], "convp", waits=[tu])
                c1 = R.op("dve", lambda e, ct=ct: e.tensor_scalar(out=hbuf[:, 0:S], in0=ubuf[:, 0:S], scalar1=cw[:, l, ct, 0:1], scalar2=None, op0=ALU.mult), waits=[tu, tz])
                c2 = R.op("dve", lambda e, ct=ct: e.scalar_tensor_tensor(out=hbuf[:, 0:S], in0=ubuf[:, 1:S + 1], scalar=cw[:, l, ct, 1:2], in1=hbuf[:, 0:S], op0=ALU.mult, op1=ALU.add), waits=[c1])
                c3 = R.op("dve", lambda e, ct=ct: e.scalar_tensor_tensor(out=hbuf[:, 0:S], in0=ubuf[:, 2:S + 2], scalar=cw[:, l, ct, 2:3], in1=hbuf[:, 0:S], op0=ALU.mult, op1=ALU.add), waits=[c2])
                sst = A.alloc("sst", [128, NS, 2], F32)
                nst = A.alloc("nst", [128, NS, 2], F32)
                dS = R.dma("sp", sst[:, :, :], stT[l, :, ct, :, :], "sst")
                s1 = R.op("dve", lambda e, ct=ct: e.tensor_scalar(out=hbuf[:, S:S + NS], in0=sst[:, :, 0], scalar1=cw[:, l, ct, 0:1], scalar2=None, op0=ALU.mult), waits=[dS, tu])
                s2 = R.op("dve", lambda e, ct=ct: e.scalar_tensor_tensor(out=hbuf[:, S:S + NS], in0=sst[:, :, 1], scalar=cw[:, l, ct, 1:2], in1=hbuf[:, S:S + NS], op0=ALU.mult, op1=ALU.add), waits=[s1])
                s3 = R.op("dve", lambda e, ct=ct: e.scalar_tensor_tensor(out=hbuf[:, S:S + NS], in0=ubuf[:, 2 + S:2 + S + NS], scalar=cw[:, l, ct, 2:3], in1=hbuf[:, S:S + NS], op0=ALU.mult, op1=ALU.add), waits=[s2])
                n1 = R.op("pool", lambda e: e.tensor_copy(out=nst[:, :, 0], in_=sst[:, :, 1]), waits=[dS])
                n2 = R.op("pool", lambda e: e.tensor_copy(out=nst[:, :, 1], in_=ubuf[:, 2 + S:2 + S + NS]), waits=[tu])
                R.dma("sp", convs_out[l, :, ct, :, :], nst[:, :, :], "small_out", waits=[n1, n2])
                tcb = None
                for (t0, w) in blocks:
                    k, bk, tk = fm_tile(wg[0], wg[2], j, t0, w)
                    ko, ob, fro = obfR.get()
                    tcb = R.op("dve", lambda e, bk=bk, ob=ob, t0=t0, w=w: e.tensor_tensor(out=ob[:, :w], in0=bk[:, :w], in1=hbuf[:, t0:t0 + w], op=ALU.mult), waits=[tk, c3, s3] + fro)
                    PS.free[k] = [tcb]
                    d1 = R.dma("sp", cb_s[:, ct, t0:t0 + w], ob[:, :w], f"obf{ko}", waits=[tcb])
                    obfR.free[ko] = [d1]
                conv_start_wait = [tcb, n2, dcp]
            lastpe = R.last["pe"]
            for ww_ in (wh, wc, wg):
                wbfR.free[ww_[1]] = [lastpe]

        stage_gate()
        for g in range(8):
            wb, kb, wtok = load_w(w_in[l, :, OFF_G + g * GW: OFF_G + (g + 1) * GW], GW)
            for j in range(2):
                gt = g * 2 + j
                for (t0, w) in blocks:
                    k, bk, tk = fm_tile(wb, wtok, j, t0, w)
                    ko, ob, fro = obfR.get()
                    ev = R.op("act", lambda e, bk=bk, ob=ob, w=w, gt=gt: e.activation(out=ob[:, :w], in_=bk[:, :w], func=AF.Sigmoid, bias=bg[:, l, gt:gt + 1]), waits=[tk] + fro)
                    PS.free[k] = [ev]
                    d1 = R.dma("sp", g_s[:, gt, t0:t0 + w], ob[:, :w], f"obf{ko}", waits=[ev])
                    obfR.free[ko] = [d1]
            wbfR.free[kb] = [R.last["pe"]]
        R.barrier()

        stage_gate()
        A.reset(PBASE)
        kbuf = [A.alloc("kbuf", [KA, S], BF16) for _ in range(2)]
        qbuf = [A.alloc("qbuf", [KA, S], BF16) for _ in range(2)]
        vbuf = [A.alloc("vbuf", [128, NST, HD + 1], BF16) for _ in range(2)]
        pT = [A.alloc("pT", [128, TB], BF16) for _ in range(4)]
        pTR = Ring(pT)
        rl = A.alloc("rl", [65, TB], F32)
        rb = A.alloc("rb", [64, TB], F32)
        ast = [A.alloc("ast", [64, TB], BF16) for _ in range(2)]
        astR = Ring(ast)
        PO = Ring(banks[0:2])
        PSb = Ring(banks[2:5])
        PB = Ring(banks[5:6])
        OS = banks[6]
        MISC = banks[7]

        idx = A.alloc("idx", [GP, NGRP], I32)
        ind = A.alloc("ind", [GP, NGRP, NS], F32)
        indb = A.alloc("indb", [GP, NGRP, NS], BF16)
        indT = A.alloc("indT", [NS, NGRP, GP], F32)
        aft = A.alloc("aft", [GP, GP], F32)
        lfp = A.alloc("lfp", [GP, 128, H], F32)
        pfx = A.alloc("pfx", [GP, 128, H], F32)
        bias = A.alloc("bias", [GP, 128, H], F32)
        tot = A.alloc("tot", [GP, H], F32)
        qpp = A.alloc("qpp", [GP, 512], F32)
        Kc = [A.alloc("Kc", [GP, TK, 512], F32) for _ in range(2)]
        Vc = [A.alloc("Vc", [GP, TK, 512], F32) for _ in range(2)]
        prod = A.alloc("prod", [GP, TK, 512], F32)
        pv = A.alloc("pv", [GP, TK, 512], BF16)
        sc = A.alloc("sc", [GP, TK, H], F32)
        pe_ = A.alloc("pe_", [GP, 128, H], F32)
        rs = A.alloc("rs", [GP, H], F32)
        sm = A.alloc("sm", [NS, 8, 512], F32)
        zt = A.alloc("zt", [GP, 128], F32)
        asb = A.alloc("asb", [NS, 512], BF16)
        qs_d = dscr(f"qs_d{l}", [NS, 512], F32)
        as_d = dscr(f"as_d{l}", [NS, 512], BF16)

        def sample_stage():
            R.dma("sp", qs_d, qs_t[:, :], "smp")
            R.dma("sp", idx[:, :], pt, "smp")
            R.dma("sp", ind[:, :, :], c_ind.rearrange("g p b -> p g b"), "smp")
            R.dma("sp", indT[:, :, :], c_indT.rearrange("g b p -> b g p"), "smp")
            dcst = R.dma("sp", aft[:, :], c_after, "smp")
            tzt = R.op("pool", lambda e: e.memset(zt[:, :], 0.0))
            tib = R.op("pool", lambda e: e.tensor_copy(out=indb[:, :, :], in_=ind[:, :, :]), waits=[dcst])
            LSg = [MISC[0:NS, g_ * H:(g_ + 1) * H] for g_ in range(NGRP)]
            Bq = MISC[0:GP, 64:64 + H]
            bq_free = []
            first_mm = True
            r2_ = None
            grp_done = None
            prev_tmm = None
            tmm = None
            BG = GP // NPG
            for g in range(NGRP):
                dqq = None
                for bl in range(BG):
                    b = g * BG + bl
                    dqq = R.dma("sp", qpp[bl * NPG:(bl + 1) * NPG, :], qs_d[b:b + 1, :].partition_broadcast(NPG), "smp2", waits=[dcst, grp_done])
                glf = R.gather(lfp[:, :, :].rearrange("p t h -> p (t h)"), cache_lf[l], idx[:, g:g + 1], "glf", waits=[dcst, grp_done])
                tp = None
                for hh in range(H):
                    tp = R.op("dve", lambda e: e.tensor_tensor_scan(out=pfx[:, :, hh], data0=lfp[:, :, hh], data1=zt[:, :],
                                                                   initial=0.0, op0=ALU.add, op1=ALU.add), waits=[glf, tzt, grp_done])
                tt_ = R.op("dve", lambda e: e.tensor_copy(out=tot[:, :], in_=pfx[:, 127, :]), waits=[tp])
                R.op("pe", lambda e: e.matmul(Bq, lhsT=aft[:, :], rhs=tot[:, :], start=True, stop=False), waits=[tt_, dcst] + bq_free, sig=False)
                tb_ = R.op("pe", lambda e: e.matmul(Bq, lhsT=indT[:, g, :], rhs=lfs_t[:, :], start=False, stop=True), waits=[lfs_tok])
                tt2 = R.op("dve", lambda e: e.tensor_tensor(out=tot[:, :], in0=tot[:, :], in1=Bq, op=ALU.add), waits=[tb_])
                bq_free = [tt2]
                tbias = R.op("dve", lambda e: e.tensor_tensor(out=bias[:, :, :], in0=tot[:, :].unsqueeze(1).to_broadcast([GP, 128, H]), in1=pfx[:, :, :], op=ALU.subtract), waits=[tt2])
                kfree = [[], []]
                vfree = [[], []]
                prev_m4 = None
                prev_m2 = None
                yield
                for ch in range(NCHUNK):
                    sl = ch % 2
                    a0 = ch * TK * 512
                    gk = R.gather(Kc[sl][:, :, :].rearrange("p t e -> p (t e)"), cache_k[l], idx[:, g:g + 1], f"gk{sl}", waits=kfree[sl], eoff=a0)
                    gv = R.gather(Vc[sl][:, :, :].rearrange("p t e -> p (t e)"), cache_v[l], idx[:, g:g + 1], f"gv{sl}", waits=vfree[sl], eoff=a0)
                    m1 = R.op("pool", lambda e: e.tensor_tensor(out=prod[:, :, :], in0=Kc[sl][:, :, :], in1=qpp[:, :].unsqueeze(1).to_broadcast([GP, TK, 512]), op=ALU.mult), waits=[gk, dqq, prev_m2])
                    kfree[sl] = [m1]
                    m2 = R.op("dve", lambda e: e.tensor_reduce(out=sc[:, :, :].rearrange("p t h -> p (t h)"), in_=prod[:, :, :].rearrange("p t (h d) -> p (t h) d", h=H), axis=AX.X, op=ALU.add), waits=[m1, prev_m4])
                    prev_m2 = m2
                    m3 = R.op("dve", lambda e: e.scalar_tensor_tensor(out=sc[:, :, :], in0=sc[:, :, :], scalar=0.125, in1=bias[:, ch * TK:(ch + 1) * TK, :], op0=ALU.mult, op1=ALU.add), waits=[m2, tbias])
                    m4 = R.op("act", lambda e: e.activation(out=pe_[:, ch * TK:(ch + 1) * TK, :], in_=sc[:, :, :], func=AF.Exp), waits=[m3])
                    prev_m4 = m4
                    m5 = R.op("dve", lambda e: e.tensor_tensor(out=pv[:, :, :].rearrange("p t (h d) -> p t h d", h=H), in0=Vc[sl][:, :, :].rearrange("p t (h d) -> p t h d", h=H),
                                                               in1=pe_[:, ch * TK:(ch + 1) * TK, :].unsqueeze(3).to_broadcast([GP, TK, H, HD]), op=ALU.mult), waits=[m4, gv, prev_tmm])
                    vfree[sl] = [m5]
                    for tk_ in range(TK):
                        lastmm = (g == NGRP - 1 and ch == NCHUNK - 1 and tk_ == TK - 1)
                        tmm = R.op("pe", lambda e: e.matmul(OS[0:NS, :], lhsT=indb[:, g, :], rhs=pv[:, tk_, :], start=first_mm, stop=lastmm),
                                   waits=([m5, tib]) if tk_ == 0 else [], sig=(tk_ == TK - 1))
                        first_mm = False
                    prev_tmm = tmm
                    yield
                r1 = R.op("dve", lambda e: e.tensor_reduce(out=rs[:, :], in_=pe_[:, :, :].rearrange("p t h -> p h t"), axis=AX.X, op=ALU.add), waits=[prev_m4, r2_])
                r2_ = R.op("pe", lambda e: e.matmul(LSg[g], lhsT=ind[:, g, :], rhs=rs[:, :], start=True, stop=True), waits=[r1])
                grp_done = r1
            w_q, w_p, w_o_, w_e, w_l = sm[:, 0, :], sm[:, 1, :], sm[:, 2, :], sm[:, 3, 0:H], sm[:, 4, 0:H]
            n1 = R.op("dve", lambda e: e.tensor_tensor(out=w_q, in0=qs_t[:, :], in1=ks_t[:, :], op=ALU.mult))
            n2 = R.op("dve", lambda e: e.tensor_reduce(out=w_e, in_=w_q.rearrange("p (h d) -> p h d", h=H), axis=AX.X, op=ALU.add), waits=[n1])
            n3 = R.op("act", lambda e: e.activation(out=w_e, in_=w_e, func=AF.Exp, scale=0.125), waits=[n2])
            n4 = R.op("dve", lambda e: e.tensor_tensor(out=w_l, in0=w_e, in1=LSg[0], op=ALU.add), waits=[n3, r2_])
            for g_ in range(1, NGRP):
                n4 = R.op("dve", lambda e: e.tensor_tensor(out=w_l, in0=w_l, in1=LSg[g_], op=ALU.add), waits=[n4])
            n5 = R.op("dve", lambda e: e.reciprocal(out=w_l, in_=w_l), waits=[n4])
            n6 = R.op("dve", lambda e: e.tensor_tensor(out=w_p.rearrange("p (h d) -> p h d", h=H), in0=vs_t[:, :].rearrange("p (h d) -> p h d", h=H),
                                                       in1=w_e.unsqueeze(2).to_broadcast([NS, H, HD]), op=ALU.mult), waits=[n3])
            n7 = R.op("dve", lambda e: e.tensor_tensor(out=w_o_, in0=w_p, in1=OS[0:NS, :], op=ALU.add), waits=[n6, tmm])
            n8 = R.op("dve", lambda e: e.tensor_tensor(out=asb[:, :].rearrange("p (h d) -> p h d", h=H), in0=w_o_.rearrange("p (h d) -> p h d", h=H),
                                                       in1=w_l.unsqueeze(2).to_broadcast([NS, H, HD]), op=ALU.mult), waits=[n7, n5])
            d1 = R.dma("sp", as_d, asb[:, :], "smp3", waits=[n8])
            for b in range(NS):
                R.dma("sp", a_s[:, :, S + b:S + b + 1], as_d[b:b + 1, :].rearrange("b (c p) -> p c b", p=128), "smp3", waits=[d1], allow_slow_non_contiguous=True)
            yield

        sgen = sample_stage()

        hfree = [[], []]
        htoks = {}

        def load_head(h):
            sl = h % 2
            fr = hfree[sl]
            R.dma("sp", kbuf[sl][:, :], kT_s[h, :, 0:S], f"hd{sl}", waits=fr)
            R.dma("sp", qbuf[sl][:, :], qT_s[h, :, 0:S], f"hd{sl}", waits=fr)
            htoks[h] = R.dma("sp", vbuf[sl][:, :, :], V_s[h].rearrange("(t p) d -> p t d", p=128), f"hd{sl}", waits=fr)

        tiles = []
        for h in range(H):
            for tb in range(NB):
                n_s = (tb * TB + TB) // 128
                for si in range(n_s):
                    tiles.append((h, tb, si, n_s))
        qk_state = {}
        blk_state = {}
        norm_state = {"rl_free": [], "rb_free": []}

        def emit_qk(i):
            h, tb, si, n_s = tiles[i]
            sl = h % 2
            kb_, qb_ = kbuf[sl], qbuf[sl]
            t0 = tb * TB
            j = si - t0 // 128
            diag = j >= 0
            c0 = 128 * j if diag else 0
            ks_, Sb, frs = PSb.get()
            w0 = [htoks[h]] + frs
            if diag:
                R.op("pe", lambda e: e.matmul(Sb[:, c0:c0 + 128], lhsT=kb_[:, si * 128:(si + 1) * 128], rhs=qb_[:, t0 + c0:t0 + c0 + 128], start=True, stop=False), waits=w0, sig=False)
                tq = R.op("pe", lambda e: e.matmul(Sb[:, c0:c0 + 128], lhsT=ident[:, :], rhs=maskb[:, :], start=False, stop=True))
                if c0 + 128 < TB:
                    tq = R.op("pe", lambda e: e.matmul(Sb[:, c0 + 128:TB], lhsT=kb_[:, si * 128:(si + 1) * 128], rhs=qb_[:, t0 + c0 + 128:t0 + TB], start=True, stop=True))
            else:
                tq = R.op("pe", lambda e: e.matmul(Sb[:, :], lhsT=kb_[:, si * 128:(si + 1) * 128], rhs=qb_[:, t0:t0 + TB], start=True, stop=True), waits=w0)
            kp, P_, frp = pTR.get()
            te = R.op("act", lambda e: e.activation(out=P_[:, c0:TB], in_=Sb[:, c0:TB], func=AF.Exp, scale=0.125), waits=[tq] + frp)
            PSb.free[ks_] = [te]
            qk_state[i] = (te, P_, kp, c0)

        def emit_pv(i):
            h, tb, si, n_s = tiles[i]
            sl = h % 2
            vb_ = vbuf[sl]
            t0 = tb * TB
            te, P_, kp, c0 = qk_state.pop(i)
            if si == 0:
                ko_, O, fro = PO.get()
                blk_state[(h, tb)] = (ko_, O)
            else:
                ko_, O = blk_state[(h, tb)]
                fro = []
            pv_tok = R.op("pe", lambda e: e.matmul(O[0:HD + 1, c0:TB], lhsT=vb_[:, si, :], rhs=P_[:, c0:TB], start=(si == 0), stop=(si == n_s - 1)),
                          waits=[te] + fro)
            pTR.free[kp] = [pv_tok]
            if si == n_s - 1:
                t1 = R.op("dve", lambda e: e.reciprocal(out=rl[64:65, :], in_=O[64:65, :]), waits=[pv_tok] + norm_state["rl_free"])
                kb2, Bc, frb = PB.get()
                t2 = R.op("pe", lambda e: e.matmul(Bc[0:64, :], lhsT=onesf[64:65, 0:64], rhs=rl[64:65, :], start=True, stop=True), waits=[t1] + frb)
                norm_state["rl_free"] = [t2]
                t3 = R.op("act", lambda e: e.activation(out=rb[:, :], in_=Bc[0:64, :], func=AF.Copy), waits=[t2] + norm_state["rb_free"])
                PB.free[kb2] = [t3]
                ka, as_, fra = astR.get()
                t4 = R.op("dve", lambda e: e.tensor_tensor(out=as_[:, :], in0=O[0:64, :], in1=rb[:, :], op=ALU.mult), waits=[t3] + fra)
                norm_state["rb_free"] = [t4]
                PO.free[ko_] = [t4]
                d = R.dma("sp", a_s[(h % 2) * 64:(h % 2) * 64 + 64, h // 2, t0:t0 + TB], as_[:, :], f"ast{ka}", waits=[t4])
                astR.free[ka] = [d]
                del blk_state[(h, tb)]
                if tb == NB - 1:
                    hfree[sl] = [pv_tok]
                    if h + 2 < H:
                        load_head(h + 2)

        load_head(0)
        if H > 1:
            load_head(1)
        LA = 2
        NTL = len(tiles)
        for i in range(min(LA, NTL)):
            emit_qk(i)
        for i in range(NTL):
            if i + LA < NTL:
                emit_qk(i + LA)
            emit_pv(i)
            if i % 20 == 10:
                next(sgen, None)
        for _ in sgen:
            pass
        R.barrier()

        stage_gate()
        A.reset(PBASE)
        wpa = A.alloc("wpa", [128, 4, D], BF16)
        wpc = A.alloc("wpc", [128, 4, D], BF16)
        wo = A.alloc("wo", [128, DCH, D], BF16)
        wst3 = [A.alloc("wst3", [128, DCH, D // 4], F32) for _ in range(2)]
        wst3R = Ring(wst3)
        for (dst_, src_, nch) in ((wpa, w_pa[l], 4), (wpc, w_pc[l], 4), (wo, w_o[l], DCH)):
            for hf in range(4):
                k3_, st3, fr3_ = wst3R.get()
                d = R.dma("sp", st3[:, 0:nch, :], src_.rearrange("(c p) n -> p c n", p=128)[:, :, hf * 256:(hf + 1) * 256], f"wst3{k3_}", waits=fr3_)
                c = R.op("dve", lambda e: e.tensor_copy(out=dst_[:, :, hf * 256:(hf + 1) * 256], in_=st3[:, 0:nch, :]), waits=[d])
                wst3R.free[k3_] = [c]
        w3tok = c
        MB = A.mark()
        ab = A.alloc("ab", [128, 4, TB], BF16)
        cbb = A.alloc("cbb", [128, 4, TB], BF16)
        gb_ = A.alloc("gb_", [128, 16, TB], BF16)
        xb = A.alloc("xb", [128, DCH, TB], F32)
        mb = A.alloc("mb", [128, DCH, TB], BF16)
        t1R = Ring([A.alloc("t1b", [128, TB], F32) for _ in range(2)])
        t2R = Ring([A.alloc("t2b", [128, TB], F32) for _ in range(2)])
        rbufR = Ring([A.alloc("rbuf", [128, DCH, TB], F32), A.alloc("rbufb", [128, DCH, TB], F32)])
        lb = ln_bufs()
        r_free = []
        free3 = {"ab": [], "gb": [], "xb": []}

        def s3_loads(t0, w):
            R.dma("sp", ab[:, :, :w], a_s[:, :, t0:t0 + w], "s3in", waits=free3["ab"])
            R.dma("sp", cbb[:, :, :w], cb_s[:, :, t0:t0 + w], "s3in", waits=free3["ab"])
            R.dma("sp", gb_[:, :, :w], g_s[:, :, t0:t0 + w], "s3in", waits=free3["gb"])
            return R.dma("sp", xb[:, :, :w], x_src[:, :, t0:t0 + w], "s3in", waits=free3["xb"])

        dld_next = s3_loads(*blocks[0])
        for bi, (t0, w) in enumerate(blocks):
            kr_, rbuf, frr = rbufR.get()
            dld = dld_next
            mt = []
            for dt in range(DCH):
                ka_, Ab, fra = psget()
                for e_ in range(4):
                    ta = R.op("pe", lambda e, e_=e_, dt=dt, Ab=Ab: e.matmul(Ab[:, :w], lhsT=wpa[:, e_, dt * 128:(dt + 1) * 128], rhs=ab[:, e_, :w], start=(e_ == 0), stop=(e_ == 3)),
                              waits=([dld, w3tok] + fra) if e_ == 0 else [], sig=(e_ == 3))
                kc_, Cb, frc = psget()
                for e_ in range(4):
                    tc_ = R.op("pe", lambda e, e_=e_, dt=dt, Cb=Cb: e.matmul(Cb[:, :w], lhsT=wpc[:, e_, dt * 128:(dt + 1) * 128], rhs=cbb[:, e_, :w], start=(e_ == 0), stop=(e_ == 3)),
                               waits=frc if e_ == 0 else [], sig=(e_ == 3))
                k1_, t1b, f1_ = t1R.get()
                k2_, t2b, f2_ = t2R.get()
                u1 = R.op("dve", lambda e: e.tensor_tensor(out=t1b[:, :w], in0=Ab[:, :w], in1=gb_[:, dt, :w], op=ALU.mult), waits=[ta] + f1_)
                u2 = R.op("dve", lambda e: e.tensor_tensor(out=t2b[:, :w], in0=Cb[:, :w], in1=gb_[:, 8 + dt, :w], op=ALU.mult), waits=[tc_] + f2_)
                PS.free[ka_] = [u1]
                PS.free[kc_] = [u2]
                u3 = R.op("pool", lambda e: e.tensor_tensor(out=mb[:, dt, :w], in0=t1b[:, :w], in1=t2b[:, :w], op=ALU.add), waits=[u1, u2] + (r_free if dt == 0 else []))
                t1R.free[k1_] = [u3]
                t2R.free[k2_] = [u3]
                mt.append(u3)
            rtoks = []
            for dt in range(DCH):
                kt_, Tb, frt = psget()
                for c in range(DCH):
                    tt_ = R.op("pe", lambda e, c=c, dt=dt, Tb=Tb: e.matmul(Tb[:, :w], lhsT=wo[:, c, dt * 128:(dt + 1) * 128], rhs=mb[:, c, :w], start=(c == 0), stop=(c == DCH - 1)),
                               waits=(mt + frt) if c == 0 else [], sig=(c == DCH - 1))
                rt = R.op("dve", lambda e, dt=dt, Tb=Tb: e.scalar_tensor_tensor(out=rbuf[:, dt, :w], in0=xb[:, dt, :w], scalar=ALPHA, in1=Tb[:, :w], op0=ALU.mult, op1=ALU.add),
                          waits=[tt_] + (frr if dt == 0 else []))
                PS.free[kt_] = [rt]
                rtoks.append(rt)
            free3["ab"] = [tc_]
            free3["gb"] = [u2]
            free3["xb"] = [rtoks[-1]]
            r_free = [R.last["pe"]]
            if bi + 1 < len(blocks):
                dld_next = s3_loads(*blocks[bi + 1])
            outs, lt = layer_norm_block(l, 1, rbuf, w, x32_s[0], x1bf_s, t0, lb, rtoks)
            rbufR.free[kr_] = lt
        R.barrier()

        stage_gate()
        A.reset(PBASE)
        xbf = A.alloc("xbf4", [128, DCH, NT], BF16)
        dxb = {}
        for (t0_, w_) in blocks:
            dxb[t0_] = R.dma("sp", xbf[:, :, t0_:t0_ + w_], x1bf_s[:, :, t0_:t0_ + w_], "s4x")
        dx = dxb[blocks[-1][0]]
        wst4 = [A.alloc("wst4", [128, DCH, 256], F32) for _ in range(2)]
        wst4R = Ring(wst4)
        wb4 = [A.alloc("wb4", [128, DCH, 256], BF16) for _ in range(3)]
        wb4R = Ring(wb4)
        sg = [A.alloc("sg", [128, TB], F32) for _ in range(2)]
        sgR = Ring(sg)
        so = [A.alloc("so", [128, TB], BF16) for _ in range(3)]
        soR = Ring(so)
        for ft in range(FT):
            k, st, fr = wst4R.get()
            d1 = R.dma("sp", st[:, :, 0:128], w_gu[l, :, ft * 128:(ft + 1) * 128].rearrange("(c p) n -> p c n", p=128), f"wst4{k}", waits=fr)
            d2 = R.dma("sp", st[:, :, 128:256], w_gu[l, :, DFF + ft * 128:DFF + (ft + 1) * 128].rearrange("(c p) n -> p c n", p=128), f"wst4{k}", waits=fr)
            kb, wb, frb = wb4R.get()
            c = R.op("pool", lambda e, wb=wb, st=st: e.tensor_copy(out=wb[:, :, :], in_=st[:, :, :]), waits=[d2] + frb)
            wst4R.free[k] = [c]
            for (t0, w) in blocks:
                kg, Gb, frg = psget()
                for cc in range(DCH):
                    tg = R.op("pe", lambda e, cc=cc, Gb=Gb, wb=wb, t0=t0, w=w: e.matmul(Gb[:, :w], lhsT=wb[:, cc, 0:128], rhs=xbf[:, cc, t0:t0 + w], start=(cc == 0), stop=(cc == DCH - 1)),
                              waits=([c, dx] + frg) if cc == 0 else [], sig=(cc == DCH - 1))
                ku, Ub, fru = psget()
                for cc in range(DCH):
                    tu = R.op("pe", lambda e, cc=cc, Ub=Ub, wb=wb, t0=t0, w=w: e.matmul(Ub[:, :w], lhsT=wb[:, cc, 128:256], rhs=xbf[:, cc, t0:t0 + w], start=(cc == 0), stop=(cc == DCH - 1)),
                              waits=fru if cc == 0 else [], sig=(cc == DCH - 1))
                ksg, sgt, frsg = sgR.get()
                a1 = R.op("act", lambda e, sgt=sgt, Gb=Gb, w=w: e.activation(out=sgt[:, :w], in_=Gb[:, :w], func=AF.Silu), waits=[tg] + frsg)
                PS.free[kg] = [a1]
                kso, sot, frso = soR.get()
                a2 = R.op("dve", lambda e, sgt=sgt, Ub=Ub, sot=sot, w=w: e.tensor_tensor(out=sot[:, :w], in0=Ub[:, :w], in1=sgt[:, :w], op=ALU.mult), waits=[a1, tu] + frso)
                PS.free[ku] = [a2]
                sgR.free[ksg] = [a2]
                d = R.dma("sp", s_s[:, ft, t0:t0 + w], sot[:, :w], f"so{kso}", waits=[a2])
                soR.free[kso] = [d]
            wb4R.free[kb] = [R.last["pe"]]
        R.barrier()

        stage_gate()
        A.reset(PBASE)
        wd = A.alloc("wd", [128, FT, D], BF16)
        wstd = [A.alloc("wstd", [128, 2, D], F32) for _ in range(2)]
        wstdR = Ring(wstd)
        c = None
        for f2 in range(FT // 2):
            k, st, fr = wstdR.get()
            d = R.dma("sp", st[:, :, :], w_dn[l, f2 * 256:(f2 + 1) * 256, :].rearrange("(c p) n -> p c n", p=128), f"wstd{k}", waits=fr)
            c = R.op("dve", lambda e, st=st, f2=f2: e.tensor_copy(out=wd[:, 2 * f2:2 * f2 + 2, :], in_=st[:, :, :]), waits=[d])
            wstdR.free[k] = [c]
        wdtok = c
        sb_ = A.alloc("sb_", [128, FT, TB], BF16)
        xb = A.alloc("xb4", [128, DCH, TB], F32)
        rbufR = Ring([A.alloc("rbuf4", [128, DCH, TB], F32), A.alloc("rbuf4b", [128, DCH, TB], F32)])
        lb = ln_bufs()
        free4 = {"sb": [], "xb": []}

        def s4_loads(t0, w):
            R.dma("sp", sb_[:, :, :w], s_s[:, :, t0:t0 + w], "s4in", waits=free4["sb"])
            return R.dma("sp", xb[:, :, :w], x32_s[0][:, :, t0:t0 + w], "s4in", waits=free4["xb"])

        d2_next = s4_loads(*blocks[0])
        for bi, (t0, w) in enumerate(blocks):
            kr_, rbuf, frr = rbufR.get()
            d2 = d2_next
            rtoks = []
            for dt in range(DCH):
                kf, Fb, frf = psget()
                for ft in range(FT):
                    tf = R.op("pe", lambda e, ft=ft, dt=dt, Fb=Fb: e.matmul(Fb[:, :w], lhsT=wd[:, ft, dt * 128:(dt + 1) * 128], rhs=sb_[:, ft, :w], start=(ft == 0), stop=(ft == FT - 1)),
                              waits=([d2, wdtok] + frf) if ft == 0 else [], sig=(ft == FT - 1))
                rt = R.op("dve", lambda e, dt=dt, Fb=Fb: e.scalar_tensor_tensor(out=rbuf[:, dt, :w], in0=xb[:, dt, :w], scalar=ALPHA, in1=Fb[:, :w], op0=ALU.mult, op1=ALU.add),
                          waits=[tf] + (frr if dt == 0 else []))
                PS.free[kf] = [rt]
                rtoks.append(rt)
            free4["sb"] = [tf]
            free4["xb"] = [rtoks[-1]]
            if bi + 1 < len(blocks):
                d2_next = s4_loads(*blocks[bi + 1])
            outs, lt = layer_norm_block(l, 2, rbuf, w, y_dst, None, t0, lb, rtoks)
            rbufR.free[kr_] = lt
        R.barrier()

    except _Stop:
        pass
    final = R.all_tokens()
    R._waits("sp", final)
    with nc.Block() as block:
        @block.sync
        def _(e):
            for f in R.st["sp"]:
                f(e)

        @block.tensor
        def _(e):
            for f in R.st["pe"]:
                f(e)

        @block.scalar
        def _(e):
            for f in R.st["act"]:
                f(e)

        @block.vector
        def _(e):
            for f in R.st["dve"]:
                f(e)

        @block.gpsimd
        def _(e):
            for f in R.st["pool"]:
                f(e)
    return nc


def consts(NPG):
    NPAGES = NS * NPG
    GP = min(128, NPAGES)
    NGRP = NPAGES // GP
    bf = ml_dtypes.bfloat16
    s = np.arange(128)[:, None]
    t = np.arange(128)[None, :]
    c = {
        "c_ident": np.eye(128, dtype=np.float32).astype(bf),
        "c_maskb": np.where(s <= t, 0.0, NEG).astype(np.float32).astype(bf),
        "c_onesb": np.ones((128, 128), np.float32).astype(bf),
        "c_onesf": np.ones((128, 128), np.float32),
    }
    ind = np.zeros((NGRP, GP, NS), np.float32)
    p = np.arange(GP)
    for g in range(NGRP):
        ind[g, p, (g * GP + p) // NPG] = 1.0
    c["c_ind"] = ind
    c["c_indT"] = np.ascontiguousarray(ind.transpose(0, 2, 1))
    bb = p // NPG
    pg = p % NPG
    c["c_after"] = ((bb[:, None] == bb[None, :]) & (pg[:, None] > pg[None, :])).astype(np.float32)
    return c


def make_in_maps(inp, n_cores, S, NPG):
    L = inp["w_in"].shape[0]
    NPOOL = inp["cache_k"].shape[1]
    NPAGES = NS * NPG
    GP = min(128, NPAGES)
    NGRP = NPAGES // GP
    f = np.float32
    cst = consts(NPG)
    shared = {
        "w_in": np.ascontiguousarray(inp["w_in"], f),
        "b_f": np.ascontiguousarray(inp["b_f"].reshape(L, H, 1), f),
        "b_fr": np.ascontiguousarray(np.broadcast_to(inp["b_f"][:, None, :], (L, NS, H)), f),
        "b_gate": np.ascontiguousarray(inp["b_gate"].reshape(L, 16, 128).transpose(0, 2, 1), f),
        "conv_w": np.ascontiguousarray(inp["conv_w"].reshape(L, 3, 4, 128).transpose(0, 3, 2, 1), f),
        "w_pa": np.ascontiguousarray(inp["w_attn_proj"], f),
        "w_pc": np.ascontiguousarray(inp["w_conv_proj"], f),
        "w_o": np.ascontiguousarray(inp["w_out"], f),
        "ln1g": np.ascontiguousarray(inp["ln1_g"].reshape(L, DCH, 128).transpose(0, 2, 1), f),
        "ln1b": np.ascontiguousarray(inp["ln1_b"].reshape(L, DCH, 128).transpose(0, 2, 1), f),
        "ln2g": np.ascontiguousarray(inp["ln2_g"].reshape(L, DCH, 128).transpose(0, 2, 1), f),
        "ln2b": np.ascontiguousarray(inp["ln2_b"].reshape(L, DCH, 128).transpose(0, 2, 1), f),
        "w_gu": np.ascontiguousarray(inp["w_gate_up"], f),
        "w_dn": np.ascontiguousarray(inp["w_down"], f),
    }
    for l in range(L):
        shared[f"cache_k{l}"] = np.ascontiguousarray(inp["cache_k"][l], f).reshape(NPOOL, 128 * 512)
        shared[f"cache_v{l}"] = np.ascontiguousarray(inp["cache_v"][l], f).reshape(NPOOL, 128 * 512)
        shared[f"cache_lf{l}"] = np.ascontiguousarray(inp["cache_logf"][l], f).reshape(NPOOL, 128 * H)
    shared.update(cst)
    nb = inp["x_prompt"].shape[0]
    maps = []
    for c in range(n_cores):
        b = c % nb
        sb = slice(NS * c, NS * c + NS)
        m = dict(shared)
        m["xT"] = np.ascontiguousarray(np.concatenate([inp["x_prompt"][b].T, inp["x_sample"][sb, 0, :].T], axis=1), f)
        st = inp["state_conv"][:, sb]
        m["stT"] = np.ascontiguousarray(st.transpose(0, 3, 1, 2).reshape(L, 4, 128, NS, 2).transpose(0, 2, 1, 3, 4), f)
        m["pt"] = np.ascontiguousarray(inp["page_table"][sb].reshape(NGRP, GP).T, np.int32)
        maps.append(m)
    return maps


def assemble(res, inp, n_cores, S):
    L = inp["w_in"].shape[0]
    nb = inp["x_prompt"].shape[0]
    nsb = inp["x_sample"].shape[0]
    f = np.float32
    y_p = np.zeros((nb, S, D), f)
    y_s = np.zeros((nsb, 1, D), f)
    k_p = np.zeros((L, nb, S, H, HD), f)
    v_p = np.zeros((L, nb, S, H, HD), f)
    lf_p = np.zeros((L, nb, S, H), f)
    cv_p = np.zeros((L, nb, 2, 512), f)
    k_s = np.zeros((L, nsb, 1, H, HD), f)
    v_s = np.zeros((L, nsb, 1, H, HD), f)
    lf_s = np.zeros((L, nsb, 1, H), f)
    cv_s = np.zeros((L, nsb, 2, 512), f)
    for c in range(n_cores):
        r = res[c]
        sb = slice(NS * c, NS * c + NS)
        yT = np.asarray(r["yT"])
        y_s[sb, 0, :] = yT[:, S:].T
        k_s[:, sb, 0] = np.asarray(r["ks_out"]).reshape(L, NS, H, HD)
        v_s[:, sb, 0] = np.asarray(r["vs_out"]).reshape(L, NS, H, HD)
        lf_s[:, sb, 0] = np.asarray(r["lfs_out"])
        cv_s[:, sb] = np.asarray(r["convs_out"]).transpose(0, 3, 4, 2, 1).reshape(L, NS, 2, 512)
        if c < nb:
            b = c
            y_p[b] = yT[:, :S].T
            k_p[:, b] = np.asarray(r["kT_out"]).transpose(0, 2, 1).reshape(L, S, H, HD)
            v_p[:, b] = np.asarray(r["v_out"]).reshape(L, S, H, HD)
            lf_p[:, b] = np.asarray(r["lf_out"]).transpose(0, 2, 1)
            cv_p[:, b] = np.asarray(r["convp_out"]).transpose(0, 3, 2, 1).reshape(L, 2, 512)
    return (y_p, y_s, k_p, v_p, lf_p, cv_p, k_s, v_s, lf_s, cv_s)


_NC_CACHE = {}


def kernel(**inputs):
    inp = {k: np.asarray(v) for k, v in inputs.items()}
    S = inp["x_prompt"].shape[1]
    NPG = inp["page_table"].shape[1]
    NPOOL = inp["cache_k"].shape[1]
    n_cores = 8
    key = (S, NPG, NPOOL)
    if key not in _NC_CACHE:
        _NC_CACHE[key] = build(S, NPG, NPOOL)
    nc = _NC_CACHE[key]
    in_maps = make_in_maps(inp, n_cores, S, NPG)
    res = run_bass_kernel_spmd(nc, in_maps, core_ids=list(range(n_cores)))
    return assemble(res.results, inp, n_cores, S)
```
